# Optimizing a Trainium2 kernel written in Bass

```python
import math
import jax, jax.numpy as jnp
from jax import lax
import numpy as np


D_MODEL = 1024
BATCH = 16
SEQ = 2048
DEPTH = 2

HEAD_DIM = 64
ROPE_THETA = 500000.0
ROT_DIM = HEAD_DIM // 4
NORM_EPS = 1e-6
Q_BLOCK = 128
NEG_INF = -1e30
N_NORMS = 7

A_HEADS = 4
A_QK_DIM = 64
A_V_DIM = 2 * A_QK_DIM
B_HEADS = 4
B_PATTERNS = ((128, 1), (512, 4), (2048, 16))
C_HEADS = 4
C_Q_RANK = 256
C_KV_RANK = 128
C_NOPE_DIM = 64
C_ROPE_DIM = 32
C_V_DIM = 64
MEM_LEN = 256
MEM_HEADS = 4
MEM_HEAD_DIM = 64
D_FF = -(-(8 * D_MODEL) // (3 * 256)) * 256

A_WIDTH = A_HEADS * A_V_DIM
B_WIDTH = B_HEADS * HEAD_DIM
C_WIDTH = C_HEADS * C_V_DIM
D_MIX = A_WIDTH + B_WIDTH + C_WIDTH
IN_SPLITS = (A_HEADS * 2 * A_QK_DIM, A_HEADS * 2 * A_QK_DIM, A_WIDTH,
             B_WIDTH, B_WIDTH, B_WIDTH, C_Q_RANK, C_KV_RANK, C_ROPE_DIM)
D_IN = sum(IN_SPLITS)

kernel_name = 'hybrid_parallel_head_encoder'


def rmsnorm(x, g):
    xf = x.astype(jnp.float32)
    y = xf * lax.rsqrt(jnp.mean(xf * xf, axis=-1, keepdims=True) + NORM_EPS)
    return (y * g.astype(jnp.float32)).astype(x.dtype)


def rope_tables(positions, dim):
    inv = ROPE_THETA ** (-jnp.arange(0, dim, 2, dtype=jnp.float32) / dim)
    ang = positions.astype(jnp.float32)[..., None] * inv
    return jnp.cos(ang), jnp.sin(ang)


def apply_rope(x, cos, sin):
    half = cos.shape[-1]
    shape = cos.shape[:2] + (1,) * (x.ndim - 3) + (half,)
    c = cos.reshape(shape).astype(x.dtype)
    s = sin.reshape(shape).astype(x.dtype)
    x1, x2 = x[..., :half], x[..., half:]
    return jnp.concatenate([x1 * c - x2 * s, x2 * c + x1 * s], axis=-1)


def partial_rope(x, cos, sin):
    rd = 2 * cos.shape[-1]
    return jnp.concatenate([apply_rope(x[..., :rd], cos, sin), x[..., rd:]], axis=-1)


def diff_attention(q, k, v, lam, subln_g, lam_init):
    B, S, H = q.shape[:3]
    nb = S // Q_BLOCK
    kh = k.transpose(0, 2, 3, 1, 4)
    vh = v.transpose(0, 2, 1, 3)
    qb = q.transpose(0, 2, 3, 1, 4).reshape(B, H, 2, nb, Q_BLOCK, A_QK_DIM)
    qb = qb.transpose(3, 0, 1, 2, 4, 5)
    scale = A_QK_DIM ** -0.5

    def block(q_blk):
        s = jnp.einsum('bhmqd,bhmkd->bhmqk', q_blk, kh).astype(jnp.float32) * scale
        p = jax.nn.softmax(s, axis=-1)
        w = p[:, :, 0] - lam * p[:, :, 1]
        return jnp.einsum('bhqk,bhkd->bhqd', w.astype(vh.dtype), vh)

    o = lax.map(block, qb)
    o = o.transpose(1, 0, 3, 2, 4).reshape(B, S, H, A_V_DIM)
    o = rmsnorm(o, subln_g) * (1.0 - lam_init)
    return o.reshape(B, S, H * A_V_DIM)


def dilated_branch(q, k, v, window, dilation):
    B, S, H, E = q.shape
    half = window // (2 * dilation)
    blk = half
    L = S // dilation
    Lp = -(-L // blk) * blk
    nb = Lp // blk

    def sub(t):
        return t.reshape(B, L, dilation, H, E).transpose(0, 2, 3, 1, 4)

    qs, ks, vs = sub(q), sub(k), sub(v)
    qb = jnp.pad(qs, ((0, 0), (0, 0), (0, 0), (0, Lp - L), (0, 0))).reshape(B, dilation, H, nb, blk, E)
    pad_kv = ((0, 0), (0, 0), (0, 0), (blk, Lp - L + blk), (0, 0))

    def band(t):
        tp = jnp.pad(t, pad_kv).reshape(B, dilation, H, nb + 2, blk, E)
        return jnp.concatenate([tp[:, :, :, 0:nb], tp[:, :, :, 1:nb + 1], tp[:, :, :, 2:nb + 2]], axis=4)

    kw, vw = band(ks), band(vs)
    jq = np.arange(nb)[:, None, None] * blk + np.arange(blk)[None, :, None]
    jk = (np.arange(nb)[:, None, None] - 1) * blk + np.arange(3 * blk)[None, None, :]
    valid = (np.abs(jq - jk) <= half) & (jk >= 0) & (jk < L)
    s = jnp.einsum('bdhnqe,bdhnke->bdhnqk', qb, kw).astype(jnp.float32) * (E ** -0.5)
    s = jnp.where(valid, s, NEG_INF)
    m = jnp.max(s, axis=-1, keepdims=True)
    e = jnp.exp(s - m)
    den = jnp.sum(e, axis=-1)
    o = jnp.einsum('bdhnqk,bdhnke->bdhnqe', e, vw.astype(jnp.float32)) / den[..., None]
    lse = m[..., 0] + jnp.log(den)
    o = o.reshape(B, dilation, H, Lp, E)[:, :, :, :L].transpose(0, 3, 1, 2, 4).reshape(B, S, H, E)
    lse = lse.reshape(B, dilation, H, Lp)[:, :, :, :L].transpose(0, 3, 1, 2).reshape(B, S, H)
    return o, lse


def dilated_attention(q, k, v):
    B, S, H, E = q.shape
    outs, lses = [], []
    for window, dilation in B_PATTERNS:
        o, lse = dilated_branch(q, k, v, window, dilation)
        outs.append(o)
        lses.append(lse)
    alpha = jax.nn.softmax(jnp.stack(lses, axis=0), axis=0)
    out = jnp.sum(alpha[..., None] * jnp.stack(outs, axis=0), axis=0)
    return out.astype(q.dtype).reshape(B, S, H * E)


def latent_attention(c_q, c_kv, k_rope, q_norm_g, kv_norm_g, w_q_up, w_kv_up, cos_c, sin_c):
    B, S, _ = c_q.shape
    nb = S // Q_BLOCK
    q = (rmsnorm(c_q, q_norm_g) @ w_q_up).reshape(B, S, C_HEADS, C_NOPE_DIM + C_ROPE_DIM)
    q_nope = q[..., :C_NOPE_DIM]
    q_rope = apply_rope(q[..., C_NOPE_DIM:], cos_c, sin_c)
    kv = (rmsnorm(c_kv, kv_norm_g) @ w_kv_up).reshape(B, S, C_HEADS, C_NOPE_DIM + C_V_DIM)
    kn = kv[..., :C_NOPE_DIM].transpose(0, 2, 1, 3)
    vh = kv[..., C_NOPE_DIM:].transpose(0, 2, 1, 3)
    kr = apply_rope(k_rope, cos_c, sin_c)
    qn = q_nope.transpose(0, 2, 1, 3).reshape(B, C_HEADS, nb, Q_BLOCK, C_NOPE_DIM).transpose(2, 0, 1, 3, 4)
    qr = q_rope.transpose(0, 2, 1, 3).reshape(B, C_HEADS, nb, Q_BLOCK, C_ROPE_DIM).transpose(2, 0, 1, 3, 4)
    scale = (C_NOPE_DIM + C_ROPE_DIM) ** -0.5

    def block(args):
        qn_b, qr_b = args
        s = (jnp.einsum('bhqd,bhkd->bhqk', qn_b, kn)
             + jnp.einsum('bhqr,bkr->bhqk', qr_b, kr)).astype(jnp.float32) * scale
        p = jax.nn.softmax(s, axis=-1)
        return jnp.einsum('bhqk,bhkd->bhqd', p.astype(vh.dtype), vh)

    o = lax.map(block, (qn, qr))
    return o.transpose(1, 0, 3, 2, 4).reshape(B, S, C_HEADS * C_V_DIM)


def memory_attention(h, mem_n, w_q, w_kv, w_o):
    B, S, _ = h.shape
    M = mem_n.shape[1]
    q = (h @ w_q).reshape(B, S, MEM_HEADS, MEM_HEAD_DIM)
    kv = (mem_n @ w_kv).reshape(B, M, 2, MEM_HEADS, MEM_HEAD_DIM)
    k, v = kv[:, :, 0], kv[:, :, 1]
    s = jnp.einsum('bqhd,bkhd->bhqk', q, k).astype(jnp.float32) * (MEM_HEAD_DIM ** -0.5)
    p = jax.nn.softmax(s, axis=-1)
    o = jnp.einsum('bhqk,bkhd->bqhd', p.astype(v.dtype), v).reshape(B, S, MEM_HEADS * MEM_HEAD_DIM)
    return o @ w_o


def setup_inputs(seed: int = 0) -> dict:
    key = jax.random.key(seed)
    ks = jax.random.split(key, 20)
    f32 = jnp.float32

    def nrm(k, shape, fan_in):
        return jax.random.normal(k, shape, f32) * (fan_in ** -0.5)

    def gain(k, shape):
        return 1.0 + 0.05 * jax.random.normal(k, shape, f32)

    x = jax.random.normal(ks[0], (BATCH, SEQ, D_MODEL), f32)
    mem = jax.random.normal(ks[1], (BATCH, MEM_LEN, D_MODEL), f32)
    offsets = jax.random.randint(ks[2], (BATCH, 1), 0, 4096, dtype=jnp.int32)
    positions = (offsets + jnp.arange(SEQ, dtype=jnp.int32)[None, :]).astype(jnp.int32)
    return {
        'x': x,
        'mem': mem,
        'positions': positions,
        'norm_gains': gain(ks[3], (DEPTH, N_NORMS, D_MODEL)),
        'w_in': nrm(ks[4], (DEPTH, D_MODEL, D_IN), D_MODEL),
        'w_out': nrm(ks[5], (DEPTH, D_MIX, D_MODEL), D_MIX),
        'diff_lambda': 0.1 * jax.random.normal(ks[6], (DEPTH, 4, A_QK_DIM), f32),
        'diff_subln': gain(ks[7], (DEPTH, A_V_DIM)),
        'mla_q_norm': gain(ks[8], (DEPTH, C_Q_RANK)),
        'mla_kv_norm': gain(ks[9], (DEPTH, C_KV_RANK)),
        'w_mla_q_up': nrm(ks[10], (DEPTH, C_Q_RANK, C_HEADS * (C_NOPE_DIM + C_ROPE_DIM)), C_Q_RANK),
        'w_mla_kv_up': nrm(ks[11], (DEPTH, C_KV_RANK, C_HEADS * (C_NOPE_DIM + C_V_DIM)), C_KV_RANK),
        'w_mem_q': nrm(ks[12], (DEPTH, D_MODEL, MEM_HEADS * MEM_HEAD_DIM), D_MODEL),
        'w_mem_kv': nrm(ks[13], (DEPTH, D_MODEL, 2 * MEM_HEADS * MEM_HEAD_DIM), D_MODEL),
        'w_mem_o': nrm(ks[14], (DEPTH, MEM_HEADS * MEM_HEAD_DIM, D_MODEL), MEM_HEADS * MEM_HEAD_DIM),
        'w_ffn_gate': nrm(ks[15], (DEPTH, D_MODEL, D_FF), D_MODEL),
        'w_ffn_up': nrm(ks[16], (DEPTH, D_MODEL, D_FF), D_MODEL),
        'w_ffn_down': nrm(ks[17], (DEPTH, D_FF, D_MODEL), D_FF),
    }


def reference(x, mem, positions, norm_gains, w_in, w_out, diff_lambda, diff_subln,
              mla_q_norm, mla_kv_norm, w_mla_q_up, w_mla_kv_up, w_mem_q, w_mem_kv,
              w_mem_o, w_ffn_gate, w_ffn_up, w_ffn_down):
    B, S, _ = x.shape
    cos_p, sin_p = rope_tables(positions, ROT_DIM)
    cos_c, sin_c = rope_tables(positions, C_ROPE_DIM)
    split_points = np.cumsum(IN_SPLITS)[:-1].tolist()
    for l in range(DEPTH):
        g = norm_gains[l]
        h = rmsnorm(x, g[0])
        qa, ka, va, qb, kb, vb, cq, ckv, kr = jnp.split(h @ w_in[l], split_points, axis=-1)
        lam_init = 0.8 - 0.6 * math.exp(-0.3 * l)
        lv = diff_lambda[l].astype(jnp.float32)
        lam = jnp.exp(jnp.sum(lv[0] * lv[1])) - jnp.exp(jnp.sum(lv[2] * lv[3])) + lam_init
        qa = partial_rope(qa.reshape(B, S, A_HEADS, 2, A_QK_DIM), cos_p, sin_p)
        ka = partial_rope(ka.reshape(B, S, A_HEADS, 2, A_QK_DIM), cos_p, sin_p)
        oa = diff_attention(qa, ka, va.reshape(B, S, A_HEADS, A_V_DIM), lam, diff_subln[l], lam_init)
        qb = partial_rope(qb.reshape(B, S, B_HEADS, HEAD_DIM), cos_p, sin_p)
        kb = partial_rope(kb.reshape(B, S, B_HEADS, HEAD_DIM), cos_p, sin_p)
        ob = dilated_attention(qb, kb, vb.reshape(B, S, B_HEADS, HEAD_DIM))
        oc = latent_attention(cq, ckv, kr, mla_q_norm[l], mla_kv_norm[l],
                              w_mla_q_up[l], w_mla_kv_up[l], cos_c, sin_c)
        mixed = jnp.concatenate([oa, ob, oc], axis=-1) @ w_out[l]
        x = x + rmsnorm(mixed, g[1])
        h = rmsnorm(x, g[2])
        mem_n = rmsnorm(mem, g[3])
        x = x + rmsnorm(memory_attention(h, mem_n, w_mem_q[l], w_mem_kv[l], w_mem_o[l]), g[4])
        h = rmsnorm(x, g[5])
        f = (jax.nn.silu(h @ w_ffn_gate[l]) * (h @ w_ffn_up[l])) @ w_ffn_down[l]
        x = x + rmsnorm(f, g[6])
    return x
```

```python
import math
from contextlib import ExitStack

import numpy as np
import concourse.bass as bass
import concourse.mybir as mybir
from concourse.bass_utils import run_bass_kernel_spmd

F32 = mybir.dt.float32
BF16 = mybir.dt.bfloat16
I32 = mybir.dt.int32
AF = mybir.ActivationFunctionType
ALU = mybir.AluOpType

N_CORES = 8
D = 1024
S_LEN = 2048
NT = 16
DEPTH = 2
D_IN = 2720
D_FF = 2816
NFC = 22
MEM_LEN = 256
EPS = 1e-6
MASK_W = 2944
MASK_C0 = 1408


class Buf:
    __slots__ = ("name", "w", "r")

    def __init__(self, name):
        self.name = name
        self.w = None
        self.r = []


class Op:
    __slots__ = ("eng", "idx", "fn", "deps", "waits", "signal", "semval", "dma", "dsem")

    def __init__(self, eng, fn):
        self.eng = eng
        self.fn = fn
        self.deps = []
        self.waits = []
        self.signal = False
        self.semval = 0
        self.dma = False
        self.dsem = None


class _Rec:
    def __init__(self):
        self.calls = []

    def __getattr__(self, name):
        def f(*a, **k):
            self.calls.append((name, a, k))
            return self
        return f


class Sched:
    ENGS = ("pe", "act", "dve", "pool", "sp")

    def __init__(self):
        self.ops = []
        self.streams = {e: [] for e in self.ENGS}
        self.bufs = {}
        self.dma_groups = {}

    def B(self, *key):
        b = self.bufs.get(key)
        if b is None:
            b = self.bufs[key] = Buf(key)
        return b

    def op(self, eng, fn, reads=(), writes=(), dma_key=None):
        rec = _Rec()
        fn(rec)
        o = Op(eng, rec.calls)
        deps = {}
        rawset = set()
        writes = list(writes) + [b for b in reads if b.name[0] == "ps" and b not in writes]
        for b in reads:
            if b.w is not None:
                deps[id(b.w)] = b.w
                rawset.add(id(b.w))
        for b in writes:
            for r in b.r:
                deps[id(r)] = r
            if b.w is not None:
                deps[id(b.w)] = b.w
        for d in deps.values():
            if d is o:
                continue
            if d.dma:
                if dma_key is not None and d.dsem == dma_key:
                    continue
                o.deps.append(d)
            else:
                if dma_key is None and d.eng == eng and eng == "pe":
                    continue
                o.deps.append(d)
        for b in reads:
            b.r.append(o)
        for b in writes:
            b.w = o
            b.r = []
        if dma_key is not None:
            o.dma = True
            o.dsem = dma_key
            self.dma_groups.setdefault(dma_key, []).append(o)
        o.idx = len(self.streams[eng])
        self.streams[eng].append(o)
        self.ops.append(o)
        return o

    def join(self, bufs, last_op):
        for b in bufs:
            if b.w is not None and b.w.dma and b.w.dsem == last_op.dsem:
                b.w = last_op

    def alias(self, new_bufs, old_bufs):
        latest = {}
        dmas = []
        for b in old_bufs:
            for o in list(b.r) + ([b.w] if b.w is not None else []):
                if o.dma:
                    dmas.append(o)
                else:
                    cur = latest.get(o.eng)
                    if cur is None or cur.idx < o.idx:
                        latest[o.eng] = o
        ops = list(latest.values()) + dmas
        for b in new_bufs:
            b.r = list(b.r) + ops

    def finalize(self):
        for k, lst in self.dma_groups.items():
            for i, o in enumerate(lst):
                o.semval = 16 * (i + 1)
        known = {e: {} for e in self.ENGS}
        for o in self.ops:
            kn = known[o.eng]
            need = {}
            for d in o.deps:
                src = ("dma", d.dsem) if d.dma else d.eng
                pos = d.semval if d.dma else d.idx
                if kn.get(src, -1) >= pos:
                    continue
                if src not in need or need[src][0] < pos:
                    need[src] = (pos, d)
            for src, (pos, d) in need.items():
                kn[src] = pos
                if not d.dma:
                    d.signal = True
                o.waits.append(d)
        for e in self.ENGS:
            c = 0
            for o in self.streams[e]:
                if o.dma:
                    continue
                if o.signal:
                    c += 1
                    o.semval = c

    def emit_stream(self, engname, engobj, sems, dma_sems):
        for o in self.streams[engname]:
            for d in o.waits:
                if d.dma:
                    engobj.wait_ge(dma_sems[d.dsem], d.semval)
                else:
                    engobj.wait_ge(sems[d.eng], d.semval)
            ins = None
            for (name, a_, k_) in o.fn:
                ins = getattr(engobj, name)(*a_, **k_)
            if ins is None:
                continue
            if o.dma:
                ins.then_inc(dma_sems[o.dsem], 16)
            elif o.signal:
                ins.then_inc(sems[o.eng], 1)


SUB = 99
N_WARM = 0


class _Stop(Exception):
    pass


def build_program(NB=2, layers=(0, 1), taps=(), upto=99, wdepth=DEPTH):
    nc = bass.Bass("TRN2", target_bir_lowering=False)
    DT = {}

    def din(name, shape, dt=F32):
        DT[name] = nc.dram_tensor(name, list(shape), dt, kind="ExternalInput").ap()

    def dout(name, shape, dt=F32):
        DT[name] = nc.dram_tensor(name, list(shape), dt, kind="ExternalOutput").ap()

    din("x", [NB, NT, 128, D])
    din("mem", [NB, 2, 128, D])
    din("pos", [NB, 128, NT], I32)
    din("ident", [128, 128])
    din("maskb", [128, MASK_W])
    din("invf", [128, 24])
    din("gT", [128, DEPTH * 7 * 8])
    din("grow", [DEPTH * 7, D])
    din("w_in", [wdepth, 128, 8, D_IN])
    din("w_out", [wdepth, 128, 8, D])
    din("lam", [DEPTH, 256])
    din("subln", [128, DEPTH])
    din("qng", [128, DEPTH * 2])
    din("kvng", [128, DEPTH])
    din("wq_up", [wdepth, 128, 2, 384])
    din("wkv_up", [wdepth, 128, 512])
    din("w_mq", [wdepth, 128, 8, 256])
    din("w_mkv", [wdepth, 128, 8, 512])
    din("w_mo", [wdepth, 128, 2, D])
    din("w_g", [wdepth, NFC, 128, 8, 128])
    din("w_u", [wdepth, NFC, 128, 8, 128])
    din("w_d", [wdepth, 128, NFC, D])
    dout("out", [NB, NT, 128, D])
    for name, shape, dt in taps:
        dout(name, shape, dt)

    S = Sched()
    B = S.B
    es = ExitStack()
    with es:
        def sb(name, shape, dt):
            return es.enter_context(nc.sbuf_tensor(name, list(shape), dt))


        xs = sb("xs", [128, NT, D], F32)
        ident = sb("ident_s", [128, 128], BF16)
        ones = sb("ones_s", [128, 128], BF16)
        maskb = sb("maskb_s", [128, MASK_W], BF16)
        invf = sb("invf_s", [128, 24], F32)
        epsT = sb("eps_s", [128, 1], F32)
        subln = sb("subln_s", [128, DEPTH], F32)
        qng = sb("qng_s", [128, DEPTH * 2], F32)
        kvng = sb("kvng_s", [128, DEPTH], F32)
        sublnS = sb("sublnS", [128, DEPTH], F32)
        lamraw = sb("lamraw", [128, 256], F32)
        lamt = sb("lamt", [128, 8], F32)
        neglam = sb("neglam", [128, DEPTH], F32)
        posi = sb("posi", [128, NT], I32)
        posf = sb("posf", [128, NT], F32)
        angP = sb("angP", [128, NT, 8], F32)
        angC = sb("angC", [128, NT, 16], F32)
        cosP = sb("cosP", [128, NT, 8], F32)
        sinP = sb("sinP", [128, NT, 8], F32)
        cosC = sb("cosC", [128, NT, 16], F32)
        sinC = sb("sinC", [128, NT, 16], F32)
        gpre = sb("gpre", [128, D], F32)
        gb = sb("gb", [128, D], F32)
        junk = sb("junk", [128, D], BF16)
        hn = [sb("hn%d" % i, [128, D], BF16) for i in range(2)]
        hTt = [sb("hTt%d" % i, [128, 8, 128], BF16) for i in range(2)]
        ssq = sb("ssq", [128, 64], F32)
        rstd = sb("rstd", [128, 64], F32)
        rcache = sb("rcache", [128, NT], F32)
        PT = [sb("PT%d" % i, [128, 1024], BF16) for i in range(3)]
        stA = [sb("stA%d" % i, [128, 512], BF16) for i in range(2)]
        ropeT = [sb("ropeT%d" % i, [128, 128], BF16) for i in range(2)]
        ropeU = [sb("ropeU%d" % i, [128, 128], BF16) for i in range(2)]
        stC = [sb("stC%d" % i, [128, 384], BF16) for i in range(2)]
        stKR = [sb("stKR%d" % i, [128, 32], BF16) for i in range(2)]
        stQC = [sb("stQC%d" % i, [128, 192], BF16) for i in range(2)]
        stKC = [sb("stKC%d" % i, [128, 2, 96], BF16) for i in range(2)]
        cT = [sb("cT%d" % i, [128, 3, 128], BF16) for i in range(2)]
        finAall = sb("finAall", [128, 4, 512], F32)
        finA = [finAall[:, i, :] for i in range(4)]
        finB = sb("finB", [128, 512], BF16)
        sg = [sb("sg%d" % i, [128, 512], BF16) for i in range(2)]
        ARENA = 45056
        arena = sb("arena", [128, ARENA], BF16)
        psum = es.enter_context(nc.psum_tensor("psum", [128, 4096], F32))

        def pbank(b0, nb=1):
            return psum[:, b0 * 512:(b0 + nb) * 512]

        def pbank_bf(b0):
            return psum[:, b0 * 512:(b0 + 1) * 512].bitcast(BF16)

        PB = lambda i: B("ps", i)

        O0, Q0, W0 = 0, 16384, 32768
        OT = arena[:, O0:O0 + 16384].rearrange("p (c t) -> p c t", c=8)
        w_pass = arena[:, W0:W0 + 6144].rearrange("p (c n) -> p c n", c=8)
        wq_up_s = arena[:, W0 + 6144:W0 + 6144 + 768].rearrange("p (c n) -> p c n", c=2)
        wkv_up_s = arena[:, W0 + 6912:W0 + 6912 + 512]
        w_out_s = arena[:, W0:W0 + 8192].rearrange("p (c n) -> p c n", c=8)
        QAT = arena[:, Q0:Q0 + 4096].rearrange("p (h t) -> p h t", h=2)
        KAT = arena[:, Q0 + 4096:Q0 + 8192].rearrange("p (h t) -> p h t", h=2)
        VA = arena[:, Q0 + 8192:Q0 + 12288].rearrange("p (t n) -> p t n", t=NT)
        QBT = arena[:, Q0:Q0 + 4096].rearrange("p (j t) -> p j t", j=2)
        KBT = arena[:, Q0 + 4096:Q0 + 8192].rearrange("p (j t) -> p j t", j=2)
        VB = arena[:, Q0 + 8192:Q0 + 16384].rearrange("p (t h n) -> p t h n", t=NT, h=4)
        QCT = arena[:, Q0:Q0 + 4096].rearrange("p (h t) -> p h t", h=2)
        KCT = arena[:, Q0 + 4096:Q0 + 8192].rearrange("p (h t) -> p h t", h=2)
        VC = arena[:, Q0 + 8192:Q0 + 12288].rearrange("p (t h n) -> p t h n", t=NT, h=2)
        Pacc = arena[:, W0 + 8192:W0 + 12288].bitcast(F32).rearrange("p (a n) -> p a n", a=2)
        QmT = arena[:, Q0:Q0 + 4096].rearrange("p (j t) -> p j t", j=2)
        OmT = arena[:, Q0 + 4096:Q0 + 8192].rearrange("p (j t) -> p j t", j=2)
        memT = arena[:, Q0 + 8192:Q0 + 10240].rearrange("p (c t) -> p c t", c=8)
        KmT = arena[:, Q0 + 10240:Q0 + 10752].rearrange("p (j t) -> p j t", j=2)
        Vm = arena[:, Q0 + 10752:Q0 + 11776].rearrange("p (t h n) -> p t h n", t=2, h=4)
        w_mq_s = arena[:, Q0 + 11776:Q0 + 13824].rearrange("p (c n) -> p c n", c=8)
        w_mo_s = arena[:, Q0 + 13824:Q0 + 15872].rearrange("p (c n) -> p c n", c=2)
        w_mkv_s = arena[:, W0:W0 + 4096].rearrange("p (c n) -> p c n", c=8)
        memf = arena[:, W0 + 4096:W0 + 8192].bitcast(F32).rearrange("p (t n) -> p t n", t=2)
        hTq = arena[:, O0:O0 + 4096].rearrange("p (c t) -> p c t", c=8)
        HT = arena[:, O0 + 4096:O0 + 4096 + 11264].rearrange("p (f t) -> p f t", f=NFC)
        w_d_s = arena[:, Q0:Q0 + 22528].rearrange("p (f n) -> p f n", f=NFC)
        NSLOT = 3
        SL0 = Q0 + 22528
        wgu = [arena[:, SL0 + i * 2048:SL0 + (i + 1) * 2048].rearrange("p (g c n) -> p g c n", g=2, c=8) for i in range(NSLOT)]
        assert SL0 + NSLOT * 2048 <= ARENA

        cnt = {"ss": 0}

        def stat_col():
            cnt["ss"] = (cnt["ss"] + 1) % 64
            return cnt["ss"]

        def dma_cast(dst_ap, src_ap, wbufs, key):
            return S.op("pool", lambda e: e.dma_start(out=dst_ap, in_=src_ap), writes=list(wbufs), dma_key=key)

        def dma_sp(dst_ap, src_ap, wbufs, key, reads=()):
            return S.op("sp", lambda e: e.dma_start(out=dst_ap, in_=src_ap), reads=list(reads), writes=list(wbufs), dma_key=key)

        def rstd_to(dst_ap, dst_buf, src_ap, src_bufs, n):
            S.op("act", lambda e: e.activation(out=dst_ap, in_=src_ap, func=AF.Ln, scale=1.0 / n, bias=epsT[:, 0:1]),
                 reads=list(src_bufs) + [B("eps")], writes=[dst_buf])
            S.op("act", lambda e: e.activation(out=dst_ap, in_=dst_ap, func=AF.Exp, scale=-0.5), reads=[dst_buf], writes=[dst_buf])

        dma_cast(ident[:], DT["ident"][:, :], [B("ident")], "c_ident")
        dma_cast(maskb[:], DT["maskb"][:, :], [B("maskb")], "c_maskb")
        dma_sp(invf[:], DT["invf"][:, :], [B("invf")], "c_invf")
        dma_sp(subln[:], DT["subln"][:, :], [B("subln")], "c_subln")
        dma_sp(qng[:], DT["qng"][:, :], [B("qng")], "c_qng")
        dma_sp(kvng[:], DT["kvng"][:, :], [B("kvng")], "c_kvng")
        S.op("dve", lambda e: e.memset(ones[:], 1.0), writes=[B("ones")])
        S.op("dve", lambda e: e.memset(epsT[:], EPS), writes=[B("eps")])
        for l in range(DEPTH):
            lam_init = 0.8 - 0.6 * math.exp(-0.3 * l)
            dma_sp(lamraw[:], DT["lam"][l:l + 1, :].partition_broadcast(128), [B("lamraw")], ("c_lam", l))
            S.op("dve", lambda e: e.tensor_tensor(out=lamraw[:, 0:64], in0=lamraw[:, 0:64], in1=lamraw[:, 64:128], op=ALU.mult), reads=[B("lamraw")], writes=[B("lamraw")])
            S.op("dve", lambda e: e.tensor_tensor(out=lamraw[:, 128:192], in0=lamraw[:, 128:192], in1=lamraw[:, 192:256], op=ALU.mult), reads=[B("lamraw")], writes=[B("lamraw")])
            S.op("dve", lambda e: e.tensor_reduce(out=lamt[:, 0:1], in_=lamraw[:, 0:64], axis=mybir.AxisListType.X, op=ALU.add), reads=[B("lamraw")], writes=[B("lamt")])
            S.op("dve", lambda e: e.tensor_reduce(out=lamt[:, 1:2], in_=lamraw[:, 128:192], axis=mybir.AxisListType.X, op=ALU.add), reads=[B("lamraw"), B("lamt")], writes=[B("lamt")])
            S.op("act", lambda e: e.activation(out=lamt[:, 2:4], in_=lamt[:, 0:2], func=AF.Exp), reads=[B("lamt")], writes=[B("lamt")])
            S.op("dve", lambda e, l=l, li=lam_init: e.scalar_tensor_tensor(out=neglam[:, l:l + 1], in0=lamt[:, 3:4], scalar=-li, in1=lamt[:, 2:3], op0=ALU.add, op1=ALU.subtract),
                 reads=[B("lamt")], writes=[B("neglam", l)])
            S.op("dve", lambda e, l=l, li=lam_init: e.tensor_scalar(out=sublnS[:, l:l + 1], in0=subln[:, l:l + 1], scalar1=1.0 - li, scalar2=None, op0=ALU.mult),
                 reads=[B("subln")], writes=[B("sublnS", l)])

        angK_all = finA[0][:, 0:256].rearrange("p (t w) -> p t w", t=NT)
        angI_all = finA[1][:, 0:256].bitcast(I32).rearrange("p (t w) -> p t w", t=NT)
        angT_all = finA[2][:, 0:256].rearrange("p (t w) -> p t w", t=NT)

        def range_reduce_sin(dst, ang, width, bname):
            BA = B(*bname) if isinstance(bname, tuple) else B(bname)
            k = angK_all[:, :, 0:width]
            ki = angI_all[:, :, 0:width]
            bk, bi = B("finA", 0), B("finA", 1)
            S.op("dve", lambda e: e.tensor_scalar(out=k, in0=ang, scalar1=1.0 / (2 * math.pi), scalar2=None, op0=ALU.mult), reads=[BA], writes=[bk])
            S.op("dve", lambda e: e.tensor_copy(out=ki, in_=k), reads=[bk], writes=[bi])
            S.op("dve", lambda e: e.tensor_copy(out=k, in_=ki), reads=[bi], writes=[bk])
            S.op("dve", lambda e: e.scalar_tensor_tensor(out=ang, in0=k, scalar=-2 * math.pi, in1=ang, op0=ALU.mult, op1=ALU.add), reads=[bk, BA], writes=[BA])
            S.op("dve", lambda e: e.tensor_scalar(out=k, in0=ang, scalar1=math.pi, scalar2=-2 * math.pi, op0=ALU.is_gt, op1=ALU.mult), reads=[BA], writes=[bk])
            S.op("dve", lambda e: e.tensor_tensor(out=ang, in0=ang, in1=k, op=ALU.add), reads=[bk, BA], writes=[BA])
            S.op("dve", lambda e: e.tensor_scalar(out=k, in0=ang, scalar1=-math.pi, scalar2=2 * math.pi, op0=ALU.is_lt, op1=ALU.mult), reads=[BA], writes=[bk])
            S.op("dve", lambda e: e.tensor_tensor(out=ang, in0=ang, in1=k, op=ALU.add), reads=[bk, BA], writes=[BA])
            S.op("act", lambda e: e.activation(out=dst, in_=ang, func=AF.Sin), reads=[BA], writes=[B("ropetab")])

        def norm_tile(t, first, dst_ap, dst_buf, par):
            xap = xs[:, t, :]
            if first:
                col = stat_col()
                S.op("act", lambda e: e.activation(out=junk[:], in_=xap, func=AF.Square, accum_out=ssq[:, col:col + 1]),
                     reads=[B("x", t)], writes=[B("junk"), B("ssq", col)])
                rstd_to(rcache[:, t:t + 1], B("rcache", t), ssq[:, col:col + 1], [B("ssq", col)], D)
            hb = hn[par]
            S.op("dve", lambda e: e.scalar_tensor_tensor(out=hb[:], in0=xap, scalar=rcache[:, t:t + 1], in1=gpre[:], op0=ALU.mult, op1=ALU.mult),
                 reads=[B("x", t), B("rcache", t), B("gpre")], writes=[B("hn", par)])
            pv = pbank_bf(7)
            for c in range(8):
                S.op("pe", lambda e, c=c: e.transpose(pv[:, c * 128:(c + 1) * 128], hb[:, c * 128:(c + 1) * 128], ident[:, :]),
                     reads=[B("hn", par), B("ident")], writes=[PB(7)])
            if t % 2 == 0:
                S.op("act", lambda e: e.activation(out=dst_ap, in_=pv[:, :].rearrange("p (c k) -> p c k", c=8), func=AF.Copy), reads=[PB(7)], writes=[dst_buf])
            else:
                S.op("dve", lambda e: e.tensor_copy(out=dst_ap, in_=pv[:, :].rearrange("p (c k) -> p c k", c=8)), reads=[PB(7)], writes=[dst_buf])

        def load_grow(dst, bufname, l, n):
            r = l * 7 + n
            dma_sp(dst[:], DT["grow"][r:r + 1, :].partition_broadcast(128), [B(bufname)], bufname)

        def post_norm_residual(ybank, t):
            yap = pbank(ybank, 2)
            col = stat_col()
            tmp = tmp_pn[t % 2]
            S.op("act", lambda e: e.activation(out=junk[:], in_=yap, func=AF.Square, accum_out=ssq[:, col:col + 1]),
                 reads=[PB(ybank), PB(ybank + 1)], writes=[B("junk"), B("ssq", col)])
            rstd_to(rstd[:, col:col + 1], B("rstd", col), ssq[:, col:col + 1], [B("ssq", col)], D)
            S.op("dve", lambda e: e.scalar_tensor_tensor(out=tmp, in0=yap, scalar=rstd[:, col:col + 1], in1=gb[:], op0=ALU.mult, op1=ALU.mult),
                 reads=[PB(ybank), PB(ybank + 1), B("rstd", col), B("gb")], writes=[B("finA", 2), B("finA", 3)])
            S.op("pool", lambda e: e.tensor_tensor(out=xs[:, t, :], in0=xs[:, t, :], in1=tmp, op=ALU.add),
                 reads=[B("x", t), B("finA", 2), B("finA", 3)], writes=[B("x", t)])

        tmp_pn = [finAall[:, 2:4, :].rearrange("p a b -> p (a b)")] * 2

        def rope_inplace(v, ngroups, roff, half, cos_t, sin_t, t, bufs, par):
            x1 = v[:, 0:ngroups, roff:roff + half]
            x2 = v[:, 0:ngroups, roff + half:roff + 2 * half]
            c = cos_t[:, t:t + 1, :].to_broadcast([128, ngroups, half])
            s = sin_t[:, t:t + 1, :].to_broadcast([128, ngroups, half])
            T = ropeT[par][:, 0:ngroups * 2 * half].rearrange("p (g d) -> p g d", g=ngroups)
            U = ropeU[par][:, 0:ngroups * 2 * half].rearrange("p (g d) -> p g d", g=ngroups)
            ta = T[:, 0:ngroups, 0:half]
            tb = T[:, 0:ngroups, half:2 * half]
            ua = U[:, 0:ngroups, 0:half]
            ub = U[:, 0:ngroups, half:2 * half]
            rb = [B("ropeT", par), B("ropeU", par)]
            S.op("pool", lambda e: e.tensor_tensor(out=ta, in0=x1, in1=c, op=ALU.mult), reads=bufs + [B("ropetab")], writes=[rb[0]])
            S.op("pool", lambda e: e.tensor_tensor(out=tb, in0=x2, in1=c, op=ALU.mult), reads=bufs + [B("ropetab"), rb[0]], writes=[rb[0]])
            S.op("pool", lambda e: e.tensor_tensor(out=ua, in0=x2, in1=s, op=ALU.mult), reads=bufs + [B("ropetab")], writes=[rb[1]])
            S.op("pool", lambda e: e.tensor_tensor(out=ub, in0=x1, in1=s, op=ALU.mult), reads=bufs + [B("ropetab"), rb[1]], writes=[rb[1]])
            S.op("pool", lambda e: e.tensor_tensor(out=x1, in0=ta, in1=ua, op=ALU.subtract), reads=rb + bufs, writes=bufs)
            S.op("pool", lambda e: e.tensor_tensor(out=x2, in0=tb, in1=ub, op=ALU.add), reads=rb + bufs, writes=bufs)

        def attention_pairs(npairs, q_ap, k_ap, v_ap, qbufs, kbuf, vbuf, scale, ot_dst, ot_buf, masked, kt_list_fn, part_of):
            acc = (0, 1)
            sbanks = [(2, 3), (4, 5), (6, 7)]
            blocks = [(pj, qc) for pj in range(npairs) for qc in range(4)]

            def emit_scores(pj, qc, i, kt):
                sb_ = sbanks[i % 3]
                for hh in range(2):
                    S.op("pe", lambda e, hh=hh: e.matmul(pbank(sb_[hh]), k_ap(pj, hh, kt), q_ap(pj, hh, qc), start=True, stop=True),
                         reads=[kbuf(kt)] + qbufs(qc), writes=[PB(sb_[hh])])

            def emit_exp(qc, i, kt):
                sb_ = sbanks[i % 3]
                pt = PT[i % 3]
                S.op("act", lambda e: e.activation(out=pt[:], in_=pbank(sb_[0], 2), func=AF.Exp, scale=scale),
                     reads=[PB(sb_[0]), PB(sb_[1])], writes=[B("PT", i % 3)])
                if masked:
                    c0 = qc * 512 - kt * 128 + MASK_C0
                    m = maskb[:, c0:c0 + 512].unsqueeze(1).to_broadcast([128, 2, 512])
                    ptv = pt[:].rearrange("p (a q) -> p a q", a=2)
                    S.op("dve", lambda e: e.tensor_tensor(out=ptv, in0=ptv, in1=m, op=ALU.mult),
                         reads=[B("PT", i % 3), B("maskb")], writes=[B("PT", i % 3)])

            def emit_pv(pj, i, kt, n):
                pt = PT[i % 3]
                for hh in range(2):
                    S.op("pe", lambda e, hh=hh: e.matmul(pbank(acc[hh]), v_ap(pj, hh, kt), pt[:, hh * 512:(hh + 1) * 512], start=(i == 0), stop=(i == n - 1)),
                         reads=[vbuf(kt), B("PT", i % 3)], writes=[PB(acc[hh])])

            def finalize(pj, qc):
                for hh in range(2):
                    i_ = part_of(pj, hh)
                    lo, hi = i_ * 64, (i_ + 1) * 64
                    dlo, dhi = (1 - i_) * 64, (2 - i_) * 64
                    rd = finA[hh]
                    S.op("act", lambda e: e.activation(out=rd[lo:hi, :], in_=pbank(acc[hh])[dlo:dhi, :], func=AF.Ln),
                         reads=[PB(acc[hh])], writes=[B("finA", hh)])
                    S.op("act", lambda e: e.activation(out=rd[lo:hi, :], in_=rd[lo:hi, :], func=AF.Exp, scale=-1.0), reads=[B("finA", hh)], writes=[B("finA", hh)])
                    dst = ot_dst(pj, hh, qc)
                    S.op("dve", lambda e: e.tensor_tensor(out=dst, in0=pbank(acc[hh])[lo:hi, :], in1=rd[lo:hi, :], op=ALU.mult),
                         reads=[PB(acc[hh]), B("finA", hh)], writes=[ot_buf(pj, hh, qc)])

            def prologue(pj, qc):
                kts = kt_list_fn(qc)
                emit_scores(pj, qc, 0, kts[0])
                if len(kts) > 1:
                    emit_scores(pj, qc, 1, kts[1])

            prologue(*blocks[0])
            for bi, (pj, qc) in enumerate(blocks):
                kts = kt_list_fn(qc)
                n = len(kts)
                for i in range(n):
                    if i + 2 < n:
                        emit_scores(pj, qc, i + 2, kts[i + 2])
                    emit_exp(qc, i, kts[i])
                    emit_pv(pj, i, kts[i], n)
                if bi + 1 < len(blocks):
                    prologue(*blocks[bi + 1])
                finalize(pj, qc)

        def proj_pass(l, col_ranges, first, stages):
            ncols = sum(c1 - c0 for c0, c1 in col_ranges)
            for c in range(8):
                o = 0
                for (c0, c1) in col_ranges:
                    lastw = dma_cast(w_pass[:, c, o:o + (c1 - c0)], DT["w_in"][l, :, c, c0:c1], [B("w_pass", c)], "w_pass")
                    o += c1 - c0
            S.join([B("w_pass", c) for c in range(8)], lastw)
            nb = (ncols + 511) // 512
            norm_tile(0, first, hTt[0][:, :, :], B("hTt", 0), 0)
            for i in range(NT + len(stages) - 1):
                if i + 1 < NT:
                    p1 = (i + 1) % 2
                    norm_tile(i + 1, first, hTt[p1][:, :, :], B("hTt", p1), p1)
                if i < NT:
                    par = i % 2
                    base = par * 2
                    for c in range(8):
                        for n in range(nb):
                            w = min(512, ncols - n * 512)
                            S.op("pe", lambda e, c=c, n=n, w=w, par=par: e.matmul(pbank(base + n)[:, 0:w], hTt[par][:, c, :], w_pass[:, c, n * 512:n * 512 + w], start=(c == 0), stop=(c == 7)),
                                 reads=[B("hTt", par), B("w_pass", c)], writes=[PB(base + n)])
                    for _ in range(N_WARM):
                        S.op("pe", lambda e: e.matmul(pbank(6), ident[:, :], w_pass[:, 0, 0:512], start=True, stop=True),
                             reads=[B("ident"), B("w_pass", 0)], writes=[PB(6)])
                for k, fn in enumerate(stages):
                    t = i - k
                    if 0 <= t < NT:
                        fn(t, t % 2, (t % 2) * 2)

        for b in range(NB):
            if b == 0:
                for t in range(NT):
                    dma_sp(xs[:, t, :], DT["x"][b, t, :, :], [B("x", t)], ("x", t))
            dma_sp(posi[:], DT["pos"][b, :, :], [B("posi")], "posi")
            S.op("dve", lambda e: e.tensor_copy(out=posf[:], in_=posi[:]), reads=[B("posi")], writes=[B("posf")])
            S.op("dve", lambda e: e.tensor_tensor(out=angP[:], in0=posf[:].unsqueeze(2).to_broadcast([128, NT, 8]), in1=invf[:, 0:8].unsqueeze(1).to_broadcast([128, NT, 8]), op=ALU.mult),
                 reads=[B("posf"), B("invf")], writes=[B("angP")])
            S.op("dve", lambda e: e.tensor_tensor(out=angC[:], in0=posf[:].unsqueeze(2).to_broadcast([128, NT, 16]), in1=invf[:, 8:24].unsqueeze(1).to_broadcast([128, NT, 16]), op=ALU.mult),
                 reads=[B("posf"), B("invf")], writes=[B("angC")])
            aT8 = angT_all[:, :, 0:8]
            S.op("dve", lambda e: e.tensor_scalar(out=aT8, in0=angP[:], scalar1=math.pi / 2, scalar2=None, op0=ALU.add), reads=[B("angP")], writes=[B("finA", 2)])
            range_reduce_sin(cosP[:], aT8, 8, ("finA", 2))
            range_reduce_sin(sinP[:], angP[:], 8, "angP")
            S.op("dve", lambda e: e.tensor_scalar(out=angT_all[:], in0=angC[:], scalar1=math.pi / 2, scalar2=None, op0=ALU.add), reads=[B("angC")], writes=[B("finA", 2)])
            range_reduce_sin(cosC[:], angT_all[:], 16, ("finA", 2))
            range_reduce_sin(sinC[:], angC[:], 16, "angC")

            def stage(n):
                if n > upto:
                    raise _Stop()

            for l in layers:
              try:
                  stage(2)
                  load_grow(gpre, "gpre", l, 0)
                  scaleA = 64 ** -0.5
                  for j in range(2):
                      def tileA(t, par, base):
                          st = stA[par]
                          S.op("act", lambda e: e.activation(out=st[:], in_=pbank(base), func=AF.Copy), reads=[PB(base)], writes=[B("stA", par)])
                          S.op("dve", lambda e: e.tensor_copy(out=VA[:, t, :], in_=pbank(base + 1)[:, 0:256]), reads=[PB(base + 1)], writes=[B("VA", t)])
                          rope_inplace(st[:].rearrange("p (g d) -> p g d", d=64), 8, 0, 8, cosP, sinP, t, [B("stA", par)], par)

                      def tileA2(t, par, base):
                          st = stA[par]
                          bank = 4 + par
                          pv = pbank_bf(bank)
                          for k4 in range(4):
                              S.op("pe", lambda e, k4=k4: e.transpose(pv[:, k4 * 128:(k4 + 1) * 128], st[:, k4 * 128:(k4 + 1) * 128], ident[:, :]),
                                   reads=[B("stA", par), B("ident")], writes=[PB(bank)])
                          S.op("act", lambda e: e.activation(out=QAT[:, :, t * 128:(t + 1) * 128], in_=pv[:, 0:256].rearrange("p (h k) -> p h k", h=2), func=AF.Copy),
                               reads=[PB(bank)], writes=[B("QT", t)])
                          S.op("dve", lambda e: e.tensor_copy(out=KAT[:, :, t * 128:(t + 1) * 128], in_=pv[:, 256:512].rearrange("p (h k) -> p h k", h=2)),
                               reads=[PB(bank)], writes=[B("KT", t)])
                      proj_pass(l, [(j * 256, (j + 1) * 256), (512 + j * 256, 512 + (j + 1) * 256), (1024 + j * 256, 1024 + (j + 1) * 256)], (j == 0), [tileA, tileA2])
                      stage(3 if j == 0 else 4)
                      sbanksA = [(2, 3), (4, 5), (6, 7)]
                      itersA = [(hh, qc) for hh in range(2) for qc in range(4)]

                      def a_scores(hh, qc, kt):
                          sb_ = sbanksA[kt % 3]
                          for m in range(2):
                              S.op("pe", lambda e, m=m: e.matmul(pbank(sb_[m]), KAT[m * 64:(m + 1) * 64, hh, kt * 128:(kt + 1) * 128], QAT[m * 64:(m + 1) * 64, hh, qc * 512:(qc + 1) * 512], start=True, stop=True),
                                   reads=[B("KT", kt)] + [B("QT", qc * 4 + i) for i in range(4)], writes=[PB(sb_[m])])

                      def a_exp(kt, ai):
                          sb_ = sbanksA[kt % 3]
                          pt = PT[kt % 3]
                          S.op("act", lambda e: e.activation(out=pt[:], in_=pbank(sb_[0], 2), func=AF.Exp, scale=scaleA),
                               reads=[PB(sb_[0]), PB(sb_[1])], writes=[B("PT", kt % 3)])
                          pa = Pacc[:, ai, :]
                          if kt == 0:
                              S.op("dve", lambda e: e.tensor_copy(out=pa, in_=pt[:]), reads=[B("PT", kt % 3)], writes=[B("Pacc", ai, 0), B("Pacc", ai, 1)])
                          else:
                              S.op("dve", lambda e: e.tensor_tensor(out=pa, in0=pa, in1=pt[:], op=ALU.add), reads=[B("PT", kt % 3), B("Pacc", ai, 0), B("Pacc", ai, 1)], writes=[B("Pacc", ai, 0), B("Pacc", ai, 1)])

                      def a_pv(hh, kt):
                          pt = PT[kt % 3]
                          for m in range(2):
                              S.op("pe", lambda e, m=m: e.matmul(pbank(m), VA[:, kt, hh * 128:(hh + 1) * 128], pt[:, m * 512:(m + 1) * 512], start=(kt == 0), stop=(kt == NT - 1)),
                                   reads=[B("VA", kt), B("PT", kt % 3)], writes=[PB(m)])

                      def a_finalize(hh, qc, ai):
                          h = 2 * j + hh
                          rD0, rD1, oo, lnss = finA
                          S.op("dve", lambda e: e.tensor_copy(out=junk[:], in_=Pacc[:, ai, :]), reads=[B("Pacc", ai, 0), B("Pacc", ai, 1)], writes=[B("junk")])
                          for m in range(2):
                              S.op("pe", lambda e, m=m: e.matmul(pbank(6 + m), ones[:, :], junk[:, m * 512:(m + 1) * 512], start=True, stop=True), reads=[B("ones"), B("junk")], writes=[PB(6 + m)])
                          S.op("act", lambda e: e.activation(out=rD0[:], in_=pbank(6), func=AF.Ln), reads=[PB(6)], writes=[B("finA", 0)])
                          S.op("act", lambda e: e.activation(out=rD1[:], in_=pbank(7), func=AF.Ln), reads=[PB(7)], writes=[B("finA", 1)])
                          S.op("act", lambda e: e.activation(out=rD0[:], in_=rD0[:], func=AF.Exp, scale=-1.0), reads=[B("finA", 0)], writes=[B("finA", 0)])
                          S.op("act", lambda e: e.activation(out=rD1[:], in_=rD1[:], func=AF.Exp, scale=-1.0), reads=[B("finA", 1)], writes=[B("finA", 1)])
                          S.op("dve", lambda e: e.tensor_tensor(out=oo[:], in0=pbank(0), in1=rD0[:], op=ALU.mult), reads=[PB(0), B("finA", 0)], writes=[B("finA", 2)])
                          S.op("dve", lambda e: e.scalar_tensor_tensor(out=rD1[:], in0=pbank(1), scalar=neglam[:, l:l + 1], in1=rD1[:], op0=ALU.mult, op1=ALU.mult),
                               reads=[PB(1), B("finA", 1), B("neglam", l)], writes=[B("finA", 1)])
                          S.op("pool", lambda e: e.tensor_tensor(out=oo[:], in0=oo[:], in1=rD1[:], op=ALU.add), reads=[B("finA", 1), B("finA", 2)], writes=[B("finA", 2)])
                          S.op("act", lambda e: e.activation(out=finB[:], in_=oo[:], func=AF.Square), reads=[B("finA", 2)], writes=[B("finB")])
                          S.op("pe", lambda e: e.matmul(pbank(6), ones[:, :], finB[:, :], start=True, stop=True), reads=[B("ones"), B("finB")], writes=[PB(6)])
                          rstd_to(lnss[:], B("finA", 3), pbank(6), [PB(6)], 128)
                          S.op("dve", lambda e: e.scalar_tensor_tensor(out=OT[:, h, qc * 512:(qc + 1) * 512], in0=oo[:], scalar=sublnS[:, l:l + 1], in1=lnss[:], op0=ALU.mult, op1=ALU.mult),
                               reads=[B("finA", 2), B("finA", 3), B("sublnS", l)], writes=[B("OT", h, qc, 0), B("OT", h, qc, 1)])

                      a_scores(itersA[0][0], itersA[0][1], 0)
                      a_scores(itersA[0][0], itersA[0][1], 1)
                      for ai_, (hh, qc) in enumerate(itersA):
                          ai = ai_ % 2
                          for kt in range(NT):
                              if kt + 2 < NT:
                                  a_scores(hh, qc, kt + 2)
                              a_exp(kt, ai)
                              a_pv(hh, kt)
                          if ai_ + 1 < len(itersA):
                              a_scores(itersA[ai_ + 1][0], itersA[ai_ + 1][1], 0)
                              a_scores(itersA[ai_ + 1][0], itersA[ai_ + 1][1], 1)
                          a_finalize(hh, qc, ai)

                  stage(5)
                  S.alias([B("VBt", t) for t in range(NT)] + [B("Vones")], [B("VA", t) for t in range(NT)] + [B("Vones")])
                  for h in range(4):
                      i_ = h % 2
                      S.op("pool", lambda e, h=h, i_=i_: e.memset(VB[:, :, h, (1 - i_) * 64:(2 - i_) * 64], 1.0), writes=[B("Vones")])

                  def tileB(t, par, base):
                      st = stA[par]
                      S.op("act", lambda e: e.activation(out=st[:], in_=pbank(base), func=AF.Copy), reads=[PB(base)], writes=[B("stA", par)])
                      for h in range(4):
                          i_ = h % 2
                          S.op("dve", lambda e, h=h, i_=i_: e.tensor_copy(out=VB[:, t, h, i_ * 64:(i_ + 1) * 64], in_=pbank(base + 1)[:, h * 64:(h + 1) * 64]),
                               reads=[PB(base + 1), B("Vones")], writes=[B("VBt", t)])
                      rope_inplace(st[:].rearrange("p (g d) -> p g d", d=64), 8, 0, 8, cosP, sinP, t, [B("stA", par)], par)

                  def tileB2(t, par, base):
                      st = stA[par]
                      bank = 4 + par
                      pv = pbank_bf(bank)
                      for k4 in range(4):
                          S.op("pe", lambda e, k4=k4: e.transpose(pv[:, k4 * 128:(k4 + 1) * 128], st[:, k4 * 128:(k4 + 1) * 128], ident[:, :]),
                               reads=[B("stA", par), B("ident")], writes=[PB(bank)])
                      S.op("act", lambda e: e.activation(out=QBT[:, :, t * 128:(t + 1) * 128], in_=pv[:, 0:256].rearrange("p (h k) -> p h k", h=2), func=AF.Copy),
                           reads=[PB(bank)], writes=[B("QT", t)])
                      S.op("dve", lambda e: e.tensor_copy(out=KBT[:, :, t * 128:(t + 1) * 128], in_=pv[:, 256:512].rearrange("p (h k) -> p h k", h=2)),
                           reads=[PB(bank)], writes=[B("KT", t)])
                  proj_pass(l, [(1536, 2304)], False, [tileB, tileB2])

                  def kts_B(qc):
                      lo = max(0, (qc * 512 - 1024) // 128)
                      hi = min(NT - 1, (qc * 512 + 511 + 1024) // 128)
                      return list(range(lo, hi + 1))
                  attention_pairs(
                      2,
                      q_ap=lambda pj, hh, qc: QBT[hh * 64:(hh + 1) * 64, pj, qc * 512:(qc + 1) * 512],
                      k_ap=lambda pj, hh, kt: KBT[hh * 64:(hh + 1) * 64, pj, kt * 128:(kt + 1) * 128],
                      v_ap=lambda pj, hh, kt: VB[:, kt, 2 * pj + hh, :],
                      qbufs=lambda qc: [B("QT", qc * 4 + i) for i in range(4)], kbuf=lambda kt: B("KT", kt), vbuf=lambda kt: B("VBt", kt),
                      scale=64 ** -0.5,
                      ot_dst=lambda pj, hh, qc: OT[hh * 64:(hh + 1) * 64, 4 + pj, qc * 512:(qc + 1) * 512],
                      ot_buf=lambda pj, hh, qc: B("OT", 4 + pj, qc, hh), masked=True, kt_list_fn=kts_B, part_of=lambda pj, hh: hh)

                  stage(6)
                  for j in range(2):
                      dma_cast(wq_up_s[:], DT["wq_up"][l, :, :, :], [B("wq_up")], "wq_up")
                      dma_cast(wkv_up_s[:], DT["wkv_up"][l, :, :], [B("wkv_up")], "wkv_up")
                      for c in range(2):
                          S.op("pool", lambda e, c=c: e.tensor_scalar(out=wq_up_s[:, c, :], in0=wq_up_s[:, c, :], scalar1=qng[:, l * 2 + c:l * 2 + c + 1], scalar2=None, op0=ALU.mult),
                               reads=[B("wq_up"), B("qng")], writes=[B("wq_up")])
                      S.op("pool", lambda e: e.tensor_scalar(out=wkv_up_s[:], in0=wkv_up_s[:], scalar1=kvng[:, l:l + 1], scalar2=None, op0=ALU.mult),
                           reads=[B("wkv_up"), B("kvng")], writes=[B("wkv_up")])
                      S.alias([B("VA", t) for t in range(NT)] + [B("Vones")], [B("VBt", t) for t in range(NT)] + [B("VA", t) for t in range(NT)] + [B("Vones")])
                      for hh in range(2):
                          S.op("pool", lambda e, hh=hh: e.memset(VC[:, :, hh, (1 - hh) * 64:(2 - hh) * 64], 1.0), writes=[B("Vones")])

                      def tileC(t, par, base):
                          pc = pbank(base)
                          colq, colk = stat_col(), stat_col()
                          S.op("act", lambda e: e.activation(out=junk[:, 0:256], in_=pc[:, 0:256], func=AF.Square, accum_out=ssq[:, colq:colq + 1]),
                               reads=[PB(base)], writes=[B("junk"), B("ssq", colq)])
                          S.op("act", lambda e: e.activation(out=junk[:, 256:384], in_=pc[:, 256:384], func=AF.Square, accum_out=ssq[:, colk:colk + 1]),
                               reads=[PB(base)], writes=[B("junk"), B("ssq", colk)])
                          rstd_to(rstd[:, colq:colq + 1], B("rstd", colq), ssq[:, colq:colq + 1], [B("ssq", colq)], 256)
                          rstd_to(rstd[:, colk:colk + 1], B("rstd", colk), ssq[:, colk:colk + 1], [B("ssq", colk)], 128)
                          sc = stC[par]
                          S.op("dve", lambda e: e.tensor_scalar(out=sc[:, 0:256], in0=pc[:, 0:256], scalar1=rstd[:, colq:colq + 1], scalar2=None, op0=ALU.mult),
                               reads=[PB(base), B("rstd", colq)], writes=[B("stC", par)])
                          S.op("dve", lambda e: e.tensor_scalar(out=sc[:, 256:384], in0=pc[:, 256:384], scalar1=rstd[:, colk:colk + 1], scalar2=None, op0=ALU.mult),
                               reads=[PB(base), B("rstd", colk), B("stC", par)], writes=[B("stC", par)])
                          skr = stKR[par]
                          S.op("dve", lambda e: e.tensor_copy(out=skr[:], in_=pc[:, 384:416]), reads=[PB(base)], writes=[B("stKR", par)])
                          rope_inplace(skr[:].rearrange("p (g d) -> p g d", g=1), 1, 0, 16, cosC, sinC, t, [B("stKR", par)], par)

                      def tileC2(t, par, base):
                          sc = stC[par]
                          skr = stKR[par]
                          bank = 4 + par
                          pv = pbank_bf(bank)
                          for k3 in range(3):
                              S.op("pe", lambda e, k3=k3: e.transpose(pv[:, k3 * 128:(k3 + 1) * 128], sc[:, k3 * 128:(k3 + 1) * 128], ident[:, :]),
                                   reads=[B("stC", par), B("ident")], writes=[PB(bank)])
                          ct = cT[par]
                          S.op("act", lambda e: e.activation(out=ct[:].rearrange("p a b -> p (a b)"), in_=pv[:, 0:384], func=AF.Copy), reads=[PB(bank)], writes=[B("cT", par)])
                          bu = base + 1
                          for c in range(2):
                              S.op("pe", lambda e, c=c: e.matmul(pbank(bu)[:, 0:192], ct[:, c, :], wq_up_s[:, c, j * 192:(j + 1) * 192], start=(c == 0), stop=(c == 1)),
                                   reads=[B("cT", par), B("wq_up")], writes=[PB(bu)])
                          S.op("pe", lambda e: e.matmul(pbank(bu)[:, 256:512], ct[:, 2, :], wkv_up_s[:, j * 256:(j + 1) * 256], start=True, stop=True),
                               reads=[B("cT", par), B("wkv_up")], writes=[PB(bu)])
                          sq = stQC[par]
                          S.op("act", lambda e: e.activation(out=sq[:], in_=pbank(bu)[:, 0:192], func=AF.Copy), reads=[PB(bu)], writes=[B("stQC", par)])
                          rope_inplace(sq[:].rearrange("p (g d) -> p g d", d=96), 2, 64, 16, cosC, sinC, t, [B("stQC", par)], par)
                          sk = stKC[par]
                          kvv = pbank(bu)[:, 256:512].rearrange("p (h n) -> p h n", h=2)
                          S.op("dve", lambda e: e.tensor_copy(out=sk[:, :, 0:64], in_=kvv[:, :, 0:64]), reads=[PB(bu)], writes=[B("stKC", par)])
                          S.op("pool", lambda e: e.tensor_copy(out=sk[:, :, 64:96], in_=skr[:].unsqueeze(1).to_broadcast([128, 2, 32])),
                               reads=[B("stKR", par), B("stKC", par)], writes=[B("stKC", par)])
                          for hh in range(2):
                              S.op("dve", lambda e, hh=hh: e.tensor_copy(out=VC[:, t, hh, hh * 64:(hh + 1) * 64], in_=kvv[:, hh, 64:128]),
                                   reads=[PB(bu), B("Vones")], writes=[B("VA", t)])

                      def tileC3(t, par, base):
                          sq = stQC[par]
                          sk = stKC[par]
                          bank = 4 + par
                          pv = pbank_bf(bank)
                          for which in range(2):
                              off0 = 512 if which == 0 else 0
                              for hh in range(2):
                                  src = sq[:, hh * 96:(hh + 1) * 96] if which == 0 else sk[:, hh, :]
                                  S.op("pe", lambda e, off0=off0, hh=hh, src=src: e.transpose(pv[0:96, off0 + hh * 128:off0 + (hh + 1) * 128], src, ident[:, :]),
                                       reads=[B("stQC", par) if which == 0 else B("stKC", par), B("ident")], writes=[PB(bank)])
                              dstT = (QCT if which == 0 else KCT)[0:96, :, t * 128:(t + 1) * 128]
                              srcv = pv[0:96, off0:off0 + 256].rearrange("p (h k) -> p h k", h=2)
                              if which == 0:
                                  S.op("act", lambda e, dstT=dstT, srcv=srcv: e.activation(out=dstT, in_=srcv, func=AF.Copy), reads=[PB(bank)], writes=[B("QT", t)])
                              else:
                                  S.op("dve", lambda e, dstT=dstT, srcv=srcv: e.tensor_copy(out=dstT, in_=srcv), reads=[PB(bank)], writes=[B("KT", t)])
                      proj_pass(l, [(2304, 2720)], False, [tileC, tileC2, tileC3])
                      attention_pairs(
                          1,
                          q_ap=lambda pj, hh, qc: QCT[0:96, hh, qc * 512:(qc + 1) * 512],
                          k_ap=lambda pj, hh, kt: KCT[0:96, hh, kt * 128:(kt + 1) * 128],
                          v_ap=lambda pj, hh, kt: VC[:, kt, hh, :],
                          qbufs=lambda qc: [B("QT", qc * 4 + i) for i in range(4)], kbuf=lambda kt: B("KT", kt), vbuf=lambda kt: B("VA", kt),
                          scale=96 ** -0.5,
                          ot_dst=lambda pj, hh, qc, j=j: OT[hh * 64:(hh + 1) * 64, 6 + j, qc * 512:(qc + 1) * 512],
                          ot_buf=lambda pj, hh, qc, j=j: B("OT", 6 + j, qc, hh), masked=False, kt_list_fn=lambda qc: list(range(NT)), part_of=lambda pj, hh: hh)

                  stage(7)
                  S.alias([B("w_out", c) for c in range(8)], [B("w_pass", c) for c in range(8)] + [B("wq_up"), B("wkv_up")])
                  for c in range(8):
                      lastw = dma_cast(w_out_s[:, c, :], DT["w_out"][l, :, c, :], [B("w_out", c)], "w_out")
                  S.join([B("w_out", c) for c in range(8)], lastw)
                  load_grow(gb, "gb", l, 1)
                  for t in range(NT):
                      yb = (t % 2) * 2
                      for c in range(8):
                          for n in range(2):
                              S.op("pe", lambda e, c=c, n=n, yb=yb, t=t: e.matmul(pbank(yb + n), OT[:, c, t * 128:(t + 1) * 128], w_out_s[:, c, n * 512:(n + 1) * 512], start=(c == 0), stop=(c == 7)),
                                   reads=[B("OT", c, t // 4, 0), B("OT", c, t // 4, 1), B("w_out", c)], writes=[PB(yb + n)])
                      post_norm_residual(yb, t)

                  stage(8)
                  mixbufs = [B("VA", t) for t in range(NT)] + [B("VBt", t) for t in range(NT)] + [B("QT", t) for t in range(NT)] + [B("KT", t) for t in range(NT)] + [B("Vones")]
                  wbufs_old = [B("w_out", c) for c in range(8)]
                  membufs = [B("memf"), B("w_mkv")]
                  S.alias(membufs, wbufs_old)
                  S.alias([B("memT"), B("w_mq"), B("w_mo"), B("KmT"), B("Vm")] + [B("QmT", t) for t in range(NT)] + [B("OmT", pj, qc, hh) for pj in range(2) for qc in range(4) for hh in range(2)], mixbufs)
                  load_grow(gpre, "gpre", l, 2)
                  load_grow(gb, "gb", l, 4)
                  dma_cast(w_mq_s[:], DT["w_mq"][l, :, :, :], [B("w_mq")], "w_mq")
                  dma_cast(w_mkv_s[:], DT["w_mkv"][l, :, :, :], [B("w_mkv")], "w_mkv")
                  dma_cast(w_mo_s[:], DT["w_mo"][l, :, :, :], [B("w_mo")], "w_mo")
                  for kt in range(2):
                      dma_sp(memf[:, kt, :], DT["mem"][b, kt, :, :], [B("memf")], ("memf", kt))
                  gmem = tmp_pn[0]
                  dma_sp(gmem, DT["grow"][l * 7 + 3:l * 7 + 4, :].partition_broadcast(128), [B("finA", 2), B("finA", 3)], "gmem")
                  for kt in range(2):
                      col = stat_col()
                      S.op("act", lambda e, kt=kt, col=col: e.activation(out=junk[:], in_=memf[:, kt, :], func=AF.Square, accum_out=ssq[:, col:col + 1]),
                           reads=[B("memf")], writes=[B("junk"), B("ssq", col)])
                      rstd_to(rstd[:, col:col + 1], B("rstd", col), ssq[:, col:col + 1], [B("ssq", col)], D)
                      hb = hn[kt]
                      S.op("dve", lambda e, kt=kt, col=col, hb=hb: e.scalar_tensor_tensor(out=hb[:], in0=memf[:, kt, :], scalar=rstd[:, col:col + 1], in1=gmem, op0=ALU.mult, op1=ALU.mult),
                           reads=[B("memf"), B("rstd", col), B("finA", 2), B("finA", 3)], writes=[B("hn", kt)])
                      pv = pbank_bf(7)
                      for c in range(8):
                          S.op("pe", lambda e, c=c, hb=hb: e.transpose(pv[:, c * 128:(c + 1) * 128], hb[:, c * 128:(c + 1) * 128], ident[:, :]),
                               reads=[B("hn", kt), B("ident")], writes=[PB(7)])
                      S.op("act", lambda e, kt=kt: e.activation(out=memT[:, :, kt * 128:(kt + 1) * 128], in_=pv[:, :].rearrange("p (c k) -> p c k", c=8), func=AF.Copy),
                           reads=[PB(7)], writes=[B("memT")])
                  for jj in range(2):
                      for c in range(8):
                          S.op("pe", lambda e, jj=jj, c=c: e.matmul(pbank(jj)[:, 0:256], w_mkv_s[:, c, jj * 128:(jj + 1) * 128], memT[:, c, :], start=(c == 0), stop=(c == 7)),
                               reads=[B("w_mkv"), B("memT")], writes=[PB(jj)])
                      S.op("act", lambda e, jj=jj: e.activation(out=KmT[:, jj, :], in_=pbank(jj)[:, 0:256], func=AF.Copy), reads=[PB(jj)], writes=[B("KmT")])
                  for h in range(4):
                      i_ = h % 2
                      S.op("pool", lambda e, h=h, i_=i_: e.memset(Vm[:, :, h, (1 - i_) * 64:(2 - i_) * 64], 1.0), reads=[B("Vm")], writes=[B("Vm")])
                  for kt in range(2):
                      for c in range(8):
                          S.op("pe", lambda e, kt=kt, c=c: e.matmul(pbank(2 + kt)[:, 0:256], memT[:, c, kt * 128:(kt + 1) * 128], w_mkv_s[:, c, 256:512], start=(c == 0), stop=(c == 7)),
                               reads=[B("w_mkv"), B("memT")], writes=[PB(2 + kt)])
                      for h in range(4):
                          i_ = h % 2
                          S.op("dve", lambda e, kt=kt, h=h, i_=i_: e.tensor_copy(out=Vm[:, kt, h, i_ * 64:(i_ + 1) * 64], in_=pbank(2 + kt)[:, h * 64:(h + 1) * 64]),
                               reads=[PB(2 + kt), B("Vm")], writes=[B("Vm")])
                  def memq_mm(t):
                      par = t % 2
                      for c in range(8):
                          S.op("pe", lambda e, c=c: e.matmul(pbank(par)[:, 0:256], hTt[par][:, c, :], w_mq_s[:, c, :], start=(c == 0), stop=(c == 7)),
                               reads=[B("hTt", par), B("w_mq")], writes=[PB(par)])
                      st = stA[par]
                      S.op("act", lambda e: e.activation(out=st[:, 0:256], in_=pbank(par)[:, 0:256], func=AF.Copy), reads=[PB(par)], writes=[B("stA", par)])

                  def memq_tr(t):
                      par = t % 2
                      st = stA[par]
                      bank = 4 + par
                      pv = pbank_bf(bank)
                      for k2 in range(2):
                          S.op("pe", lambda e, k2=k2: e.transpose(pv[:, k2 * 128:(k2 + 1) * 128], st[:, k2 * 128:(k2 + 1) * 128], ident[:, :]),
                               reads=[B("stA", par), B("ident")], writes=[PB(bank)])
                      S.op("dve", lambda e: e.tensor_copy(out=QmT[:, :, t * 128:(t + 1) * 128], in_=pv[:, 0:256].rearrange("p (h k) -> p h k", h=2)),
                           reads=[PB(bank)], writes=[B("QmT", t)])

                  norm_tile(0, True, hTt[0][:, :, :], B("hTt", 0), 0)
                  for i in range(NT + 1):
                      if i + 1 < NT:
                          p1 = (i + 1) % 2
                          norm_tile(i + 1, True, hTt[p1][:, :, :], B("hTt", p1), p1)
                      if i < NT:
                          memq_mm(i)
                      if i - 1 >= 0:
                          memq_tr(i - 1)
                  attention_pairs(
                      2,
                      q_ap=lambda pj, hh, qc: QmT[hh * 64:(hh + 1) * 64, pj, qc * 512:(qc + 1) * 512],
                      k_ap=lambda pj, hh, kt: KmT[hh * 64:(hh + 1) * 64, pj, kt * 128:(kt + 1) * 128],
                      v_ap=lambda pj, hh, kt: Vm[:, kt, 2 * pj + hh, :],
                      qbufs=lambda qc: [B("QmT", qc * 4 + i) for i in range(4)], kbuf=lambda kt: B("KmT"), vbuf=lambda kt: B("Vm"),
                      scale=64 ** -0.5,
                      ot_dst=lambda pj, hh, qc: OmT[hh * 64:(hh + 1) * 64, pj, qc * 512:(qc + 1) * 512],
                      ot_buf=lambda pj, hh, qc: B("OmT", pj, qc, hh), masked=False, kt_list_fn=lambda qc: [0, 1], part_of=lambda pj, hh: hh)
                  for t in range(NT):
                      yb = (t % 2) * 2 + 4
                      for c in range(2):
                          for n in range(2):
                              S.op("pe", lambda e, c=c, n=n, yb=yb, t=t: e.matmul(pbank(yb + n), OmT[:, c, t * 128:(t + 1) * 128], w_mo_s[:, c, n * 512:(n + 1) * 512], start=(c == 0), stop=(c == 1)),
                                   reads=[B("OmT", c, t // 4, 0), B("OmT", c, t // 4, 1), B("w_mo")], writes=[PB(yb + n)])
                      post_norm_residual(yb, t)

                  stage(9)
                  memall = [B("memT"), B("w_mq"), B("w_mo"), B("KmT"), B("Vm"), B("memf"), B("w_mkv")] + [B("QmT", t) for t in range(NT)] + [B("OmT", pj, qc, hh) for pj in range(2) for qc in range(4) for hh in range(2)]
                  S.alias([B("w_d", f) for f in range(NFC)] + [B("wgu", i) for i in range(NSLOT)], memall + [B("Pacc", 0, 0), B("Pacc", 0, 1), B("Pacc", 1, 0), B("Pacc", 1, 1)])
                  S.alias([B("hTq"), B("HT")] , [B("OT", c, qc, i) for c in range(8) for qc in range(4) for i in range(2)])
                  load_grow(gpre, "gpre", l, 5)
                  load_grow(gb, "gb", l, 6)
                  nload = [0]

                  def load_slot(f):
                      s_ = f % NSLOT
                      dma_cast(wgu[s_][:, 0, :, :], DT["w_g"][l, f, :, :, :], [B("wgu", s_)], ("wgu", s_))
                      dma_cast(wgu[s_][:, 1, :, :], DT["w_u"][l, f, :, :, :], [B("wgu", s_)], ("wgu", s_))
                  for q4 in range(4):
                      for ti in range(4):
                          t = q4 * 4 + ti
                          norm_tile(t, True, hTq[:, :, ti * 128:(ti + 1) * 128], B("hTq"), ti % 2)
                      if SUB == 7:
                          continue
                      for f in range(min(NSLOT - 1, NFC)):
                          load_slot(f)
                      for f in range(NFC):
                          if f + NSLOT - 1 < NFC:
                              load_slot(f + NSLOT - 1)
                          if q4 == 0:
                              lastw = dma_cast(w_d_s[:, f, :], DT["w_d"][l, :, f, :], [B("w_d", f)], "w_d")
                              if f == NFC - 1:
                                  S.join([B("w_d", ff) for ff in range(NFC)], lastw)
                          s_ = f % NSLOT
                          gbk, ubk = (f % 2) * 2, (f % 2) * 2 + 1
                          for c in range(8):
                              S.op("pe", lambda e, c=c, s_=s_, gbk=gbk: e.matmul(pbank(gbk), wgu[s_][:, 0, c, :], hTq[:, c, :], start=(c == 0), stop=(c == 7)),
                                   reads=[B("wgu", s_), B("hTq")], writes=[PB(gbk)])
                          for c in range(8):
                              S.op("pe", lambda e, c=c, s_=s_, ubk=ubk: e.matmul(pbank(ubk), wgu[s_][:, 1, c, :], hTq[:, c, :], start=(c == 0), stop=(c == 7)),
                                   reads=[B("wgu", s_), B("hTq")], writes=[PB(ubk)])
                          sgt = sg[f % 2]
                          S.op("act", lambda e, sgt=sgt, gbk=gbk: e.activation(out=sgt[:], in_=pbank(gbk), func=AF.Silu), reads=[PB(gbk)], writes=[B("sg", f % 2)])
                          S.op("dve", lambda e, sgt=sgt, ubk=ubk, f=f: e.tensor_tensor(out=HT[:, f, :], in0=pbank(ubk), in1=sgt[:], op=ALU.mult),
                               reads=[PB(ubk), B("sg", f % 2)], writes=[B("HT")])
                      if SUB == 8:
                          continue
                      for ti in range(4):
                          t = q4 * 4 + ti
                          yb = 4 if ti % 2 == 0 else 2
                          for f in range(NFC):
                              for n in range(2):
                                  S.op("pe", lambda e, f=f, n=n, yb=yb, ti=ti: e.matmul(pbank(yb + n), HT[:, f, ti * 128:(ti + 1) * 128], w_d_s[:, f, n * 512:(n + 1) * 512], start=(f == 0), stop=(f == NFC - 1)),
                                       reads=[B("HT"), B("w_d", f)], writes=[PB(yb + n)])
                          if SUB == 9:
                              continue
                          post_norm_residual(yb, t)
                      if SUB == 10:
                          break
                  nxt = [B("OT", c, qc, i) for c in range(8) for qc in range(4) for i in range(2)] + mixbufs + [B("w_pass", c) for c in range(8)] + [B("wq_up"), B("wkv_up"), B("Pacc", 0, 0), B("Pacc", 0, 1), B("Pacc", 1, 0), B("Pacc", 1, 1)]
                  S.alias(nxt, [B("hTq"), B("HT")] + [B("w_d", f) for f in range(NFC)] + [B("wgu", i) for i in range(NSLOT)])

              except _Stop:
                pass

            outbufs = []
            for t0_, t1_ in ((0, 12), (12, NT)):
                for t in range(t0_, t1_):
                    dma_sp(DT["out"][b, t, :, :], xs[:, t, :], [B("hbm_out", b, t)], ("out", t), reads=[B("x", t)])
                    outbufs.append(B("hbm_out", b, t))
                if b + 1 < NB:
                    for t in range(t0_, t1_):
                        dma_sp(xs[:, t, :], DT["x"][b + 1, t, :, :], [B("x", t)], ("x", t))
            if b == NB - 1:
                S.op("sp", lambda e: None, reads=outbufs)

        S.finalize()
        sems = {e_: es.enter_context(nc.semaphore("s_" + e_)) for e_ in S.ENGS}
        dsems = {k: es.enter_context(nc.semaphore("d_%d" % i)) for i, k in enumerate(S.dma_groups)}
        block = es.enter_context(nc.Block())
        block.sync(lambda e: S.emit_stream("sp", e, sems, dsems))
        block.tensor(lambda e: S.emit_stream("pe", e, sems, dsems))
        block.scalar(lambda e: S.emit_stream("act", e, sems, dsems))
        block.vector(lambda e: S.emit_stream("dve", e, sems, dsems))
        block.gpsimd(lambda e: S.emit_stream("pool", e, sems, dsems))
        build_program.stats = {e_: len(S.streams[e_]) for e_ in S.ENGS}
        build_program.nsem = len(dsems) + 5
    return nc


def _mask_table():
    m = np.zeros((128, MASK_W), np.float32)
    p = np.arange(128)[:, None]
    c = np.arange(MASK_W)[None, :]
    diff = c - p - MASK_C0 + 0
    a = np.abs(diff)
    m += (a <= 64)
    m += ((a % 4 == 0) & (a <= 256))
    m += ((a % 16 == 0) & (a <= 1024))
    return m.astype(np.float32)


def prep_shared(inp):
    f = np.float32
    L = DEPTH
    ng = np.asarray(inp["norm_gains"], f)
    sh = {}
    sh["ident"] = np.eye(128, dtype=f)
    sh["maskb"] = _mask_table()
    inv8 = (500000.0 ** (-np.arange(0, 16, 2, dtype=np.float32) / np.float32(16))).astype(f)
    inv16 = (500000.0 ** (-np.arange(0, 32, 2, dtype=np.float32) / np.float32(32))).astype(f)
    sh["invf"] = np.ascontiguousarray(np.tile(np.concatenate([inv8, inv16])[None, :], (128, 1)).astype(f))
    sh["gT"] = np.ascontiguousarray(ng.reshape(L, 7, 8, 128).transpose(3, 0, 1, 2).reshape(128, L * 7 * 8))
    sh["grow"] = np.ascontiguousarray(ng.reshape(L * 7, D))
    sh["w_in"] = np.ascontiguousarray(np.asarray(inp["w_in"], f).reshape(L, 8, 128, D_IN).transpose(0, 2, 1, 3))
    sh["w_out"] = np.ascontiguousarray(np.asarray(inp["w_out"], f).reshape(L, 8, 128, D).transpose(0, 2, 1, 3))
    sh["lam"] = np.ascontiguousarray(np.asarray(inp["diff_lambda"], f).reshape(L, 256))
    sh["subln"] = np.ascontiguousarray(np.asarray(inp["diff_subln"], f).T)
    sh["qng"] = np.ascontiguousarray(np.asarray(inp["mla_q_norm"], f).reshape(L, 2, 128).transpose(2, 0, 1).reshape(128, L * 2))
    sh["kvng"] = np.ascontiguousarray(np.asarray(inp["mla_kv_norm"], f).T)
    sh["wq_up"] = np.ascontiguousarray(np.asarray(inp["w_mla_q_up"], f).reshape(L, 2, 128, 384).transpose(0, 2, 1, 3))
    sh["wkv_up"] = np.ascontiguousarray(np.asarray(inp["w_mla_kv_up"], f))
    sh["w_mq"] = np.ascontiguousarray(np.asarray(inp["w_mem_q"], f).reshape(L, 8, 128, 256).transpose(0, 2, 1, 3))
    sh["w_mkv"] = np.ascontiguousarray(np.asarray(inp["w_mem_kv"], f).reshape(L, 8, 128, 512).transpose(0, 2, 1, 3))
    sh["w_mo"] = np.ascontiguousarray(np.asarray(inp["w_mem_o"], f).reshape(L, 2, 128, D).transpose(0, 2, 1, 3))
    sh["w_g"] = np.ascontiguousarray(np.asarray(inp["w_ffn_gate"], f).reshape(L, 8, 128, NFC, 128).transpose(0, 3, 2, 1, 4))
    sh["w_u"] = np.ascontiguousarray(np.asarray(inp["w_ffn_up"], f).reshape(L, 8, 128, NFC, 128).transpose(0, 3, 2, 1, 4))
    sh["w_d"] = np.ascontiguousarray(np.asarray(inp["w_ffn_down"], f).reshape(L, NFC, 128, D).transpose(0, 2, 1, 3))
    return sh


def prep_core(inp, b0, nb):
    x = np.asarray(inp["x"], np.float32)[b0:b0 + nb]
    mem = np.asarray(inp["mem"], np.float32)[b0:b0 + nb]
    pos = np.asarray(inp["positions"], np.int32)[b0:b0 + nb]
    return {
        "x": np.ascontiguousarray(x.reshape(nb, NT, 128, D)),
        "mem": np.ascontiguousarray(mem.reshape(nb, 2, 128, D)),
        "pos": np.ascontiguousarray(pos.reshape(nb, NT, 128).transpose(0, 2, 1)),
    }


_PROG = {}


def kernel(**inputs):
    nb = 16 // N_CORES
    key = (nb, (0, 1))
    if key not in _PROG:
        _PROG[key] = build_program(NB=nb, layers=(0, 1))
    nc = _PROG[key]
    sh = prep_shared(inputs)
    in_maps = []
    for c in range(N_CORES):
        m = dict(sh)
        m.update(prep_core(inputs, c * nb, nb))
        in_maps.append(m)
    res = run_bass_kernel_spmd(nc, in_maps, core_ids=list(range(N_CORES)))
    outs = [np.asarray(r["out"], np.float32).reshape(nb, S_LEN, D) for r in res.results]
    return np.concatenate(outs, axis=0)
```

```python
import math
from contextlib import ExitStack

import numpy as np
import concourse.bass as bass
import concourse.mybir as mybir
from concourse.bass_utils import run_bass_kernel_spmd

F32 = mybir.dt.float32
BF16 = mybir.dt.bfloat16
I32 = mybir.dt.int32
AF = mybir.ActivationFunctionType
ALU = mybir.AluOpType

N_CORES = 8
D = 1024
S_LEN = 2048
NT = 16
DEPTH = 2
D_IN = 2720
D_FF = 2816
NFC = 22
MEM_LEN = 256
EPS = 1e-6
MASK_W = 2944
MASK_C0 = 1408


class Buf:
    __slots__ = ("name", "w", "r")

    def __init__(self, name):
        self.name = name
        self.w = None
        self.r = []


class Op:
    __slots__ = ("eng", "idx", "fn", "deps", "waits", "signal", "semval", "dma", "dsem")

    def __init__(self, eng, fn):
        self.eng = eng
        self.fn = fn
        self.deps = []
        self.waits = []
        self.signal = False
        self.semval = 0
        self.dma = False
        self.dsem = None


class _Rec:
    def __init__(self):
        self.calls = []

    def __getattr__(self, name):
        def f(*a, **k):
            self.calls.append((name, a, k))
            return self
        return f


class Sched:
    ENGS = ("pe", "act", "dve", "pool", "sp")

    def __init__(self):
        self.ops = []
        self.streams = {e: [] for e in self.ENGS}
        self.bufs = {}
        self.dma_groups = {}

    def B(self, *key):
        b = self.bufs.get(key)
        if b is None:
            b = self.bufs[key] = Buf(key)
        return b

    def op(self, eng, fn, reads=(), writes=(), dma_key=None):
        rec = _Rec()
        fn(rec)
        o = Op(eng, rec.calls)
        deps = {}
        rawset = set()
        writes = list(writes) + [b for b in reads if b.name[0] == "ps" and b not in writes]
        for b in reads:
            if b.w is not None:
                deps[id(b.w)] = b.w
                rawset.add(id(b.w))
        for b in writes:
            for r in b.r:
                deps[id(r)] = r
            if b.w is not None:
                deps[id(b.w)] = b.w
        for d in deps.values():
            if d is o:
                continue
            if d.dma:
                if dma_key is not None and d.dsem == dma_key:
                    continue
                o.deps.append(d)
            else:
                if dma_key is None and d.eng == eng and eng == "pe":
                    continue
                o.deps.append(d)
        for b in reads:
            b.r.append(o)
        for b in writes:
            b.w = o
            b.r = []
        if dma_key is not None:
            o.dma = True
            o.dsem = dma_key
            self.dma_groups.setdefault(dma_key, []).append(o)
        o.idx = len(self.streams[eng])
        self.streams[eng].append(o)
        self.ops.append(o)
        return o

    def join(self, bufs, last_op):
        for b in bufs:
            if b.w is not None and b.w.dma and b.w.dsem == last_op.dsem:
                b.w = last_op

    def alias(self, new_bufs, old_bufs):
        latest = {}
        dmas = []
        for b in old_bufs:
            for o in list(b.r) + ([b.w] if b.w is not None else []):
                if o.dma:
                    dmas.append(o)
                else:
                    cur = latest.get(o.eng)
                    if cur is None or cur.idx < o.idx:
                        latest[o.eng] = o
        ops = list(latest.values()) + dmas
        for b in new_bufs:
            b.r = list(b.r) + ops

    def finalize(self):
        for k, lst in self.dma_groups.items():
            for i, o in enumerate(lst):
                o.semval = 16 * (i + 1)
        known = {e: {} for e in self.ENGS}
        for o in self.ops:
            kn = known[o.eng]
            need = {}
            for d in o.deps:
                src = ("dma", d.dsem) if d.dma else d.eng
                pos = d.semval if d.dma else d.idx
                if kn.get(src, -1) >= pos:
                    continue
                if src not in need or need[src][0] < pos:
                    need[src] = (pos, d)
            for src, (pos, d) in need.items():
                kn[src] = pos
                if not d.dma:
                    d.signal = True
                o.waits.append(d)
        for e in self.ENGS:
            c = 0
            for o in self.streams[e]:
                if o.dma:
                    continue
                if o.signal:
                    c += 1
                    o.semval = c

    def emit_stream(self, engname, engobj, sems, dma_sems):
        for o in self.streams[engname]:
            for d in o.waits:
                if d.dma:
                    engobj.wait_ge(dma_sems[d.dsem], d.semval)
                else:
                    engobj.wait_ge(sems[d.eng], d.semval)
            ins = None
            for (name, a_, k_) in o.fn:
                ins = getattr(engobj, name)(*a_, **k_)
            if ins is None:
                continue
            if o.dma:
                ins.then_inc(dma_sems[o.dsem], 16)
            elif o.signal:
                ins.then_inc(sems[o.eng], 1)


SUB = 99
N_WARM = 0


class _Stop(Exception):
    pass


def build_program(NB=2, layers=(0, 1), taps=(), upto=99, wdepth=DEPTH):
    nc = bass.Bass("TRN2", target_bir_lowering=False)
    DT = {}

    def din(name, shape, dt=F32):
        DT[name] = nc.dram_tensor(name, list(shape), dt, kind="ExternalInput").ap()

    def dout(name, shape, dt=F32):
        DT[name] = nc.dram_tensor(name, list(shape), dt, kind="ExternalOutput").ap()

    din("x", [NB, NT, 128, D])
    din("mem", [NB, 2, 128, D])
    din("pos", [NB, 128, NT], I32)
    din("ident", [128, 128])
    din("maskb", [128, MASK_W])
    din("invf", [128, 24])
    din("gT", [128, DEPTH * 7 * 8])
    din("grow", [DEPTH * 7, D])
    din("w_in", [wdepth, 128, 8, D_IN])
    din("w_out", [wdepth, 128, 8, D])
    din("lam", [DEPTH, 256])
    din("subln", [128, DEPTH])
    din("qng", [128, DEPTH * 2])
    din("kvng", [128, DEPTH])
    din("wq_up", [wdepth, 128, 2, 384])
    din("wkv_up", [wdepth, 128, 512])
    din("w_mq", [wdepth, 128, 8, 256])
    din("w_mkv", [wdepth, 128, 8, 512])
    din("w_mo", [wdepth, 128, 2, D])
    din("w_g", [wdepth, NFC, 128, 8, 128])
    din("w_u", [wdepth, NFC, 128, 8, 128])
    din("w_d", [wdepth, 128, NFC, D])
    dout("out", [NB, NT, 128, D])
    for name, shape, dt in taps:
        dout(name, shape, dt)

    S = Sched()
    B = S.B
    es = ExitStack()
    with es:
        def sb(name, shape, dt):
            return es.enter_context(nc.sbuf_tensor(name, list(shape), dt))


        xs = sb("xs", [128, NT, D], F32)
        ident = sb("ident_s", [128, 128], BF16)
        ones = sb("ones_s", [128, 128], BF16)
        maskb = sb("maskb_s", [128, MASK_W], BF16)
        invf = sb("invf_s", [128, 24], F32)
        epsT = sb("eps_s", [128, 1], F32)
        subln = sb("subln_s", [128, DEPTH], F32)
        qng = sb("qng_s", [128, DEPTH * 2], F32)
        kvng = sb("kvng_s", [128, DEPTH], F32)
        sublnS = sb("sublnS", [128, DEPTH], F32)
        lamraw = sb("lamraw", [128, 256], F32)
        lamt = sb("lamt", [128, 8], F32)
        neglam = sb("neglam", [128, DEPTH], F32)
        posi = sb("posi", [128, NT], I32)
        posf = sb("posf", [128, NT], F32)
        angP = sb("angP", [128, NT, 8], F32)
        angC = sb("angC", [128, NT, 16], F32)
        cosP = sb("cosP", [128, NT, 8], F32)
        sinP = sb("sinP", [128, NT, 8], F32)
        cosC = sb("cosC", [128, NT, 16], F32)
        sinC = sb("sinC", [128, NT, 16], F32)
        gpre = sb("gpre", [128, D], F32)
        gb = sb("gb", [128, D], F32)
        junk = sb("junk", [128, D], BF16)
        hn = [sb("hn%d" % i, [128, D], BF16) for i in range(2)]
        hTt = [sb("hTt%d" % i, [128, 8, 128], BF16) for i in range(2)]
        ssq = sb("ssq", [128, 64], F32)
        rstd = sb("rstd", [128, 64], F32)
        rcache = sb("rcache", [128, NT], F32)
        PT = [sb("PT%d" % i, [128, 1024], BF16) for i in range(3)]
        stA = [sb("stA%d" % i, [128, 512], BF16) for i in range(2)]
        ropeT = [sb("ropeT%d" % i, [128, 128], BF16) for i in range(2)]
        ropeU = [sb("ropeU%d" % i, [128, 128], BF16) for i in range(2)]
        stC = [sb("stC%d" % i, [128, 384], BF16) for i in range(2)]
        stKR = [sb("stKR%d" % i, [128, 32], BF16) for i in range(4)]
        stQC = [sb("stQC%d" % i, [128, 192], BF16) for i in range(2)]
        stKC = [sb("stKC%d" % i, [128, 2, 96], BF16) for i in range(2)]
        cT = [sb("cT%d" % i, [128, 3, 128], BF16) for i in range(2)]
        finAall = sb("finAall", [128, 4, 512], F32)
        finA = [finAall[:, i, :] for i in range(4)]
        finB = sb("finB", [128, 512], BF16)
        sg = [sb("sg%d" % i, [128, 512], BF16) for i in range(2)]
        ARENA = 45056
        arena = sb("arena", [128, ARENA], BF16)
        psum = es.enter_context(nc.psum_tensor("psum", [128, 4096], F32))

        def pbank(b0, nb=1):
            return psum[:, b0 * 512:(b0 + nb) * 512]

        def pbank_bf(b0):
            return psum[:, b0 * 512:(b0 + 1) * 512].bitcast(BF16)

        PB = lambda i: B("ps", i)

        O0, Q0, W0 = 0, 16384, 32768
        OT = arena[:, O0:O0 + 16384].rearrange("p (c t) -> p c t", c=8)
        w_pass = arena[:, W0:W0 + 6144].rearrange("p (c n) -> p c n", c=8)
        wq_up_s = arena[:, W0 + 6144:W0 + 6144 + 768].rearrange("p (c n) -> p c n", c=2)
        wkv_up_s = arena[:, W0 + 6912:W0 + 6912 + 512]
        w_out_s = arena[:, W0:W0 + 8192].rearrange("p (c n) -> p c n", c=8)
        QAT = arena[:, Q0:Q0 + 4096].rearrange("p (h t) -> p h t", h=2)
        KAT = arena[:, Q0 + 4096:Q0 + 8192].rearrange("p (h t) -> p h t", h=2)
        VA = arena[:, Q0 + 8192:Q0 + 12288].rearrange("p (t n) -> p t n", t=NT)
        QBT = arena[:, Q0:Q0 + 4096].rearrange("p (j t) -> p j t", j=2)
        KBT = arena[:, Q0 + 4096:Q0 + 8192].rearrange("p (j t) -> p j t", j=2)
        VB = arena[:, Q0 + 8192:Q0 + 16384].rearrange("p (t h n) -> p t h n", t=NT, h=4)
        QCT = arena[:, Q0:Q0 + 4096].rearrange("p (h t) -> p h t", h=2)
        KCT = arena[:, Q0 + 4096:Q0 + 8192].rearrange("p (h t) -> p h t", h=2)
        VC = arena[:, Q0 + 8192:Q0 + 12288].rearrange("p (t h n) -> p t h n", t=NT, h=2)
        Pacc = arena[:, W0 + 8192:W0 + 12288].bitcast(F32).rearrange("p (a n) -> p a n", a=2)
        QmT = arena[:, Q0:Q0 + 4096].rearrange("p (j t) -> p j t", j=2)
        OmT = arena[:, Q0 + 4096:Q0 + 8192].rearrange("p (j t) -> p j t", j=2)
        memT = arena[:, Q0 + 8192:Q0 + 10240].rearrange("p (c t) -> p c t", c=8)
        KmT = arena[:, Q0 + 10240:Q0 + 10752].rearrange("p (j t) -> p j t", j=2)
        Vm = arena[:, Q0 + 10752:Q0 + 11776].rearrange("p (t h n) -> p t h n", t=2, h=4)
        w_mq_s = arena[:, Q0 + 11776:Q0 + 13824].rearrange("p (c n) -> p c n", c=8)
        w_mo_s = arena[:, Q0 + 13824:Q0 + 15872].rearrange("p (c n) -> p c n", c=2)
        w_mkv_s = arena[:, W0:W0 + 4096].rearrange("p (c n) -> p c n", c=8)
        memf = arena[:, W0 + 4096:W0 + 8192].bitcast(F32).rearrange("p (t n) -> p t n", t=2)
        hTq = arena[:, O0:O0 + 4096].rearrange("p (c t) -> p c t", c=8)
        HT = arena[:, O0 + 4096:O0 + 4096 + 11264].rearrange("p (f t) -> p f t", f=NFC)
        w_d_s = arena[:, Q0:Q0 + 22528].rearrange("p (f n) -> p f n", f=NFC)
        NSLOT = 3
        SL0 = Q0 + 22528
        wgu = [arena[:, SL0 + i * 2048:SL0 + (i + 1) * 2048].rearrange("p (g c n) -> p g c n", g=2, c=8) for i in range(NSLOT)]
        assert SL0 + NSLOT * 2048 <= ARENA

        cnt = {"ss": 0}

        def stat_col():
            cnt["ss"] = (cnt["ss"] + 1) % 64
            return cnt["ss"]

        def dma_cast(dst_ap, src_ap, wbufs, key):
            return S.op("pool", lambda e: e.dma_start(out=dst_ap, in_=src_ap), writes=list(wbufs), dma_key=key)

        def dma_sp(dst_ap, src_ap, wbufs, key, reads=()):
            return S.op("sp", lambda e: e.dma_start(out=dst_ap, in_=src_ap), reads=list(reads), writes=list(wbufs), dma_key=key)

        def rstd_to(dst_ap, dst_buf, src_ap, src_bufs, n):
            S.op("act", lambda e: e.activation(out=dst_ap, in_=src_ap, func=AF.Ln, scale=1.0 / n, bias=epsT[:, 0:1]),
                 reads=list(src_bufs) + [B("eps")], writes=[dst_buf])
            S.op("act", lambda e: e.activation(out=dst_ap, in_=dst_ap, func=AF.Exp, scale=-0.5), reads=[dst_buf], writes=[dst_buf])

        dma_cast(ident[:], DT["ident"][:, :], [B("ident")], "c_ident")
        dma_cast(maskb[:], DT["maskb"][:, :], [B("maskb")], "c_maskb")
        dma_sp(invf[:], DT["invf"][:, :], [B("invf")], "c_invf")
        dma_sp(subln[:], DT["subln"][:, :], [B("subln")], "c_subln")
        dma_sp(qng[:], DT["qng"][:, :], [B("qng")], "c_qng")
        dma_sp(kvng[:], DT["kvng"][:, :], [B("kvng")], "c_kvng")
        S.op("dve", lambda e: e.memset(ones[:], 1.0), writes=[B("ones")])
        S.op("dve", lambda e: e.memset(epsT[:], EPS), writes=[B("eps")])
        for l in range(DEPTH):
            lam_init = 0.8 - 0.6 * math.exp(-0.3 * l)
            dma_sp(lamraw[:], DT["lam"][l:l + 1, :].partition_broadcast(128), [B("lamraw")], ("c_lam", l))
            S.op("dve", lambda e: e.tensor_tensor(out=lamraw[:, 0:64], in0=lamraw[:, 0:64], in1=lamraw[:, 64:128], op=ALU.mult), reads=[B("lamraw")], writes=[B("lamraw")])
            S.op("dve", lambda e: e.tensor_tensor(out=lamraw[:, 128:192], in0=lamraw[:, 128:192], in1=lamraw[:, 192:256], op=ALU.mult), reads=[B("lamraw")], writes=[B("lamraw")])
            S.op("dve", lambda e: e.tensor_reduce(out=lamt[:, 0:1], in_=lamraw[:, 0:64], axis=mybir.AxisListType.X, op=ALU.add), reads=[B("lamraw")], writes=[B("lamt")])
            S.op("dve", lambda e: e.tensor_reduce(out=lamt[:, 1:2], in_=lamraw[:, 128:192], axis=mybir.AxisListType.X, op=ALU.add), reads=[B("lamraw"), B("lamt")], writes=[B("lamt")])
            S.op("act", lambda e: e.activation(out=lamt[:, 2:4], in_=lamt[:, 0:2], func=AF.Exp), reads=[B("lamt")], writes=[B("lamt")])
            S.op("dve", lambda e, l=l, li=lam_init: e.scalar_tensor_tensor(out=neglam[:, l:l + 1], in0=lamt[:, 3:4], scalar=-li, in1=lamt[:, 2:3], op0=ALU.add, op1=ALU.subtract),
                 reads=[B("lamt")], writes=[B("neglam", l)])
            S.op("dve", lambda e, l=l, li=lam_init: e.tensor_scalar(out=sublnS[:, l:l + 1], in0=subln[:, l:l + 1], scalar1=1.0 - li, scalar2=None, op0=ALU.mult),
                 reads=[B("subln")], writes=[B("sublnS", l)])

        angK_all = finA[0][:, 0:256].rearrange("p (t w) -> p t w", t=NT)
        angI_all = finA[1][:, 0:256].bitcast(I32).rearrange("p (t w) -> p t w", t=NT)
        angT_all = finA[2][:, 0:256].rearrange("p (t w) -> p t w", t=NT)

        def range_reduce_sin(dst, ang, width, bname):
            BA = B(*bname) if isinstance(bname, tuple) else B(bname)
            k = angK_all[:, :, 0:width]
            ki = angI_all[:, :, 0:width]
            bk, bi = B("finA", 0), B("finA", 1)
            S.op("dve", lambda e: e.tensor_scalar(out=k, in0=ang, scalar1=1.0 / (2 * math.pi), scalar2=None, op0=ALU.mult), reads=[BA], writes=[bk])
            S.op("dve", lambda e: e.tensor_copy(out=ki, in_=k), reads=[bk], writes=[bi])
            S.op("dve", lambda e: e.tensor_copy(out=k, in_=ki), reads=[bi], writes=[bk])
            S.op("dve", lambda e: e.scalar_tensor_tensor(out=ang, in0=k, scalar=-2 * math.pi, in1=ang, op0=ALU.mult, op1=ALU.add), reads=[bk, BA], writes=[BA])
            S.op("dve", lambda e: e.tensor_scalar(out=k, in0=ang, scalar1=math.pi, scalar2=-2 * math.pi, op0=ALU.is_gt, op1=ALU.mult), reads=[BA], writes=[bk])
            S.op("dve", lambda e: e.tensor_tensor(out=ang, in0=ang, in1=k, op=ALU.add), reads=[bk, BA], writes=[BA])
            S.op("dve", lambda e: e.tensor_scalar(out=k, in0=ang, scalar1=-math.pi, scalar2=2 * math.pi, op0=ALU.is_lt, op1=ALU.mult), reads=[BA], writes=[bk])
            S.op("dve", lambda e: e.tensor_tensor(out=ang, in0=ang, in1=k, op=ALU.add), reads=[bk, BA], writes=[BA])
            S.op("act", lambda e: e.activation(out=dst, in_=ang, func=AF.Sin), reads=[BA], writes=[B("ropetab")])

        def norm_tile(t, first, dst_ap, dst_buf, par):
            xap = xs[:, t, :]
            if first:
                col = stat_col()
                S.op("act", lambda e: e.activation(out=junk[:], in_=xap, func=AF.Square, accum_out=ssq[:, col:col + 1]),
                     reads=[B("x", t)], writes=[B("junk"), B("ssq", col)])
                rstd_to(rcache[:, t:t + 1], B("rcache", t), ssq[:, col:col + 1], [B("ssq", col)], D)
            hb = hn[par]
            S.op("dve", lambda e: e.scalar_tensor_tensor(out=hb[:], in0=xap, scalar=rcache[:, t:t + 1], in1=gpre[:], op0=ALU.mult, op1=ALU.mult),
                 reads=[B("x", t), B("rcache", t), B("gpre")], writes=[B("hn", par)])
            pv = pbank_bf(7)
            for c in range(8):
                S.op("pe", lambda e, c=c: e.transpose(pv[:, c * 128:(c + 1) * 128], hb[:, c * 128:(c + 1) * 128], ident[:, :]),
                     reads=[B("hn", par), B("ident")], writes=[PB(7)])
            if t % 2 == 0:
                S.op("act", lambda e: e.activation(out=dst_ap, in_=pv[:, :].rearrange("p (c k) -> p c k", c=8), func=AF.Copy), reads=[PB(7)], writes=[dst_buf])
            else:
                S.op("dve", lambda e: e.tensor_copy(out=dst_ap, in_=pv[:, :].rearrange("p (c k) -> p c k", c=8)), reads=[PB(7)], writes=[dst_buf])

        def load_grow(dst, bufname, l, n):
            r = l * 7 + n
            dma_sp(dst[:], DT["grow"][r:r + 1, :].partition_broadcast(128), [B(bufname)], bufname)

        def post_norm_residual(ybank, t):
            yap = pbank(ybank, 2)
            col = stat_col()
            tmp = tmp_pn[t % 2]
            S.op("act", lambda e: e.activation(out=junk[:], in_=yap, func=AF.Square, accum_out=ssq[:, col:col + 1]),
                 reads=[PB(ybank), PB(ybank + 1)], writes=[B("junk"), B("ssq", col)])
            rstd_to(rstd[:, col:col + 1], B("rstd", col), ssq[:, col:col + 1], [B("ssq", col)], D)
            S.op("dve", lambda e: e.scalar_tensor_tensor(out=tmp, in0=yap, scalar=rstd[:, col:col + 1], in1=gb[:], op0=ALU.mult, op1=ALU.mult),
                 reads=[PB(ybank), PB(ybank + 1), B("rstd", col), B("gb")], writes=[B("finA", 2), B("finA", 3)])
            S.op("pool", lambda e: e.tensor_tensor(out=xs[:, t, :], in0=xs[:, t, :], in1=tmp, op=ALU.add),
                 reads=[B("x", t), B("finA", 2), B("finA", 3)], writes=[B("x", t)])

        tmp_pn = [finAall[:, 2:4, :].rearrange("p a b -> p (a b)")] * 2

        def rope_inplace(v, ngroups, roff, half, cos_t, sin_t, t, bufs, par):
            x1 = v[:, 0:ngroups, roff:roff + half]
            x2 = v[:, 0:ngroups, roff + half:roff + 2 * half]
            c = cos_t[:, t:t + 1, :].to_broadcast([128, ngroups, half])
            s = sin_t[:, t:t + 1, :].to_broadcast([128, ngroups, half])
            T = ropeT[par][:, 0:ngroups * 2 * half].rearrange("p (g d) -> p g d", g=ngroups)
            U = ropeU[par][:, 0:ngroups * 2 * half].rearrange("p (g d) -> p g d", g=ngroups)
            ta = T[:, 0:ngroups, 0:half]
            tb = T[:, 0:ngroups, half:2 * half]
            ua = U[:, 0:ngroups, 0:half]
            ub = U[:, 0:ngroups, half:2 * half]
            rb = [B("ropeT", par), B("ropeU", par)]
            S.op("pool", lambda e: e.tensor_tensor(out=ta, in0=x1, in1=c, op=ALU.mult), reads=bufs + [B("ropetab")], writes=[rb[0]])
            S.op("pool", lambda e: e.tensor_tensor(out=tb, in0=x2, in1=c, op=ALU.mult), reads=bufs + [B("ropetab"), rb[0]], writes=[rb[0]])
            S.op("pool", lambda e: e.tensor_tensor(out=ua, in0=x2, in1=s, op=ALU.mult), reads=bufs + [B("ropetab")], writes=[rb[1]])
            S.op("pool", lambda e: e.tensor_tensor(out=ub, in0=x1, in1=s, op=ALU.mult), reads=bufs + [B("ropetab"), rb[1]], writes=[rb[1]])
            S.op("pool", lambda e: e.tensor_tensor(out=x1, in0=ta, in1=ua, op=ALU.subtract), reads=rb + bufs, writes=bufs)
            S.op("pool", lambda e: e.tensor_tensor(out=x2, in0=tb, in1=ub, op=ALU.add), reads=rb + bufs, writes=bufs)

        def attention_pairs(npairs, q_ap, k_ap, v_ap, qbufs, kbuf, vbuf, scale, ot_dst, ot_buf, masked, kt_list_fn, part_of):
            acc = (0, 1)
            sbanks = [(2, 3), (4, 5), (6, 7)]
            blocks = [(pj, qc) for pj in range(npairs) for qc in range(4)]

            def emit_scores(pj, qc, i, kt):
                sb_ = sbanks[i % 3]
                for hh in range(2):
                    S.op("pe", lambda e, hh=hh: e.matmul(pbank(sb_[hh]), k_ap(pj, hh, kt), q_ap(pj, hh, qc), start=True, stop=True),
                         reads=[kbuf(kt)] + qbufs(qc), writes=[PB(sb_[hh])])

            def emit_exp(qc, i, kt):
                sb_ = sbanks[i % 3]
                pt = PT[i % 3]
                S.op("act", lambda e: e.activation(out=pt[:], in_=pbank(sb_[0], 2), func=AF.Exp, scale=scale),
                     reads=[PB(sb_[0]), PB(sb_[1])], writes=[B("PT", i % 3)])
                if masked:
                    c0 = qc * 512 - kt * 128 + MASK_C0
                    m = maskb[:, c0:c0 + 512].unsqueeze(1).to_broadcast([128, 2, 512])
                    ptv = pt[:].rearrange("p (a q) -> p a q", a=2)
                    S.op("dve", lambda e: e.tensor_tensor(out=ptv, in0=ptv, in1=m, op=ALU.mult),
                         reads=[B("PT", i % 3), B("maskb")], writes=[B("PT", i % 3)])

            def emit_pv(pj, i, kt, n):
                pt = PT[i % 3]
                for hh in range(2):
                    S.op("pe", lambda e, hh=hh: e.matmul(pbank(acc[hh]), v_ap(pj, hh, kt), pt[:, hh * 512:(hh + 1) * 512], start=(i == 0), stop=(i == n - 1)),
                         reads=[vbuf(kt), B("PT", i % 3)], writes=[PB(acc[hh])])

            def finalize(pj, qc):
                for hh in range(2):
                    i_ = part_of(pj, hh)
                    lo, hi = i_ * 64, (i_ + 1) * 64
                    dlo, dhi = (1 - i_) * 64, (2 - i_) * 64
                    rd = finA[hh]
                    S.op("act", lambda e: e.activation(out=rd[lo:hi, :], in_=pbank(acc[hh])[dlo:dhi, :], func=AF.Ln),
                         reads=[PB(acc[hh])], writes=[B("finA", hh)])
                    S.op("act", lambda e: e.activation(out=rd[lo:hi, :], in_=rd[lo:hi, :], func=AF.Exp, scale=-1.0), reads=[B("finA", hh)], writes=[B("finA", hh)])
                    dst = ot_dst(pj, hh, qc)
                    S.op("dve", lambda e: e.tensor_tensor(out=dst, in0=pbank(acc[hh])[lo:hi, :], in1=rd[lo:hi, :], op=ALU.mult),
                         reads=[PB(acc[hh]), B("finA", hh)], writes=[ot_buf(pj, hh, qc)])

            def prologue(pj, qc):
                kts = kt_list_fn(qc)
                emit_scores(pj, qc, 0, kts[0])
                if len(kts) > 1:
                    emit_scores(pj, qc, 1, kts[1])

            prologue(*blocks[0])
            for bi, (pj, qc) in enumerate(blocks):
                kts = kt_list_fn(qc)
                n = len(kts)
                for i in range(n):
                    if i + 2 < n:
                        emit_scores(pj, qc, i + 2, kts[i + 2])
                    emit_exp(qc, i, kts[i])
                    emit_pv(pj, i, kts[i], n)
                if bi + 1 < len(blocks):
                    prologue(*blocks[bi + 1])
                finalize(pj, qc)

        def proj_pass(l, col_ranges, first, stages):
            ncols = sum(c1 - c0 for c0, c1 in col_ranges)
            for c in range(8):
                o = 0
                for (c0, c1) in col_ranges:
                    lastw = dma_cast(w_pass[:, c, o:o + (c1 - c0)], DT["w_in"][l, :, c, c0:c1], [B("w_pass", c)], "w_pass")
                    o += c1 - c0
            S.join([B("w_pass", c) for c in range(8)], lastw)
            nb = (ncols + 511) // 512
            norm_tile(0, first, hTt[0][:, :, :], B("hTt", 0), 0)
            for i in range(NT + len(stages) - 1):
                if i + 1 < NT:
                    p1 = (i + 1) % 2
                    norm_tile(i + 1, first, hTt[p1][:, :, :], B("hTt", p1), p1)
                if i < NT:
                    par = i % 2
                    base = par * 2
                    for c in range(8):
                        for n in range(nb):
                            w = min(512, ncols - n * 512)
                            S.op("pe", lambda e, c=c, n=n, w=w, par=par: e.matmul(pbank(base + n)[:, 0:w], hTt[par][:, c, :], w_pass[:, c, n * 512:n * 512 + w], start=(c == 0), stop=(c == 7)),
                                 reads=[B("hTt", par), B("w_pass", c)], writes=[PB(base + n)])
                    for _ in range(N_WARM):
                        S.op("pe", lambda e: e.matmul(pbank(6), ident[:, :], w_pass[:, 0, 0:512], start=True, stop=True),
                             reads=[B("ident"), B("w_pass", 0)], writes=[PB(6)])
                for k, fn in enumerate(stages):
                    t = i - k
                    if 0 <= t < NT:
                        fn(t, t % 2, (t % 2) * 2)

        for b in range(NB):
            if b == 0:
                for t in range(NT):
                    dma_sp(xs[:, t, :], DT["x"][b, t, :, :], [B("x", t)], ("x", t))
            dma_sp(posi[:], DT["pos"][b, :, :], [B("posi")], "posi")
            S.op("dve", lambda e: e.tensor_copy(out=posf[:], in_=posi[:]), reads=[B("posi")], writes=[B("posf")])
            S.op("dve", lambda e: e.tensor_tensor(out=angP[:], in0=posf[:].unsqueeze(2).to_broadcast([128, NT, 8]), in1=invf[:, 0:8].unsqueeze(1).to_broadcast([128, NT, 8]), op=ALU.mult),
                 reads=[B("posf"), B("invf")], writes=[B("angP")])
            S.op("dve", lambda e: e.tensor_tensor(out=angC[:], in0=posf[:].unsqueeze(2).to_broadcast([128, NT, 16]), in1=invf[:, 8:24].unsqueeze(1).to_broadcast([128, NT, 16]), op=ALU.mult),
                 reads=[B("posf"), B("invf")], writes=[B("angC")])
            aT8 = angT_all[:, :, 0:8]
            S.op("dve", lambda e: e.tensor_scalar(out=aT8, in0=angP[:], scalar1=math.pi / 2, scalar2=None, op0=ALU.add), reads=[B("angP")], writes=[B("finA", 2)])
            range_reduce_sin(cosP[:], aT8, 8, ("finA", 2))
            range_reduce_sin(sinP[:], angP[:], 8, "angP")
            S.op("dve", lambda e: e.tensor_scalar(out=angT_all[:], in0=angC[:], scalar1=math.pi / 2, scalar2=None, op0=ALU.add), reads=[B("angC")], writes=[B("finA", 2)])
            range_reduce_sin(cosC[:], angT_all[:], 16, ("finA", 2))
            range_reduce_sin(sinC[:], angC[:], 16, "angC")

            def stage(n):
                if n > upto:
                    raise _Stop()

            for l in layers:
              try:
                  stage(2)
                  load_grow(gpre, "gpre", l, 0)
                  scaleA = 64 ** -0.5
                  for j in range(2):
                      def tileA(t, par, base):
                          st = stA[par]
                          S.op("act", lambda e: e.activation(out=st[:], in_=pbank(base), func=AF.Copy), reads=[PB(base)], writes=[B("stA", par)])
                          S.op("dve", lambda e: e.tensor_copy(out=VA[:, t, :], in_=pbank(base + 1)[:, 0:256]), reads=[PB(base + 1)], writes=[B("VA", t)])
                          rope_inplace(st[:].rearrange("p (g d) -> p g d", d=64), 8, 0, 8, cosP, sinP, t, [B("stA", par)], par)

                      def tileA2(t, par, base):
                          st = stA[par]
                          bank = 4 + par
                          pv = pbank_bf(bank)
                          for k4 in range(4):
                              S.op("pe", lambda e, k4=k4: e.transpose(pv[:, k4 * 128:(k4 + 1) * 128], st[:, k4 * 128:(k4 + 1) * 128], ident[:, :]),
                                   reads=[B("stA", par), B("ident")], writes=[PB(bank)])
                          S.op("act", lambda e: e.activation(out=QAT[:, :, t * 128:(t + 1) * 128], in_=pv[:, 0:256].rearrange("p (h k) -> p h k", h=2), func=AF.Copy),
                               reads=[PB(bank)], writes=[B("QT", t)])
                          S.op("dve", lambda e: e.tensor_copy(out=KAT[:, :, t * 128:(t + 1) * 128], in_=pv[:, 256:512].rearrange("p (h k) -> p h k", h=2)),
                               reads=[PB(bank)], writes=[B("KT", t)])
                      proj_pass(l, [(j * 256, (j + 1) * 256), (512 + j * 256, 512 + (j + 1) * 256), (1024 + j * 256, 1024 + (j + 1) * 256)], (j == 0), [tileA, tileA2])
                      stage(3 if j == 0 else 4)
                      sbanksA = [(2, 3), (4, 5), (6, 7)]
                      itersA = [(hh, qc) for hh in range(2) for qc in range(4)]

                      def a_scores(hh, qc, kt):
                          sb_ = sbanksA[kt % 3]
                          for m in range(2):
                              S.op("pe", lambda e, m=m: e.matmul(pbank(sb_[m]), KAT[m * 64:(m + 1) * 64, hh, kt * 128:(kt + 1) * 128], QAT[m * 64:(m + 1) * 64, hh, qc * 512:(qc + 1) * 512], start=True, stop=True),
                                   reads=[B("KT", kt)] + [B("QT", qc * 4 + i) for i in range(4)], writes=[PB(sb_[m])])

                      def a_exp(kt, ai):
                          sb_ = sbanksA[kt % 3]
                          pt = PT[kt % 3]
                          S.op("act", lambda e: e.activation(out=pt[:], in_=pbank(sb_[0], 2), func=AF.Exp, scale=scaleA),
                               reads=[PB(sb_[0]), PB(sb_[1])], writes=[B("PT", kt % 3)])
                          pa = Pacc[:, ai, :]
                          if kt == 0:
                              S.op("dve", lambda e: e.tensor_copy(out=pa, in_=pt[:]), reads=[B("PT", kt % 3)], writes=[B("Pacc", ai, 0), B("Pacc", ai, 1)])
                          else:
                              S.op("dve", lambda e: e.tensor_tensor(out=pa, in0=pa, in1=pt[:], op=ALU.add), reads=[B("PT", kt % 3), B("Pacc", ai, 0), B("Pacc", ai, 1)], writes=[B("Pacc", ai, 0), B("Pacc", ai, 1)])

                      def a_pv(hh, kt):
                          pt = PT[kt % 3]
                          for m in range(2):
                              S.op("pe", lambda e, m=m: e.matmul(pbank(m), VA[:, kt, hh * 128:(hh + 1) * 128], pt[:, m * 512:(m + 1) * 512], start=(kt == 0), stop=(kt == NT - 1)),
                                   reads=[B("VA", kt), B("PT", kt % 3)], writes=[PB(m)])

                      def a_finalize(hh, qc, ai):
                          h = 2 * j + hh
                          rD0, rD1, oo, lnss = finA
                          S.op("dve", lambda e: e.tensor_copy(out=junk[:], in_=Pacc[:, ai, :]), reads=[B("Pacc", ai, 0), B("Pacc", ai, 1)], writes=[B("junk")])
                          for m in range(2):
                              S.op("pe", lambda e, m=m: e.matmul(pbank(6 + m), ones[:, :], junk[:, m * 512:(m + 1) * 512], start=True, stop=True), reads=[B("ones"), B("junk")], writes=[PB(6 + m)])
                          S.op("act", lambda e: e.activation(out=rD0[:], in_=pbank(6), func=AF.Ln), reads=[PB(6)], writes=[B("finA", 0)])
                          S.op("act", lambda e: e.activation(out=rD1[:], in_=pbank(7), func=AF.Ln), reads=[PB(7)], writes=[B("finA", 1)])
                          S.op("act", lambda e: e.activation(out=rD0[:], in_=rD0[:], func=AF.Exp, scale=-1.0), reads=[B("finA", 0)], writes=[B("finA", 0)])
                          S.op("act", lambda e: e.activation(out=rD1[:], in_=rD1[:], func=AF.Exp, scale=-1.0), reads=[B("finA", 1)], writes=[B("finA", 1)])
                          S.op("dve", lambda e: e.tensor_tensor(out=oo[:], in0=pbank(0), in1=rD0[:], op=ALU.mult), reads=[PB(0), B("finA", 0)], writes=[B("finA", 2)])
                          S.op("dve", lambda e: e.scalar_tensor_tensor(out=rD1[:], in0=pbank(1), scalar=neglam[:, l:l + 1], in1=rD1[:], op0=ALU.mult, op1=ALU.mult),
                               reads=[PB(1), B("finA", 1), B("neglam", l)], writes=[B("finA", 1)])
                          S.op("pool", lambda e: e.tensor_tensor(out=oo[:], in0=oo[:], in1=rD1[:], op=ALU.add), reads=[B("finA", 1), B("finA", 2)], writes=[B("finA", 2)])
                          S.op("act", lambda e: e.activation(out=finB[:], in_=oo[:], func=AF.Square), reads=[B("finA", 2)], writes=[B("finB")])
                          S.op("pe", lambda e: e.matmul(pbank(6), ones[:, :], finB[:, :], start=True, stop=True), reads=[B("ones"), B("finB")], writes=[PB(6)])
                          rstd_to(lnss[:], B("finA", 3), pbank(6), [PB(6)], 128)
                          S.op("dve", lambda e: e.scalar_tensor_tensor(out=OT[:, h, qc * 512:(qc + 1) * 512], in0=oo[:], scalar=sublnS[:, l:l + 1], in1=lnss[:], op0=ALU.mult, op1=ALU.mult),
                               reads=[B("finA", 2), B("finA", 3), B("sublnS", l)], writes=[B("OT", h, qc, 0), B("OT", h, qc, 1)])

                      a_scores(itersA[0][0], itersA[0][1], 0)
                      a_scores(itersA[0][0], itersA[0][1], 1)
                      for ai_, (hh, qc) in enumerate(itersA):
                          ai = ai_ % 2
                          for kt in range(NT):
                              if kt + 2 < NT:
                                  a_scores(hh, qc, kt + 2)
                              a_exp(kt, ai)
                              a_pv(hh, kt)
                          if ai_ + 1 < len(itersA):
                              a_scores(itersA[ai_ + 1][0], itersA[ai_ + 1][1], 0)
                              a_scores(itersA[ai_ + 1][0], itersA[ai_ + 1][1], 1)
                          a_finalize(hh, qc, ai)

                  stage(5)
                  S.alias([B("VBt", t) for t in range(NT)] + [B("Vones")], [B("VA", t) for t in range(NT)] + [B("Vones")])
                  for h in range(4):
                      i_ = h % 2
                      S.op("pool", lambda e, h=h, i_=i_: e.memset(VB[:, :, h, (1 - i_) * 64:(2 - i_) * 64], 1.0), writes=[B("Vones")])

                  def tileB(t, par, base):
                      st = stA[par]
                      S.op("act", lambda e: e.activation(out=st[:], in_=pbank(base), func=AF.Copy), reads=[PB(base)], writes=[B("stA", par)])
                      for h in range(4):
                          i_ = h % 2
                          S.op("dve", lambda e, h=h, i_=i_: e.tensor_copy(out=VB[:, t, h, i_ * 64:(i_ + 1) * 64], in_=pbank(base + 1)[:, h * 64:(h + 1) * 64]),
                               reads=[PB(base + 1), B("Vones")], writes=[B("VBt", t)])
                      rope_inplace(st[:].rearrange("p (g d) -> p g d", d=64), 8, 0, 8, cosP, sinP, t, [B("stA", par)], par)

                  def tileB2(t, par, base):
                      st = stA[par]
                      bank = 4 + par
                      pv = pbank_bf(bank)
                      for k4 in range(4):
                          S.op("pe", lambda e, k4=k4: e.transpose(pv[:, k4 * 128:(k4 + 1) * 128], st[:, k4 * 128:(k4 + 1) * 128], ident[:, :]),
                               reads=[B("stA", par), B("ident")], writes=[PB(bank)])
                      S.op("act", lambda e: e.activation(out=QBT[:, :, t * 128:(t + 1) * 128], in_=pv[:, 0:256].rearrange("p (h k) -> p h k", h=2), func=AF.Copy),
                           reads=[PB(bank)], writes=[B("QT", t)])
                      S.op("dve", lambda e: e.tensor_copy(out=KBT[:, :, t * 128:(t + 1) * 128], in_=pv[:, 256:512].rearrange("p (h k) -> p h k", h=2)),
                           reads=[PB(bank)], writes=[B("KT", t)])
                  proj_pass(l, [(1536, 2304)], False, [tileB, tileB2])

                  def kts_B(qc):
                      lo = max(0, (qc * 512 - 1024) // 128)
                      hi = min(NT - 1, (qc * 512 + 511 + 1024) // 128)
                      return list(range(lo, hi + 1))
                  attention_pairs(
                      2,
                      q_ap=lambda pj, hh, qc: QBT[hh * 64:(hh + 1) * 64, pj, qc * 512:(qc + 1) * 512],
                      k_ap=lambda pj, hh, kt: KBT[hh * 64:(hh + 1) * 64, pj, kt * 128:(kt + 1) * 128],
                      v_ap=lambda pj, hh, kt: VB[:, kt, 2 * pj + hh, :],
                      qbufs=lambda qc: [B("QT", qc * 4 + i) for i in range(4)], kbuf=lambda kt: B("KT", kt), vbuf=lambda kt: B("VBt", kt),
                      scale=64 ** -0.5,
                      ot_dst=lambda pj, hh, qc: OT[hh * 64:(hh + 1) * 64, 4 + pj, qc * 512:(qc + 1) * 512],
                      ot_buf=lambda pj, hh, qc: B("OT", 4 + pj, qc, hh), masked=True, kt_list_fn=kts_B, part_of=lambda pj, hh: hh)

                  stage(6)
                  for j in range(2):
                      dma_cast(wq_up_s[:], DT["wq_up"][l, :, :, :], [B("wq_up")], "wq_up")
                      dma_cast(wkv_up_s[:], DT["wkv_up"][l, :, :], [B("wkv_up")], "wkv_up")
                      for c in range(2):
                          S.op("pool", lambda e, c=c: e.tensor_scalar(out=wq_up_s[:, c, :], in0=wq_up_s[:, c, :], scalar1=qng[:, l * 2 + c:l * 2 + c + 1], scalar2=None, op0=ALU.mult),
                               reads=[B("wq_up"), B("qng")], writes=[B("wq_up")])
                      S.op("pool", lambda e: e.tensor_scalar(out=wkv_up_s[:], in0=wkv_up_s[:], scalar1=kvng[:, l:l + 1], scalar2=None, op0=ALU.mult),
                           reads=[B("wkv_up"), B("kvng")], writes=[B("wkv_up")])
                      S.alias([B("VA", t) for t in range(NT)] + [B("Vones")], [B("VBt", t) for t in range(NT)] + [B("VA", t) for t in range(NT)] + [B("Vones")])
                      for hh in range(2):
                          S.op("pool", lambda e, hh=hh: e.memset(VC[:, :, hh, (1 - hh) * 64:(2 - hh) * 64], 1.0), writes=[B("Vones")])

                      def tileC(t, par, base):
                          pc = pbank(base)
                          colq, colk = stat_col(), stat_col()
                          S.op("act", lambda e: e.activation(out=junk[:, 0:256], in_=pc[:, 0:256], func=AF.Square, accum_out=ssq[:, colq:colq + 1]),
                               reads=[PB(base)], writes=[B("junk"), B("ssq", colq)])
                          S.op("act", lambda e: e.activation(out=junk[:, 256:384], in_=pc[:, 256:384], func=AF.Square, accum_out=ssq[:, colk:colk + 1]),
                               reads=[PB(base)], writes=[B("junk"), B("ssq", colk)])
                          rstd_to(rstd[:, colq:colq + 1], B("rstd", colq), ssq[:, colq:colq + 1], [B("ssq", colq)], 256)
                          rstd_to(rstd[:, colk:colk + 1], B("rstd", colk), ssq[:, colk:colk + 1], [B("ssq", colk)], 128)
                          sc = stC[par]
                          S.op("dve", lambda e: e.tensor_scalar(out=sc[:, 0:256], in0=pc[:, 0:256], scalar1=rstd[:, colq:colq + 1], scalar2=None, op0=ALU.mult),
                               reads=[PB(base), B("rstd", colq)], writes=[B("stC", par)])
                          S.op("dve", lambda e: e.tensor_scalar(out=sc[:, 256:384], in0=pc[:, 256:384], scalar1=rstd[:, colk:colk + 1], scalar2=None, op0=ALU.mult),
                               reads=[PB(base), B("rstd", colk), B("stC", par)], writes=[B("stC", par)])
                          skr = stKR[t % 4]
                          S.op("dve", lambda e: e.tensor_copy(out=skr[:], in_=pc[:, 384:416]), reads=[PB(base)], writes=[B("stKR", t % 4)])
                          rope_inplace(skr[:].rearrange("p (g d) -> p g d", g=1), 1, 0, 16, cosC, sinC, t, [B("stKR", t % 4)], par)

                      def tileC2(t, par, base):
                          sc = stC[par]
                          bank = 4 + par
                          pv = pbank_bf(bank)
                          for k3 in range(3):
                              S.op("pe", lambda e, k3=k3: e.transpose(pv[:, k3 * 128:(k3 + 1) * 128], sc[:, k3 * 128:(k3 + 1) * 128], ident[:, :]),
                                   reads=[B("stC", par), B("ident")], writes=[PB(bank)])
                          ct = cT[par]
                          S.op("act", lambda e: e.activation(out=ct[:].rearrange("p a b -> p (a b)"), in_=pv[:, 0:384], func=AF.Copy), reads=[PB(bank)], writes=[B("cT", par)])

                      def tileC2b(t, par, base):
                          ct = cT[par]
                          skr = stKR[t % 4]
                          bu = base + 1
                          for c in range(2):
                              S.op("pe", lambda e, c=c: e.matmul(pbank(bu)[:, 0:192], ct[:, c, :], wq_up_s[:, c, j * 192:(j + 1) * 192], start=(c == 0), stop=(c == 1)),
                                   reads=[B("cT", par), B("wq_up")], writes=[PB(bu)])
                          S.op("pe", lambda e: e.matmul(pbank(bu)[:, 256:512], ct[:, 2, :], wkv_up_s[:, j * 256:(j + 1) * 256], start=True, stop=True),
                               reads=[B("cT", par), B("wkv_up")], writes=[PB(bu)])
                          sq = stQC[par]
                          S.op("act", lambda e: e.activation(out=sq[:], in_=pbank(bu)[:, 0:192], func=AF.Copy), reads=[PB(bu)], writes=[B("stQC", par)])
                          rope_inplace(sq[:].rearrange("p (g d) -> p g d", d=96), 2, 64, 16, cosC, sinC, t, [B("stQC", par)], par)
                          sk = stKC[par]
                          kvv = pbank(bu)[:, 256:512].rearrange("p (h n) -> p h n", h=2)
                          S.op("dve", lambda e: e.tensor_copy(out=sk[:, :, 0:64], in_=kvv[:, :, 0:64]), reads=[PB(bu)], writes=[B("stKC", par)])
                          S.op("pool", lambda e: e.tensor_copy(out=sk[:, :, 64:96], in_=skr[:].unsqueeze(1).to_broadcast([128, 2, 32])),
                               reads=[B("stKR", t % 4), B("stKC", par)], writes=[B("stKC", par)])
                          for hh in range(2):
                              S.op("dve", lambda e, hh=hh: e.tensor_copy(out=VC[:, t, hh, hh * 64:(hh + 1) * 64], in_=kvv[:, hh, 64:128]),
                                   reads=[PB(bu), B("Vones")], writes=[B("VA", t)])

                      def tileC3(t, par, base):
                          sq = stQC[par]
                          sk = stKC[par]
                          bank = 4 + par
                          pv = pbank_bf(bank)
                          for which in range(2):
                              off0 = 512 if which == 0 else 0
                              for hh in range(2):
                                  src = sq[:, hh * 96:(hh + 1) * 96] if which == 0 else sk[:, hh, :]
                                  S.op("pe", lambda e, off0=off0, hh=hh, src=src: e.transpose(pv[0:96, off0 + hh * 128:off0 + (hh + 1) * 128], src, ident[:, :]),
                                       reads=[B("stQC", par) if which == 0 else B("stKC", par), B("ident")], writes=[PB(bank)])
                              dstT = (QCT if which == 0 else KCT)[0:96, :, t * 128:(t + 1) * 128]
                              srcv = pv[0:96, off0:off0 + 256].rearrange("p (h k) -> p h k", h=2)
                              if which == 0:
                                  S.op("act", lambda e, dstT=dstT, srcv=srcv: e.activation(out=dstT, in_=srcv, func=AF.Copy), reads=[PB(bank)], writes=[B("QT", t)])
                              else:
                                  S.op("dve", lambda e, dstT=dstT, srcv=srcv: e.tensor_copy(out=dstT, in_=srcv), reads=[PB(bank)], writes=[B("KT", t)])
                      proj_pass(l, [(2304, 2720)], False, [tileC, tileC2, tileC2b, tileC3])
                      attention_pairs(
                          1,
                          q_ap=lambda pj, hh, qc: QCT[0:96, hh, qc * 512:(qc + 1) * 512],
                          k_ap=lambda pj, hh, kt: KCT[0:96, hh, kt * 128:(kt + 1) * 128],
                          v_ap=lambda pj, hh, kt: VC[:, kt, hh, :],
                          qbufs=lambda qc: [B("QT", qc * 4 + i) for i in range(4)], kbuf=lambda kt: B("KT", kt), vbuf=lambda kt: B("VA", kt),
                          scale=96 ** -0.5,
                          ot_dst=lambda pj, hh, qc, j=j: OT[hh * 64:(hh + 1) * 64, 6 + j, qc * 512:(qc + 1) * 512],
                          ot_buf=lambda pj, hh, qc, j=j: B("OT", 6 + j, qc, hh), masked=False, kt_list_fn=lambda qc: list(range(NT)), part_of=lambda pj, hh: hh)

                  stage(7)
                  S.alias([B("w_out", c) for c in range(8)], [B("w_pass", c) for c in range(8)] + [B("wq_up"), B("wkv_up")])
                  for c in range(8):
                      lastw = dma_cast(w_out_s[:, c, :], DT["w_out"][l, :, c, :], [B("w_out", c)], "w_out")
                  S.join([B("w_out", c) for c in range(8)], lastw)
                  load_grow(gb, "gb", l, 1)
                  for t in range(NT):
                      yb = (t % 2) * 2
                      for c in range(8):
                          for n in range(2):
                              S.op("pe", lambda e, c=c, n=n, yb=yb, t=t: e.matmul(pbank(yb + n), OT[:, c, t * 128:(t + 1) * 128], w_out_s[:, c, n * 512:(n + 1) * 512], start=(c == 0), stop=(c == 7)),
                                   reads=[B("OT", c, t // 4, 0), B("OT", c, t // 4, 1), B("w_out", c)], writes=[PB(yb + n)])
                      post_norm_residual(yb, t)

                  stage(8)
                  mixbufs = [B("VA", t) for t in range(NT)] + [B("VBt", t) for t in range(NT)] + [B("QT", t) for t in range(NT)] + [B("KT", t) for t in range(NT)] + [B("Vones")]
                  wbufs_old = [B("w_out", c) for c in range(8)]
                  membufs = [B("memf"), B("w_mkv")]
                  S.alias(membufs, wbufs_old)
                  S.alias([B("memT"), B("w_mq"), B("w_mo"), B("KmT"), B("Vm")] + [B("QmT", t) for t in range(NT)] + [B("OmT", pj, qc, hh) for pj in range(2) for qc in range(4) for hh in range(2)], mixbufs)
                  load_grow(gpre, "gpre", l, 2)
                  load_grow(gb, "gb", l, 4)
                  dma_cast(w_mq_s[:], DT["w_mq"][l, :, :, :], [B("w_mq")], "w_mq")
                  dma_cast(w_mkv_s[:], DT["w_mkv"][l, :, :, :], [B("w_mkv")], "w_mkv")
                  dma_cast(w_mo_s[:], DT["w_mo"][l, :, :, :], [B("w_mo")], "w_mo")
                  for kt in range(2):
                      dma_sp(memf[:, kt, :], DT["mem"][b, kt, :, :], [B("memf")], ("memf", kt))
                  gmem = tmp_pn[0]
                  dma_sp(gmem, DT["grow"][l * 7 + 3:l * 7 + 4, :].partition_broadcast(128), [B("finA", 2), B("finA", 3)], "gmem")
                  for kt in range(2):
                      col = stat_col()
                      S.op("act", lambda e, kt=kt, col=col: e.activation(out=junk[:], in_=memf[:, kt, :], func=AF.Square, accum_out=ssq[:, col:col + 1]),
                           reads=[B("memf")], writes=[B("junk"), B("ssq", col)])
                      rstd_to(rstd[:, col:col + 1], B("rstd", col), ssq[:, col:col + 1], [B("ssq", col)], D)
                      hb = hn[kt]
                      S.op("dve", lambda e, kt=kt, col=col, hb=hb: e.scalar_tensor_tensor(out=hb[:], in0=memf[:, kt, :], scalar=rstd[:, col:col + 1], in1=gmem, op0=ALU.mult, op1=ALU.mult),
                           reads=[B("memf"), B("rstd", col), B("finA", 2), B("finA", 3)], writes=[B("hn", kt)])
                      pv = pbank_bf(7)
                      for c in range(8):
                          S.op("pe", lambda e, c=c, hb=hb: e.transpose(pv[:, c * 128:(c + 1) * 128], hb[:, c * 128:(c + 1) * 128], ident[:, :]),
                               reads=[B("hn", kt), B("ident")], writes=[PB(7)])
                      S.op("act", lambda e, kt=kt: e.activation(out=memT[:, :, kt * 128:(kt + 1) * 128], in_=pv[:, :].rearrange("p (c k) -> p c k", c=8), func=AF.Copy),
                           reads=[PB(7)], writes=[B("memT")])
                  for jj in range(2):
                      for c in range(8):
                          S.op("pe", lambda e, jj=jj, c=c: e.matmul(pbank(jj)[:, 0:256], w_mkv_s[:, c, jj * 128:(jj + 1) * 128], memT[:, c, :], start=(c == 0), stop=(c == 7)),
                               reads=[B("w_mkv"), B("memT")], writes=[PB(jj)])
                      S.op("act", lambda e, jj=jj: e.activation(out=KmT[:, jj, :], in_=pbank(jj)[:, 0:256], func=AF.Copy), reads=[PB(jj)], writes=[B("KmT")])
                  for h in range(4):
                      i_ = h % 2
                      S.op("pool", lambda e, h=h, i_=i_: e.memset(Vm[:, :, h, (1 - i_) * 64:(2 - i_) * 64], 1.0), reads=[B("Vm")], writes=[B("Vm")])
                  for kt in range(2):
                      for c in range(8):
                          S.op("pe", lambda e, kt=kt, c=c: e.matmul(pbank(2 + kt)[:, 0:256], memT[:, c, kt * 128:(kt + 1) * 128], w_mkv_s[:, c, 256:512], start=(c == 0), stop=(c == 7)),
                               reads=[B("w_mkv"), B("memT")], writes=[PB(2 + kt)])
                      for h in range(4):
                          i_ = h % 2
                          S.op("dve", lambda e, kt=kt, h=h, i_=i_: e.tensor_copy(out=Vm[:, kt, h, i_ * 64:(i_ + 1) * 64], in_=pbank(2 + kt)[:, h * 64:(h + 1) * 64]),
                               reads=[PB(2 + kt), B("Vm")], writes=[B("Vm")])
                  def memq_mm(t):
                      par = t % 2
                      for c in range(8):
                          S.op("pe", lambda e, c=c: e.matmul(pbank(par)[:, 0:256], hTt[par][:, c, :], w_mq_s[:, c, :], start=(c == 0), stop=(c == 7)),
                               reads=[B("hTt", par), B("w_mq")], writes=[PB(par)])
                      st = stA[par]
                      S.op("act", lambda e: e.activation(out=st[:, 0:256], in_=pbank(par)[:, 0:256], func=AF.Copy), reads=[PB(par)], writes=[B("stA", par)])

                  def memq_tr(t):
                      par = t % 2
                      st = stA[par]
                      bank = 4 + par
                      pv = pbank_bf(bank)
                      for k2 in range(2):
                          S.op("pe", lambda e, k2=k2: e.transpose(pv[:, k2 * 128:(k2 + 1) * 128], st[:, k2 * 128:(k2 + 1) * 128], ident[:, :]),
                               reads=[B("stA", par), B("ident")], writes=[PB(bank)])
                      S.op("dve", lambda e: e.tensor_copy(out=QmT[:, :, t * 128:(t + 1) * 128], in_=pv[:, 0:256].rearrange("p (h k) -> p h k", h=2)),
                           reads=[PB(bank)], writes=[B("QmT", t)])

                  norm_tile(0, True, hTt[0][:, :, :], B("hTt", 0), 0)
                  for i in range(NT + 1):
                      if i + 1 < NT:
                          p1 = (i + 1) % 2
                          norm_tile(i + 1, True, hTt[p1][:, :, :], B("hTt", p1), p1)
                      if i < NT:
                          memq_mm(i)
                      if i - 1 >= 0:
                          memq_tr(i - 1)
                  attention_pairs(
                      2,
                      q_ap=lambda pj, hh, qc: QmT[hh * 64:(hh + 1) * 64, pj, qc * 512:(qc + 1) * 512],
                      k_ap=lambda pj, hh, kt: KmT[hh * 64:(hh + 1) * 64, pj, kt * 128:(kt + 1) * 128],
                      v_ap=lambda pj, hh, kt: Vm[:, kt, 2 * pj + hh, :],
                      qbufs=lambda qc: [B("QmT", qc * 4 + i) for i in range(4)], kbuf=lambda kt: B("KmT"), vbuf=lambda kt: B("Vm"),
                      scale=64 ** -0.5,
                      ot_dst=lambda pj, hh, qc: OmT[hh * 64:(hh + 1) * 64, pj, qc * 512:(qc + 1) * 512],
                      ot_buf=lambda pj, hh, qc: B("OmT", pj, qc, hh), masked=False, kt_list_fn=lambda qc: [0, 1], part_of=lambda pj, hh: hh)
                  for t in range(NT):
                      yb = (t % 2) * 2 + 4
                      for c in range(2):
                          for n in range(2):
                              S.op("pe", lambda e, c=c, n=n, yb=yb, t=t: e.matmul(pbank(yb + n), OmT[:, c, t * 128:(t + 1) * 128], w_mo_s[:, c, n * 512:(n + 1) * 512], start=(c == 0), stop=(c == 1)),
                                   reads=[B("OmT", c, t // 4, 0), B("OmT", c, t // 4, 1), B("w_mo")], writes=[PB(yb + n)])
                      post_norm_residual(yb, t)

                  stage(9)
                  memall = [B("memT"), B("w_mq"), B("w_mo"), B("KmT"), B("Vm"), B("memf"), B("w_mkv")] + [B("QmT", t) for t in range(NT)] + [B("OmT", pj, qc, hh) for pj in range(2) for qc in range(4) for hh in range(2)]
                  S.alias([B("w_d", f) for f in range(NFC)] + [B("wgu", i) for i in range(NSLOT)], memall + [B("Pacc", 0, 0), B("Pacc", 0, 1), B("Pacc", 1, 0), B("Pacc", 1, 1)])
                  S.alias([B("hTq"), B("HT")] , [B("OT", c, qc, i) for c in range(8) for qc in range(4) for i in range(2)])
                  load_grow(gpre, "gpre", l, 5)
                  load_grow(gb, "gb", l, 6)
                  nload = [0]

                  def load_slot(f):
                      s_ = f % NSLOT
                      dma_cast(wgu[s_][:, 0, :, :], DT["w_g"][l, f, :, :, :], [B("wgu", s_)], ("wgu", s_))
                      dma_cast(wgu[s_][:, 1, :, :], DT["w_u"][l, f, :, :, :], [B("wgu", s_)], ("wgu", s_))
                  for q4 in range(4):
                      for ti in range(4):
                          t = q4 * 4 + ti
                          norm_tile(t, True, hTq[:, :, ti * 128:(ti + 1) * 128], B("hTq"), ti % 2)
                      if SUB == 7:
                          continue
                      for f in range(min(NSLOT - 1, NFC)):
                          load_slot(f)
                      for f in range(NFC):
                          if f + NSLOT - 1 < NFC:
                              load_slot(f + NSLOT - 1)
                          if q4 == 0:
                              lastw = dma_cast(w_d_s[:, f, :], DT["w_d"][l, :, f, :], [B("w_d", f)], "w_d")
                              if f == NFC - 1:
                                  S.join([B("w_d", ff) for ff in range(NFC)], lastw)
                          s_ = f % NSLOT
                          gbk, ubk = (f % 2) * 2, (f % 2) * 2 + 1
                          for c in range(8):
                              S.op("pe", lambda e, c=c, s_=s_, gbk=gbk: e.matmul(pbank(gbk), wgu[s_][:, 0, c, :], hTq[:, c, :], start=(c == 0), stop=(c == 7)),
                                   reads=[B("wgu", s_), B("hTq")], writes=[PB(gbk)])
                          for c in range(8):
                              S.op("pe", lambda e, c=c, s_=s_, ubk=ubk: e.matmul(pbank(ubk), wgu[s_][:, 1, c, :], hTq[:, c, :], start=(c == 0), stop=(c == 7)),
                                   reads=[B("wgu", s_), B("hTq")], writes=[PB(ubk)])
                          sgt = sg[f % 2]
                          S.op("act", lambda e, sgt=sgt, gbk=gbk: e.activation(out=sgt[:], in_=pbank(gbk), func=AF.Silu), reads=[PB(gbk)], writes=[B("sg", f % 2)])
                          S.op("dve", lambda e, sgt=sgt, ubk=ubk, f=f: e.tensor_tensor(out=HT[:, f, :], in0=pbank(ubk), in1=sgt[:], op=ALU.mult),
                               reads=[PB(ubk), B("sg", f % 2)], writes=[B("HT")])
                      if SUB == 8:
                          continue
                      for ti in range(4):
                          t = q4 * 4 + ti
                          yb = 4 if ti % 2 == 0 else 2
                          for f in range(NFC):
                              for n in range(2):
                                  S.op("pe", lambda e, f=f, n=n, yb=yb, ti=ti: e.matmul(pbank(yb + n), HT[:, f, ti * 128:(ti + 1) * 128], w_d_s[:, f, n * 512:(n + 1) * 512], start=(f == 0), stop=(f == NFC - 1)),
                                       reads=[B("HT"), B("w_d", f)], writes=[PB(yb + n)])
                          if SUB == 9:
                              continue
                          post_norm_residual(yb, t)
                      if SUB == 10:
                          break
                  nxt = [B("OT", c, qc, i) for c in range(8) for qc in range(4) for i in range(2)] + mixbufs + [B("w_pass", c) for c in range(8)] + [B("wq_up"), B("wkv_up"), B("Pacc", 0, 0), B("Pacc", 0, 1), B("Pacc", 1, 0), B("Pacc", 1, 1)]
                  S.alias(nxt, [B("hTq"), B("HT")] + [B("w_d", f) for f in range(NFC)] + [B("wgu", i) for i in range(NSLOT)])

              except _Stop:
                pass

            outbufs = []
            for t0_, t1_ in ((0, 12), (12, NT)):
                for t in range(t0_, t1_):
                    dma_sp(DT["out"][b, t, :, :], xs[:, t, :], [B("hbm_out", b, t)], ("out", t), reads=[B("x", t)])
                    outbufs.append(B("hbm_out", b, t))
                if b + 1 < NB:
                    for t in range(t0_, t1_):
                        dma_sp(xs[:, t, :], DT["x"][b + 1, t, :, :], [B("x", t)], ("x", t))
            if b == NB - 1:
                S.op("sp", lambda e: None, reads=outbufs)

        S.finalize()
        sems = {e_: es.enter_context(nc.semaphore("s_" + e_)) for e_ in S.ENGS}
        dsems = {k: es.enter_context(nc.semaphore("d_%d" % i)) for i, k in enumerate(S.dma_groups)}
        block = es.enter_context(nc.Block())
        block.sync(lambda e: S.emit_stream("sp", e, sems, dsems))
        block.tensor(lambda e: S.emit_stream("pe", e, sems, dsems))
        block.scalar(lambda e: S.emit_stream("act", e, sems, dsems))
        block.vector(lambda e: S.emit_stream("dve", e, sems, dsems))
        block.gpsimd(lambda e: S.emit_stream("pool", e, sems, dsems))
        build_program.stats = {e_: len(S.streams[e_]) for e_ in S.ENGS}
        build_program.nsem = len(dsems) + 5
    return nc


def _mask_table():
    m = np.zeros((128, MASK_W), np.float32)
    p = np.arange(128)[:, None]
    c = np.arange(MASK_W)[None, :]
    diff = c - p - MASK_C0 + 0
    a = np.abs(diff)
    m += (a <= 64)
    m += ((a % 4 == 0) & (a <= 256))
    m += ((a % 16 == 0) & (a <= 1024))
    return m.astype(np.float32)


def prep_shared(inp):
    f = np.float32
    L = DEPTH
    ng = np.asarray(inp["norm_gains"], f)
    sh = {}
    sh["ident"] = np.eye(128, dtype=f)
    sh["maskb"] = _mask_table()
    inv8 = (500000.0 ** (-np.arange(0, 16, 2, dtype=np.float32) / np.float32(16))).astype(f)
    inv16 = (500000.0 ** (-np.arange(0, 32, 2, dtype=np.float32) / np.float32(32))).astype(f)
    sh["invf"] = np.ascontiguousarray(np.tile(np.concatenate([inv8, inv16])[None, :], (128, 1)).astype(f))
    sh["gT"] = np.ascontiguousarray(ng.reshape(L, 7, 8, 128).transpose(3, 0, 1, 2).reshape(128, L * 7 * 8))
    sh["grow"] = np.ascontiguousarray(ng.reshape(L * 7, D))
    sh["w_in"] = np.ascontiguousarray(np.asarray(inp["w_in"], f).reshape(L, 8, 128, D_IN).transpose(0, 2, 1, 3))
    sh["w_out"] = np.ascontiguousarray(np.asarray(inp["w_out"], f).reshape(L, 8, 128, D).transpose(0, 2, 1, 3))
    sh["lam"] = np.ascontiguousarray(np.asarray(inp["diff_lambda"], f).reshape(L, 256))
    sh["subln"] = np.ascontiguousarray(np.asarray(inp["diff_subln"], f).T)
    sh["qng"] = np.ascontiguousarray(np.asarray(inp["mla_q_norm"], f).reshape(L, 2, 128).transpose(2, 0, 1).reshape(128, L * 2))
    sh["kvng"] = np.ascontiguousarray(np.asarray(inp["mla_kv_norm"], f).T)
    sh["wq_up"] = np.ascontiguousarray(np.asarray(inp["w_mla_q_up"], f).reshape(L, 2, 128, 384).transpose(0, 2, 1, 3))
    sh["wkv_up"] = np.ascontiguousarray(np.asarray(inp["w_mla_kv_up"], f))
    sh["w_mq"] = np.ascontiguousarray(np.asarray(inp["w_mem_q"], f).reshape(L, 8, 128, 256).transpose(0, 2, 1, 3))
    sh["w_mkv"] = np.ascontiguousarray(np.asarray(inp["w_mem_kv"], f).reshape(L, 8, 128, 512).transpose(0, 2, 1, 3))
    sh["w_mo"] = np.ascontiguousarray(np.asarray(inp["w_mem_o"], f).reshape(L, 2, 128, D).transpose(0, 2, 1, 3))
    sh["w_g"] = np.ascontiguousarray(np.asarray(inp["w_ffn_gate"], f).reshape(L, 8, 128, NFC, 128).transpose(0, 3, 2, 1, 4))
    sh["w_u"] = np.ascontiguousarray(np.asarray(inp["w_ffn_up"], f).reshape(L, 8, 128, NFC, 128).transpose(0, 3, 2, 1, 4))
    sh["w_d"] = np.ascontiguousarray(np.asarray(inp["w_ffn_down"], f).reshape(L, NFC, 128, D).transpose(0, 2, 1, 3))
    return sh


def prep_core(inp, b0, nb):
    x = np.asarray(inp["x"], np.float32)[b0:b0 + nb]
    mem = np.asarray(inp["mem"], np.float32)[b0:b0 + nb]
    pos = np.asarray(inp["positions"], np.int32)[b0:b0 + nb]
    return {
        "x": np.ascontiguousarray(x.reshape(nb, NT, 128, D)),
        "mem": np.ascontiguousarray(mem.reshape(nb, 2, 128, D)),
        "pos": np.ascontiguousarray(pos.reshape(nb, NT, 128).transpose(0, 2, 1)),
    }


_PROG = {}


def kernel(**inputs):
    nb = 16 // N_CORES
    key = (nb, (0, 1))
    if key not in _PROG:
        _PROG[key] = build_program(NB=nb, layers=(0, 1))
    nc = _PROG[key]
    sh = prep_shared(inputs)
    in_maps = []
    for c in range(N_CORES):
        m = dict(sh)
        m.update(prep_core(inputs, c * nb, nb))
        in_maps.append(m)
    res = run_bass_kernel_spmd(nc, in_maps, core_ids=list(range(N_CORES)))
    outs = [np.asarray(r["out"], np.float32).reshape(nb, S_LEN, D) for r in res.results]
    return np.concatenate(outs, axis=0)
```

```python
import math
from contextlib import ExitStack

import numpy as np
import concourse.bass as bass
import concourse.mybir as mybir
from concourse.bass_utils import run_bass_kernel_spmd

F32 = mybir.dt.float32
BF16 = mybir.dt.bfloat16
I32 = mybir.dt.int32
AF = mybir.ActivationFunctionType
ALU = mybir.AluOpType

N_CORES = 8
D = 1024
S_LEN = 2048
NT = 16
DEPTH = 2
D_IN = 2720
D_FF = 2816
NFC = 22
MEM_LEN = 256
EPS = 1e-6
MASK_W = 2944
MASK_C0 = 1408


class Buf:
    __slots__ = ("name", "w", "r")

    def __init__(self, name):
        self.name = name
        self.w = None
        self.r = []


class Op:
    __slots__ = ("eng", "idx", "fn", "deps", "waits", "signal", "semval", "dma", "dsem")

    def __init__(self, eng, fn):
        self.eng = eng
        self.fn = fn
        self.deps = []
        self.waits = []
        self.signal = False
        self.semval = 0
        self.dma = False
        self.dsem = None


class _Rec:
    def __init__(self):
        self.calls = []

    def __getattr__(self, name):
        def f(*a, **k):
            self.calls.append((name, a, k))
            return self
        return f


class Sched:
    ENGS = ("pe", "act", "dve", "pool", "sp")

    def __init__(self):
        self.ops = []
        self.streams = {e: [] for e in self.ENGS}
        self.bufs = {}
        self.dma_groups = {}

    def B(self, *key):
        b = self.bufs.get(key)
        if b is None:
            b = self.bufs[key] = Buf(key)
        return b

    def op(self, eng, fn, reads=(), writes=(), dma_key=None):
        rec = _Rec()
        fn(rec)
        o = Op(eng, rec.calls)
        deps = {}
        rawset = set()
        writes = list(writes) + [b for b in reads if b.name[0] == "ps" and b not in writes]
        for b in reads:
            if b.w is not None:
                deps[id(b.w)] = b.w
                rawset.add(id(b.w))
        for b in writes:
            for r in b.r:
                deps[id(r)] = r
            if b.w is not None:
                deps[id(b.w)] = b.w
        for d in deps.values():
            if d is o:
                continue
            if d.dma:
                if dma_key is not None and d.dsem == dma_key:
                    continue
                o.deps.append(d)
            else:
                if dma_key is None and d.eng == eng and eng == "pe":
                    continue
                o.deps.append(d)
        for b in reads:
            b.r.append(o)
        for b in writes:
            b.w = o
            b.r = []
        if dma_key is not None:
            o.dma = True
            o.dsem = dma_key
            self.dma_groups.setdefault(dma_key, []).append(o)
        o.idx = len(self.streams[eng])
        self.streams[eng].append(o)
        self.ops.append(o)
        return o

    def join(self, bufs, last_op):
        for b in bufs:
            if b.w is not None and b.w.dma and b.w.dsem == last_op.dsem:
                b.w = last_op

    def alias(self, new_bufs, old_bufs):
        latest = {}
        dmas = []
        for b in old_bufs:
            for o in list(b.r) + ([b.w] if b.w is not None else []):
                if o.dma:
                    dmas.append(o)
                else:
                    cur = latest.get(o.eng)
                    if cur is None or cur.idx < o.idx:
                        latest[o.eng] = o
        ops = list(latest.values()) + dmas
        for b in new_bufs:
            b.r = list(b.r) + ops

    def finalize(self):
        for k, lst in self.dma_groups.items():
            for i, o in enumerate(lst):
                o.semval = 16 * (i + 1)
        known = {e: {} for e in self.ENGS}
        for o in self.ops:
            kn = known[o.eng]
            need = {}
            for d in o.deps:
                src = ("dma", d.dsem) if d.dma else d.eng
                pos = d.semval if d.dma else d.idx
                if kn.get(src, -1) >= pos:
                    continue
                if src not in need or need[src][0] < pos:
                    need[src] = (pos, d)
            for src, (pos, d) in need.items():
                kn[src] = pos
                if not d.dma:
                    d.signal = True
                o.waits.append(d)
        for e in self.ENGS:
            c = 0
            for o in self.streams[e]:
                if o.dma:
                    continue
                if o.signal:
                    c += 1
                    o.semval = c

    def emit_stream(self, engname, engobj, sems, dma_sems):
        for o in self.streams[engname]:
            for d in o.waits:
                if d.dma:
                    engobj.wait_ge(dma_sems[d.dsem], d.semval)
                else:
                    engobj.wait_ge(sems[d.eng], d.semval)
            ins = None
            for (name, a_, k_) in o.fn:
                ins = getattr(engobj, name)(*a_, **k_)
            if ins is None:
                continue
            if o.dma:
                ins.then_inc(dma_sems[o.dsem], 16)
            elif o.signal:
                ins.then_inc(sems[o.eng], 1)


SUB = 99
N_WARM = 0


class _Stop(Exception):
    pass


def build_program(NB=2, layers=(0, 1), taps=(), upto=99, wdepth=DEPTH):
    nc = bass.Bass("TRN2", target_bir_lowering=False)
    DT = {}

    def din(name, shape, dt=F32):
        DT[name] = nc.dram_tensor(name, list(shape), dt, kind="ExternalInput").ap()

    def dout(name, shape, dt=F32):
        DT[name] = nc.dram_tensor(name, list(shape), dt, kind="ExternalOutput").ap()

    din("x", [NB, NT, 128, D])
    din("mem", [NB, 2, 128, D])
    din("pos", [NB, 128, NT], I32)
    din("ident", [128, 128])
    din("maskb", [128, MASK_W])
    din("invf", [128, 24])
    din("gT", [128, DEPTH * 7 * 8])
    din("grow", [DEPTH * 7, D])
    din("w_in", [wdepth, 128, 8, D_IN])
    din("w_out", [wdepth, 128, 8, D])
    din("lam", [DEPTH, 256])
    din("subln", [128, DEPTH])
    din("qng", [128, DEPTH * 2])
    din("kvng", [128, DEPTH])
    din("wq_up", [wdepth, 128, 2, 384])
    din("wkv_up", [wdepth, 128, 512])
    din("w_mq", [wdepth, 128, 8, 256])
    din("w_mkv", [wdepth, 128, 8, 512])
    din("w_mo", [wdepth, 128, 2, D])
    din("w_g", [wdepth, NFC, 128, 8, 128])
    din("w_u", [wdepth, NFC, 128, 8, 128])
    din("w_d", [wdepth, 128, NFC, D])
    dout("out", [NB, NT, 128, D])
    for name, shape, dt in taps:
        dout(name, shape, dt)

    S = Sched()
    B = S.B
    es = ExitStack()
    with es:
        def sb(name, shape, dt):
            return es.enter_context(nc.sbuf_tensor(name, list(shape), dt))


        xs = sb("xs", [128, NT, D], F32)
        ident = sb("ident_s", [128, 128], BF16)
        ones = sb("ones_s", [128, 128], BF16)
        maskb = sb("maskb_s", [128, MASK_W], BF16)
        invf = sb("invf_s", [128, 24], F32)
        epsT = sb("eps_s", [128, 1], F32)
        subln = sb("subln_s", [128, DEPTH], F32)
        qng = sb("qng_s", [128, DEPTH * 2], F32)
        kvng = sb("kvng_s", [128, DEPTH], F32)
        sublnS = sb("sublnS", [128, DEPTH], F32)
        lamraw = sb("lamraw", [128, 256], F32)
        lamt = sb("lamt", [128, 8], F32)
        neglam = sb("neglam", [128, DEPTH], F32)
        posi = sb("posi", [128, NT], I32)
        posf = sb("posf", [128, NT], F32)
        angP = sb("angP", [128, NT, 8], F32)
        angC = sb("angC", [128, NT, 16], F32)
        cosP = sb("cosP", [128, NT, 8], F32)
        sinP = sb("sinP", [128, NT, 8], F32)
        cosC = sb("cosC", [128, NT, 16], F32)
        sinC = sb("sinC", [128, NT, 16], F32)
        gpre = sb("gpre", [128, D], F32)
        gb = sb("gb", [128, D], F32)
        junk = sb("junk", [128, D], BF16)
        hn = [sb("hn%d" % i, [128, D], BF16) for i in range(2)]
        hTt = [sb("hTt%d" % i, [128, 8, 128], BF16) for i in range(2)]
        ssq = sb("ssq", [128, 64], F32)
        rstd = sb("rstd", [128, 64], F32)
        rcache = sb("rcache", [128, NT], F32)
        PT = [sb("PT%d" % i, [128, 1024], BF16) for i in range(3)]
        stA = [sb("stA%d" % i, [128, 512], BF16) for i in range(2)]
        ropeT = [sb("ropeT%d" % i, [128, 128], BF16) for i in range(2)]
        ropeU = [sb("ropeU%d" % i, [128, 128], BF16) for i in range(2)]
        stC = [sb("stC%d" % i, [128, 384], BF16) for i in range(2)]
        stKR = [sb("stKR%d" % i, [128, 32], BF16) for i in range(4)]
        stQC = [sb("stQC%d" % i, [128, 192], BF16) for i in range(2)]
        stKC = [sb("stKC%d" % i, [128, 2, 96], BF16) for i in range(2)]
        cT = [sb("cT%d" % i, [128, 3, 128], BF16) for i in range(2)]
        finAall = sb("finAall", [128, 4, 512], F32)
        finA = [finAall[:, i, :] for i in range(4)]
        finB = sb("finB", [128, 512], BF16)
        sg = [sb("sg%d" % i, [128, 512], BF16) for i in range(2)]
        ARENA = 45056
        arena = sb("arena", [128, ARENA], BF16)
        psum = es.enter_context(nc.psum_tensor("psum", [128, 4096], F32))

        def pbank(b0, nb=1):
            return psum[:, b0 * 512:(b0 + nb) * 512]

        def pbank_bf(b0):
            return psum[:, b0 * 512:(b0 + 1) * 512].bitcast(BF16)

        PB = lambda i: B("ps", i)

        O0, Q0, W0 = 0, 16384, 32768
        OT = arena[:, O0:O0 + 16384].rearrange("p (c t) -> p c t", c=8)
        w_pass = arena[:, W0:W0 + 6144].rearrange("p (c n) -> p c n", c=8)
        wq_up_s = arena[:, W0 + 6144:W0 + 6144 + 768].rearrange("p (c n) -> p c n", c=2)
        wkv_up_s = arena[:, W0 + 6912:W0 + 6912 + 512]
        w_out_s = arena[:, W0:W0 + 8192].rearrange("p (c n) -> p c n", c=8)
        QAT = arena[:, Q0:Q0 + 4096].rearrange("p (h t) -> p h t", h=2)
        KAT = arena[:, Q0 + 4096:Q0 + 8192].rearrange("p (h t) -> p h t", h=2)
        VA = arena[:, Q0 + 8192:Q0 + 12288].rearrange("p (t n) -> p t n", t=NT)
        QBT = arena[:, Q0:Q0 + 4096].rearrange("p (j t) -> p j t", j=2)
        KBT = arena[:, Q0 + 4096:Q0 + 8192].rearrange("p (j t) -> p j t", j=2)
        VB = arena[:, Q0 + 8192:Q0 + 16384].rearrange("p (t h n) -> p t h n", t=NT, h=4)
        QCT = arena[:, Q0:Q0 + 4096].rearrange("p (h t) -> p h t", h=2)
        KCT = arena[:, Q0 + 4096:Q0 + 8192].rearrange("p (h t) -> p h t", h=2)
        VC = arena[:, Q0 + 8192:Q0 + 12288].rearrange("p (t h n) -> p t h n", t=NT, h=2)
        Pacc = arena[:, W0 + 8192:W0 + 12288].bitcast(F32).rearrange("p (a n) -> p a n", a=2)
        QmT = arena[:, Q0:Q0 + 4096].rearrange("p (j t) -> p j t", j=2)
        OmT = arena[:, Q0 + 4096:Q0 + 8192].rearrange("p (j t) -> p j t", j=2)
        memT = arena[:, Q0 + 8192:Q0 + 10240].rearrange("p (c t) -> p c t", c=8)
        KmT = arena[:, Q0 + 10240:Q0 + 10752].rearrange("p (j t) -> p j t", j=2)
        Vm = arena[:, Q0 + 10752:Q0 + 11776].rearrange("p (t h n) -> p t h n", t=2, h=4)
        w_mq_s = arena[:, Q0 + 11776:Q0 + 13824].rearrange("p (c n) -> p c n", c=8)
        w_mo_s = arena[:, Q0 + 13824:Q0 + 15872].rearrange("p (c n) -> p c n", c=2)
        w_mkv_s = arena[:, W0:W0 + 4096].rearrange("p (c n) -> p c n", c=8)
        memf = arena[:, W0 + 4096:W0 + 8192].bitcast(F32).rearrange("p (t n) -> p t n", t=2)
        hTq = arena[:, O0:O0 + 4096].rearrange("p (c t) -> p c t", c=8)
        HT = arena[:, O0 + 4096:O0 + 4096 + 11264].rearrange("p (f t) -> p f t", f=NFC)
        w_d_s = arena[:, Q0:Q0 + 22528].rearrange("p (f n) -> p f n", f=NFC)
        NSLOT = 3
        SL0 = Q0 + 22528
        wgu = [arena[:, SL0 + i * 2048:SL0 + (i + 1) * 2048].rearrange("p (g c n) -> p g c n", g=2, c=8) for i in range(NSLOT)]
        assert SL0 + NSLOT * 2048 <= ARENA

        cnt = {"ss": 0}

        def stat_col():
            cnt["ss"] = (cnt["ss"] + 1) % 64
            return cnt["ss"]

        def dma_cast(dst_ap, src_ap, wbufs, key):
            return S.op("pool", lambda e: e.dma_start(out=dst_ap, in_=src_ap), writes=list(wbufs), dma_key=key)

        def dma_sp(dst_ap, src_ap, wbufs, key, reads=()):
            return S.op("sp", lambda e: e.dma_start(out=dst_ap, in_=src_ap), reads=list(reads), writes=list(wbufs), dma_key=key)

        def rstd_to(dst_ap, dst_buf, src_ap, src_bufs, n):
            S.op("act", lambda e: e.activation(out=dst_ap, in_=src_ap, func=AF.Ln, scale=1.0 / n, bias=epsT[:, 0:1]),
                 reads=list(src_bufs) + [B("eps")], writes=[dst_buf])
            S.op("act", lambda e: e.activation(out=dst_ap, in_=dst_ap, func=AF.Exp, scale=-0.5), reads=[dst_buf], writes=[dst_buf])

        dma_cast(ident[:], DT["ident"][:, :], [B("ident")], "c_ident")
        dma_cast(maskb[:], DT["maskb"][:, :], [B("maskb")], "c_maskb")
        dma_sp(invf[:], DT["invf"][:, :], [B("invf")], "c_invf")
        dma_sp(subln[:], DT["subln"][:, :], [B("subln")], "c_subln")
        dma_sp(qng[:], DT["qng"][:, :], [B("qng")], "c_qng")
        dma_sp(kvng[:], DT["kvng"][:, :], [B("kvng")], "c_kvng")
        S.op("dve", lambda e: e.memset(ones[:], 1.0), writes=[B("ones")])
        S.op("dve", lambda e: e.memset(epsT[:], EPS), writes=[B("eps")])
        for l in range(DEPTH):
            lam_init = 0.8 - 0.6 * math.exp(-0.3 * l)
            dma_sp(lamraw[:], DT["lam"][l:l + 1, :].partition_broadcast(128), [B("lamraw")], ("c_lam", l))
            S.op("dve", lambda e: e.tensor_tensor(out=lamraw[:, 0:64], in0=lamraw[:, 0:64], in1=lamraw[:, 64:128], op=ALU.mult), reads=[B("lamraw")], writes=[B("lamraw")])
            S.op("dve", lambda e: e.tensor_tensor(out=lamraw[:, 128:192], in0=lamraw[:, 128:192], in1=lamraw[:, 192:256], op=ALU.mult), reads=[B("lamraw")], writes=[B("lamraw")])
            S.op("dve", lambda e: e.tensor_reduce(out=lamt[:, 0:1], in_=lamraw[:, 0:64], axis=mybir.AxisListType.X, op=ALU.add), reads=[B("lamraw")], writes=[B("lamt")])
            S.op("dve", lambda e: e.tensor_reduce(out=lamt[:, 1:2], in_=lamraw[:, 128:192], axis=mybir.AxisListType.X, op=ALU.add), reads=[B("lamraw"), B("lamt")], writes=[B("lamt")])
            S.op("act", lambda e: e.activation(out=lamt[:, 2:4], in_=lamt[:, 0:2], func=AF.Exp), reads=[B("lamt")], writes=[B("lamt")])
            S.op("dve", lambda e, l=l, li=lam_init: e.scalar_tensor_tensor(out=neglam[:, l:l + 1], in0=lamt[:, 3:4], scalar=-li, in1=lamt[:, 2:3], op0=ALU.add, op1=ALU.subtract),
                 reads=[B("lamt")], writes=[B("neglam", l)])
            S.op("dve", lambda e, l=l, li=lam_init: e.tensor_scalar(out=sublnS[:, l:l + 1], in0=subln[:, l:l + 1], scalar1=1.0 - li, scalar2=None, op0=ALU.mult),
                 reads=[B("subln")], writes=[B("sublnS", l)])

        angK_all = finA[0][:, 0:256].rearrange("p (t w) -> p t w", t=NT)
        angI_all = finA[1][:, 0:256].bitcast(I32).rearrange("p (t w) -> p t w", t=NT)
        angT_all = finA[2][:, 0:256].rearrange("p (t w) -> p t w", t=NT)

        def range_reduce_sin(dst, ang, width, bname):
            BA = B(*bname) if isinstance(bname, tuple) else B(bname)
            k = angK_all[:, :, 0:width]
            ki = angI_all[:, :, 0:width]
            bk, bi = B("finA", 0), B("finA", 1)
            S.op("dve", lambda e: e.tensor_scalar(out=k, in0=ang, scalar1=1.0 / (2 * math.pi), scalar2=None, op0=ALU.mult), reads=[BA], writes=[bk])
            S.op("dve", lambda e: e.tensor_copy(out=ki, in_=k), reads=[bk], writes=[bi])
            S.op("dve", lambda e: e.tensor_copy(out=k, in_=ki), reads=[bi], writes=[bk])
            S.op("dve", lambda e: e.scalar_tensor_tensor(out=ang, in0=k, scalar=-2 * math.pi, in1=ang, op0=ALU.mult, op1=ALU.add), reads=[bk, BA], writes=[BA])
            S.op("dve", lambda e: e.tensor_scalar(out=k, in0=ang, scalar1=math.pi, scalar2=-2 * math.pi, op0=ALU.is_gt, op1=ALU.mult), reads=[BA], writes=[bk])
            S.op("dve", lambda e: e.tensor_tensor(out=ang, in0=ang, in1=k, op=ALU.add), reads=[bk, BA], writes=[BA])
            S.op("dve", lambda e: e.tensor_scalar(out=k, in0=ang, scalar1=-math.pi, scalar2=2 * math.pi, op0=ALU.is_lt, op1=ALU.mult), reads=[BA], writes=[bk])
            S.op("dve", lambda e: e.tensor_tensor(out=ang, in0=ang, in1=k, op=ALU.add), reads=[bk, BA], writes=[BA])
            S.op("act", lambda e: e.activation(out=dst, in_=ang, func=AF.Sin), reads=[BA], writes=[B("ropetab")])

        def norm_tile(t, first, dst_ap, dst_buf, par):
            xap = xs[:, t, :]
            if first:
                col = stat_col()
                S.op("act", lambda e: e.activation(out=junk[:], in_=xap, func=AF.Square, accum_out=ssq[:, col:col + 1]),
                     reads=[B("x", t)], writes=[B("junk"), B("ssq", col)])
                rstd_to(rcache[:, t:t + 1], B("rcache", t), ssq[:, col:col + 1], [B("ssq", col)], D)
            hb = hn[par]
            S.op("dve", lambda e: e.scalar_tensor_tensor(out=hb[:], in0=xap, scalar=rcache[:, t:t + 1], in1=gpre[:], op0=ALU.mult, op1=ALU.mult),
                 reads=[B("x", t), B("rcache", t), B("gpre")], writes=[B("hn", par)])
            pv = pbank_bf(7)
            for c in range(8):
                S.op("pe", lambda e, c=c: e.transpose(pv[:, c * 128:(c + 1) * 128], hb[:, c * 128:(c + 1) * 128], ident[:, :]),
                     reads=[B("hn", par), B("ident")], writes=[PB(7)])
            if t % 2 == 0:
                S.op("act", lambda e: e.activation(out=dst_ap, in_=pv[:, :].rearrange("p (c k) -> p c k", c=8), func=AF.Copy), reads=[PB(7)], writes=[dst_buf])
            else:
                S.op("dve", lambda e: e.tensor_copy(out=dst_ap, in_=pv[:, :].rearrange("p (c k) -> p c k", c=8)), reads=[PB(7)], writes=[dst_buf])

        def load_grow(dst, bufname, l, n):
            r = l * 7 + n
            dma_sp(dst[:], DT["grow"][r:r + 1, :].partition_broadcast(128), [B(bufname)], bufname)

        def post_norm_residual(ybank, t):
            yap = pbank(ybank, 2)
            col = stat_col()
            tmp = tmp_pn[t % 2]
            tb = [B("finA", 2), B("finA", 3)] if t % 2 == 0 else [B("finA", 0), B("finA", 1)]
            S.op("act", lambda e: e.activation(out=junk[:], in_=yap, func=AF.Square, accum_out=ssq[:, col:col + 1]),
                 reads=[PB(ybank), PB(ybank + 1)], writes=[B("junk"), B("ssq", col)])
            rstd_to(rstd[:, col:col + 1], B("rstd", col), ssq[:, col:col + 1], [B("ssq", col)], D)
            S.op("dve", lambda e: e.scalar_tensor_tensor(out=tmp, in0=yap, scalar=rstd[:, col:col + 1], in1=gb[:], op0=ALU.mult, op1=ALU.mult),
                 reads=[PB(ybank), PB(ybank + 1), B("rstd", col), B("gb")], writes=tb)
            S.op("pool", lambda e: e.tensor_tensor(out=xs[:, t, :], in0=xs[:, t, :], in1=tmp, op=ALU.add),
                 reads=[B("x", t)] + tb, writes=[B("x", t)])

        tmp_pn = [finAall[:, 2:4, :].rearrange("p a b -> p (a b)"), finAall[:, 0:2, :].rearrange("p a b -> p (a b)")]

        def rope_inplace(v, ngroups, roff, half, cos_t, sin_t, t, bufs, par):
            x1 = v[:, 0:ngroups, roff:roff + half]
            x2 = v[:, 0:ngroups, roff + half:roff + 2 * half]
            c = cos_t[:, t:t + 1, :].to_broadcast([128, ngroups, half])
            s = sin_t[:, t:t + 1, :].to_broadcast([128, ngroups, half])
            T = ropeT[par][:, 0:ngroups * 2 * half].rearrange("p (g d) -> p g d", g=ngroups)
            U = ropeU[par][:, 0:ngroups * 2 * half].rearrange("p (g d) -> p g d", g=ngroups)
            ta = T[:, 0:ngroups, 0:half]
            tb = T[:, 0:ngroups, half:2 * half]
            ua = U[:, 0:ngroups, 0:half]
            ub = U[:, 0:ngroups, half:2 * half]
            rb = [B("ropeT", par), B("ropeU", par)]
            S.op("pool", lambda e: e.tensor_tensor(out=ta, in0=x1, in1=c, op=ALU.mult), reads=bufs + [B("ropetab")], writes=[rb[0]])
            S.op("pool", lambda e: e.tensor_tensor(out=tb, in0=x2, in1=c, op=ALU.mult), reads=bufs + [B("ropetab"), rb[0]], writes=[rb[0]])
            S.op("pool", lambda e: e.tensor_tensor(out=ua, in0=x2, in1=s, op=ALU.mult), reads=bufs + [B("ropetab")], writes=[rb[1]])
            S.op("pool", lambda e: e.tensor_tensor(out=ub, in0=x1, in1=s, op=ALU.mult), reads=bufs + [B("ropetab"), rb[1]], writes=[rb[1]])
            S.op("pool", lambda e: e.tensor_tensor(out=x1, in0=ta, in1=ua, op=ALU.subtract), reads=rb + bufs, writes=bufs)
            S.op("pool", lambda e: e.tensor_tensor(out=x2, in0=tb, in1=ub, op=ALU.add), reads=rb + bufs, writes=bufs)

        def attention_pairs(npairs, q_ap, k_ap, v_ap, qbufs, kbuf, vbuf, scale, ot_dst, ot_buf, masked, kt_list_fn, part_of):
            acc = (0, 1)
            sbanks = [(2, 3), (4, 5), (6, 7)]
            blocks = [(pj, qc) for pj in range(npairs) for qc in range(4)]

            def emit_scores(pj, qc, i, kt):
                sb_ = sbanks[i % 3]
                for hh in range(2):
                    S.op("pe", lambda e, hh=hh: e.matmul(pbank(sb_[hh]), k_ap(pj, hh, kt), q_ap(pj, hh, qc), start=True, stop=True),
                         reads=[kbuf(kt)] + qbufs(qc), writes=[PB(sb_[hh])])

            def emit_exp(qc, i, kt):
                sb_ = sbanks[i % 3]
                pt = PT[i % 3]
                S.op("act", lambda e: e.activation(out=pt[:], in_=pbank(sb_[0], 2), func=AF.Exp, scale=scale),
                     reads=[PB(sb_[0]), PB(sb_[1])], writes=[B("PT", i % 3)])
                if masked:
                    c0 = qc * 512 - kt * 128 + MASK_C0
                    m = maskb[:, c0:c0 + 512].unsqueeze(1).to_broadcast([128, 2, 512])
                    ptv = pt[:].rearrange("p (a q) -> p a q", a=2)
                    S.op("dve", lambda e: e.tensor_tensor(out=ptv, in0=ptv, in1=m, op=ALU.mult),
                         reads=[B("PT", i % 3), B("maskb")], writes=[B("PT", i % 3)])

            def emit_pv(pj, i, kt, n):
                pt = PT[i % 3]
                for hh in range(2):
                    S.op("pe", lambda e, hh=hh: e.matmul(pbank(acc[hh]), v_ap(pj, hh, kt), pt[:, hh * 512:(hh + 1) * 512], start=(i == 0), stop=(i == n - 1)),
                         reads=[vbuf(kt), B("PT", i % 3)], writes=[PB(acc[hh])])

            def finalize(pj, qc):
                for hh in range(2):
                    i_ = part_of(pj, hh)
                    lo, hi = i_ * 64, (i_ + 1) * 64
                    dlo, dhi = (1 - i_) * 64, (2 - i_) * 64
                    rd = finA[hh]
                    S.op("act", lambda e: e.activation(out=rd[lo:hi, :], in_=pbank(acc[hh])[dlo:dhi, :], func=AF.Ln),
                         reads=[PB(acc[hh])], writes=[B("finA", hh)])
                    S.op("act", lambda e: e.activation(out=rd[lo:hi, :], in_=rd[lo:hi, :], func=AF.Exp, scale=-1.0), reads=[B("finA", hh)], writes=[B("finA", hh)])
                    dst = ot_dst(pj, hh, qc)
                    S.op("dve", lambda e: e.tensor_tensor(out=dst, in0=pbank(acc[hh])[lo:hi, :], in1=rd[lo:hi, :], op=ALU.mult),
                         reads=[PB(acc[hh]), B("finA", hh)], writes=[ot_buf(pj, hh, qc)])

            def prologue(pj, qc):
                kts = kt_list_fn(qc)
                emit_scores(pj, qc, 0, kts[0])
                if len(kts) > 1:
                    emit_scores(pj, qc, 1, kts[1])

            prologue(*blocks[0])
            for bi, (pj, qc) in enumerate(blocks):
                kts = kt_list_fn(qc)
                n = len(kts)
                for i in range(n):
                    if i + 2 < n:
                        emit_scores(pj, qc, i + 2, kts[i + 2])
                    emit_exp(qc, i, kts[i])
                    emit_pv(pj, i, kts[i], n)
                if bi + 1 < len(blocks):
                    prologue(*blocks[bi + 1])
                finalize(pj, qc)

        def proj_pass(l, col_ranges, first, stages):
            ncols = sum(c1 - c0 for c0, c1 in col_ranges)
            for c in range(8):
                o = 0
                for (c0, c1) in col_ranges:
                    lastw = dma_cast(w_pass[:, c, o:o + (c1 - c0)], DT["w_in"][l, :, c, c0:c1], [B("w_pass", c)], "w_pass")
                    o += c1 - c0
            S.join([B("w_pass", c) for c in range(8)], lastw)
            nb = (ncols + 511) // 512
            norm_tile(0, first, hTt[0][:, :, :], B("hTt", 0), 0)
            for i in range(NT + len(stages) - 1):
                if i + 1 < NT:
                    p1 = (i + 1) % 2
                    norm_tile(i + 1, first, hTt[p1][:, :, :], B("hTt", p1), p1)
                if i < NT:
                    par = i % 2
                    base = par * 2
                    for c in range(8):
                        for n in range(nb):
                            w = min(512, ncols - n * 512)
                            S.op("pe", lambda e, c=c, n=n, w=w, par=par: e.matmul(pbank(base + n)[:, 0:w], hTt[par][:, c, :], w_pass[:, c, n * 512:n * 512 + w], start=(c == 0), stop=(c == 7)),
                                 reads=[B("hTt", par), B("w_pass", c)], writes=[PB(base + n)])
                    for _ in range(N_WARM):
                        S.op("pe", lambda e: e.matmul(pbank(6), ident[:, :], w_pass[:, 0, 0:512], start=True, stop=True),
                             reads=[B("ident"), B("w_pass", 0)], writes=[PB(6)])
                for k, fn in enumerate(stages):
                    t = i - k
                    if 0 <= t < NT:
                        fn(t, t % 2, (t % 2) * 2)

        for b in range(NB):
            if b == 0:
                for t in range(NT):
                    dma_sp(xs[:, t, :], DT["x"][b, t, :, :], [B("x", t)], ("x", t))
            dma_sp(posi[:], DT["pos"][b, :, :], [B("posi")], "posi")
            S.op("dve", lambda e: e.tensor_copy(out=posf[:], in_=posi[:]), reads=[B("posi")], writes=[B("posf")])
            S.op("dve", lambda e: e.tensor_tensor(out=angP[:], in0=posf[:].unsqueeze(2).to_broadcast([128, NT, 8]), in1=invf[:, 0:8].unsqueeze(1).to_broadcast([128, NT, 8]), op=ALU.mult),
                 reads=[B("posf"), B("invf")], writes=[B("angP")])
            S.op("dve", lambda e: e.tensor_tensor(out=angC[:], in0=posf[:].unsqueeze(2).to_broadcast([128, NT, 16]), in1=invf[:, 8:24].unsqueeze(1).to_broadcast([128, NT, 16]), op=ALU.mult),
                 reads=[B("posf"), B("invf")], writes=[B("angC")])
            aT8 = angT_all[:, :, 0:8]
            S.op("dve", lambda e: e.tensor_scalar(out=aT8, in0=angP[:], scalar1=math.pi / 2, scalar2=None, op0=ALU.add), reads=[B("angP")], writes=[B("finA", 2)])
            range_reduce_sin(cosP[:], aT8, 8, ("finA", 2))
            range_reduce_sin(sinP[:], angP[:], 8, "angP")
            S.op("dve", lambda e: e.tensor_scalar(out=angT_all[:], in0=angC[:], scalar1=math.pi / 2, scalar2=None, op0=ALU.add), reads=[B("angC")], writes=[B("finA", 2)])
            range_reduce_sin(cosC[:], angT_all[:], 16, ("finA", 2))
            range_reduce_sin(sinC[:], angC[:], 16, "angC")

            def stage(n):
                if n > upto:
                    raise _Stop()

            for l in layers:
              try:
                  stage(2)
                  load_grow(gpre, "gpre", l, 0)
                  scaleA = 64 ** -0.5
                  for j in range(2):
                      def tileA(t, par, base):
                          st = stA[par]
                          S.op("act", lambda e: e.activation(out=st[:], in_=pbank(base), func=AF.Copy), reads=[PB(base)], writes=[B("stA", par)])
                          S.op("dve", lambda e: e.tensor_copy(out=VA[:, t, :], in_=pbank(base + 1)[:, 0:256]), reads=[PB(base + 1)], writes=[B("VA", t)])
                          rope_inplace(st[:].rearrange("p (g d) -> p g d", d=64), 8, 0, 8, cosP, sinP, t, [B("stA", par)], par)

                      def tileA2(t, par, base):
                          st = stA[par]
                          bank = 4 + par
                          pv = pbank_bf(bank)
                          for k4 in range(4):
                              S.op("pe", lambda e, k4=k4: e.transpose(pv[:, k4 * 128:(k4 + 1) * 128], st[:, k4 * 128:(k4 + 1) * 128], ident[:, :]),
                                   reads=[B("stA", par), B("ident")], writes=[PB(bank)])
                          S.op("act", lambda e: e.activation(out=QAT[:, :, t * 128:(t + 1) * 128], in_=pv[:, 0:256].rearrange("p (h k) -> p h k", h=2), func=AF.Copy),
                               reads=[PB(bank)], writes=[B("QT", t)])
                          S.op("dve", lambda e: e.tensor_copy(out=KAT[:, :, t * 128:(t + 1) * 128], in_=pv[:, 256:512].rearrange("p (h k) -> p h k", h=2)),
                               reads=[PB(bank)], writes=[B("KT", t)])
                      proj_pass(l, [(j * 256, (j + 1) * 256), (512 + j * 256, 512 + (j + 1) * 256), (1024 + j * 256, 1024 + (j + 1) * 256)], (j == 0), [tileA, tileA2])
                      stage(3 if j == 0 else 4)
                      sbanksA = [(2, 3), (4, 5), (6, 7)]
                      itersA = [(hh, qc) for hh in range(2) for qc in range(4)]

                      def a_scores(hh, qc, kt):
                          sb_ = sbanksA[kt % 3]
                          for m in range(2):
                              S.op("pe", lambda e, m=m: e.matmul(pbank(sb_[m]), KAT[m * 64:(m + 1) * 64, hh, kt * 128:(kt + 1) * 128], QAT[m * 64:(m + 1) * 64, hh, qc * 512:(qc + 1) * 512], start=True, stop=True),
                                   reads=[B("KT", kt)] + [B("QT", qc * 4 + i) for i in range(4)], writes=[PB(sb_[m])])

                      def a_exp(kt, ai):
                          sb_ = sbanksA[kt % 3]
                          pt = PT[kt % 3]
                          S.op("act", lambda e: e.activation(out=pt[:], in_=pbank(sb_[0], 2), func=AF.Exp, scale=scaleA),
                               reads=[PB(sb_[0]), PB(sb_[1])], writes=[B("PT", kt % 3)])
                          pa = Pacc[:, ai, :]
                          if kt == 0:
                              S.op("dve", lambda e: e.tensor_copy(out=pa, in_=pt[:]), reads=[B("PT", kt % 3)], writes=[B("Pacc", ai, 0), B("Pacc", ai, 1)])
                          else:
                              S.op("dve", lambda e: e.tensor_tensor(out=pa, in0=pa, in1=pt[:], op=ALU.add), reads=[B("PT", kt % 3), B("Pacc", ai, 0), B("Pacc", ai, 1)], writes=[B("Pacc", ai, 0), B("Pacc", ai, 1)])

                      def a_pv(hh, kt):
                          pt = PT[kt % 3]
                          for m in range(2):
                              S.op("pe", lambda e, m=m: e.matmul(pbank(m), VA[:, kt, hh * 128:(hh + 1) * 128], pt[:, m * 512:(m + 1) * 512], start=(kt == 0), stop=(kt == NT - 1)),
                                   reads=[B("VA", kt), B("PT", kt % 3)], writes=[PB(m)])

                      def a_finalize(hh, qc, ai):
                          h = 2 * j + hh
                          rD0, rD1, oo, lnss = finA
                          S.op("dve", lambda e: e.tensor_copy(out=junk[:], in_=Pacc[:, ai, :]), reads=[B("Pacc", ai, 0), B("Pacc", ai, 1)], writes=[B("junk")])
                          for m in range(2):
                              S.op("pe", lambda e, m=m: e.matmul(pbank(6 + m), ones[:, :], junk[:, m * 512:(m + 1) * 512], start=True, stop=True), reads=[B("ones"), B("junk")], writes=[PB(6 + m)])
                          S.op("act", lambda e: e.activation(out=rD0[:], in_=pbank(6), func=AF.Ln), reads=[PB(6)], writes=[B("finA", 0)])
                          S.op("act", lambda e: e.activation(out=rD1[:], in_=pbank(7), func=AF.Ln), reads=[PB(7)], writes=[B("finA", 1)])
                          S.op("act", lambda e: e.activation(out=rD0[:], in_=rD0[:], func=AF.Exp, scale=-1.0), reads=[B("finA", 0)], writes=[B("finA", 0)])
                          S.op("act", lambda e: e.activation(out=rD1[:], in_=rD1[:], func=AF.Exp, scale=-1.0), reads=[B("finA", 1)], writes=[B("finA", 1)])
                          S.op("dve", lambda e: e.tensor_tensor(out=oo[:], in0=pbank(0), in1=rD0[:], op=ALU.mult), reads=[PB(0), B("finA", 0)], writes=[B("finA", 2)])
                          S.op("dve", lambda e: e.scalar_tensor_tensor(out=rD1[:], in0=pbank(1), scalar=neglam[:, l:l + 1], in1=rD1[:], op0=ALU.mult, op1=ALU.mult),
                               reads=[PB(1), B("finA", 1), B("neglam", l)], writes=[B("finA", 1)])
                          S.op("pool", lambda e: e.tensor_tensor(out=oo[:], in0=oo[:], in1=rD1[:], op=ALU.add), reads=[B("finA", 1), B("finA", 2)], writes=[B("finA", 2)])
                          S.op("act", lambda e: e.activation(out=finB[:], in_=oo[:], func=AF.Square), reads=[B("finA", 2)], writes=[B("finB")])
                          S.op("pe", lambda e: e.matmul(pbank(6), ones[:, :], finB[:, :], start=True, stop=True), reads=[B("ones"), B("finB")], writes=[PB(6)])
                          rstd_to(lnss[:], B("finA", 3), pbank(6), [PB(6)], 128)
                          S.op("dve", lambda e: e.scalar_tensor_tensor(out=OT[:, h, qc * 512:(qc + 1) * 512], in0=oo[:], scalar=sublnS[:, l:l + 1], in1=lnss[:], op0=ALU.mult, op1=ALU.mult),
                               reads=[B("finA", 2), B("finA", 3), B("sublnS", l)], writes=[B("OT", h, qc, 0), B("OT", h, qc, 1)])

                      a_scores(itersA[0][0], itersA[0][1], 0)
                      a_scores(itersA[0][0], itersA[0][1], 1)
                      for ai_, (hh, qc) in enumerate(itersA):
                          ai = ai_ % 2
                          for kt in range(NT):
                              if kt + 2 < NT:
                                  a_scores(hh, qc, kt + 2)
                              a_exp(kt, ai)
                              a_pv(hh, kt)
                          if ai_ + 1 < len(itersA):
                              a_scores(itersA[ai_ + 1][0], itersA[ai_ + 1][1], 0)
                              a_scores(itersA[ai_ + 1][0], itersA[ai_ + 1][1], 1)
                          a_finalize(hh, qc, ai)

                  stage(5)
                  S.alias([B("VBt", t) for t in range(NT)] + [B("Vones")], [B("VA", t) for t in range(NT)] + [B("Vones")])
                  for h in range(4):
                      i_ = h % 2
                      S.op("pool", lambda e, h=h, i_=i_: e.memset(VB[:, :, h, (1 - i_) * 64:(2 - i_) * 64], 1.0), writes=[B("Vones")])

                  def tileB(t, par, base):
                      st = stA[par]
                      S.op("act", lambda e: e.activation(out=st[:], in_=pbank(base), func=AF.Copy), reads=[PB(base)], writes=[B("stA", par)])
                      for h in range(4):
                          i_ = h % 2
                          S.op("dve", lambda e, h=h, i_=i_: e.tensor_copy(out=VB[:, t, h, i_ * 64:(i_ + 1) * 64], in_=pbank(base + 1)[:, h * 64:(h + 1) * 64]),
                               reads=[PB(base + 1), B("Vones")], writes=[B("VBt", t)])
                      rope_inplace(st[:].rearrange("p (g d) -> p g d", d=64), 8, 0, 8, cosP, sinP, t, [B("stA", par)], par)

                  def tileB2(t, par, base):
                      st = stA[par]
                      bank = 4 + par
                      pv = pbank_bf(bank)
                      for k4 in range(4):
                          S.op("pe", lambda e, k4=k4: e.transpose(pv[:, k4 * 128:(k4 + 1) * 128], st[:, k4 * 128:(k4 + 1) * 128], ident[:, :]),
                               reads=[B("stA", par), B("ident")], writes=[PB(bank)])
                      S.op("act", lambda e: e.activation(out=QBT[:, :, t * 128:(t + 1) * 128], in_=pv[:, 0:256].rearrange("p (h k) -> p h k", h=2), func=AF.Copy),
                           reads=[PB(bank)], writes=[B("QT", t)])
                      S.op("dve", lambda e: e.tensor_copy(out=KBT[:, :, t * 128:(t + 1) * 128], in_=pv[:, 256:512].rearrange("p (h k) -> p h k", h=2)),
                           reads=[PB(bank)], writes=[B("KT", t)])
                  proj_pass(l, [(1536, 2304)], False, [tileB, tileB2])

                  def kts_B(qc):
                      lo = max(0, (qc * 512 - 1024) // 128)
                      hi = min(NT - 1, (qc * 512 + 511 + 1024) // 128)
                      return list(range(lo, hi + 1))
                  attention_pairs(
                      2,
                      q_ap=lambda pj, hh, qc: QBT[hh * 64:(hh + 1) * 64, pj, qc * 512:(qc + 1) * 512],
                      k_ap=lambda pj, hh, kt: KBT[hh * 64:(hh + 1) * 64, pj, kt * 128:(kt + 1) * 128],
                      v_ap=lambda pj, hh, kt: VB[:, kt, 2 * pj + hh, :],
                      qbufs=lambda qc: [B("QT", qc * 4 + i) for i in range(4)], kbuf=lambda kt: B("KT", kt), vbuf=lambda kt: B("VBt", kt),
                      scale=64 ** -0.5,
                      ot_dst=lambda pj, hh, qc: OT[hh * 64:(hh + 1) * 64, 4 + pj, qc * 512:(qc + 1) * 512],
                      ot_buf=lambda pj, hh, qc: B("OT", 4 + pj, qc, hh), masked=True, kt_list_fn=kts_B, part_of=lambda pj, hh: hh)

                  stage(6)
                  for j in range(2):
                      dma_cast(wq_up_s[:], DT["wq_up"][l, :, :, :], [B("wq_up")], "wq_up")
                      dma_cast(wkv_up_s[:], DT["wkv_up"][l, :, :], [B("wkv_up")], "wkv_up")
                      for c in range(2):
                          S.op("pool", lambda e, c=c: e.tensor_scalar(out=wq_up_s[:, c, :], in0=wq_up_s[:, c, :], scalar1=qng[:, l * 2 + c:l * 2 + c + 1], scalar2=None, op0=ALU.mult),
                               reads=[B("wq_up"), B("qng")], writes=[B("wq_up")])
                      S.op("pool", lambda e: e.tensor_scalar(out=wkv_up_s[:], in0=wkv_up_s[:], scalar1=kvng[:, l:l + 1], scalar2=None, op0=ALU.mult),
                           reads=[B("wkv_up"), B("kvng")], writes=[B("wkv_up")])
                      S.alias([B("VA", t) for t in range(NT)] + [B("Vones")], [B("VBt", t) for t in range(NT)] + [B("VA", t) for t in range(NT)] + [B("Vones")])
                      for hh in range(2):
                          S.op("pool", lambda e, hh=hh: e.memset(VC[:, :, hh, (1 - hh) * 64:(2 - hh) * 64], 1.0), writes=[B("Vones")])

                      def tileC(t, par, base):
                          pc = pbank(base)
                          colq, colk = stat_col(), stat_col()
                          S.op("act", lambda e: e.activation(out=junk[:, 0:256], in_=pc[:, 0:256], func=AF.Square, accum_out=ssq[:, colq:colq + 1]),
                               reads=[PB(base)], writes=[B("junk"), B("ssq", colq)])
                          S.op("act", lambda e: e.activation(out=junk[:, 256:384], in_=pc[:, 256:384], func=AF.Square, accum_out=ssq[:, colk:colk + 1]),
                               reads=[PB(base)], writes=[B("junk"), B("ssq", colk)])
                          rstd_to(rstd[:, colq:colq + 1], B("rstd", colq), ssq[:, colq:colq + 1], [B("ssq", colq)], 256)
                          rstd_to(rstd[:, colk:colk + 1], B("rstd", colk), ssq[:, colk:colk + 1], [B("ssq", colk)], 128)
                          sc = stC[par]
                          S.op("dve", lambda e: e.tensor_scalar(out=sc[:, 0:256], in0=pc[:, 0:256], scalar1=rstd[:, colq:colq + 1], scalar2=None, op0=ALU.mult),
                               reads=[PB(base), B("rstd", colq)], writes=[B("stC", par)])
                          S.op("dve", lambda e: e.tensor_scalar(out=sc[:, 256:384], in0=pc[:, 256:384], scalar1=rstd[:, colk:colk + 1], scalar2=None, op0=ALU.mult),
                               reads=[PB(base), B("rstd", colk), B("stC", par)], writes=[B("stC", par)])
                          skr = stKR[t % 4]
                          S.op("dve", lambda e: e.tensor_copy(out=skr[:], in_=pc[:, 384:416]), reads=[PB(base)], writes=[B("stKR", t % 4)])
                          rope_inplace(skr[:].rearrange("p (g d) -> p g d", g=1), 1, 0, 16, cosC, sinC, t, [B("stKR", t % 4)], par)

                      def tileC2(t, par, base):
                          sc = stC[par]
                          bank = 4 + par
                          pv = pbank_bf(bank)
                          for k3 in range(3):
                              S.op("pe", lambda e, k3=k3: e.transpose(pv[:, k3 * 128:(k3 + 1) * 128], sc[:, k3 * 128:(k3 + 1) * 128], ident[:, :]),
                                   reads=[B("stC", par), B("ident")], writes=[PB(bank)])
                          ct = cT[par]
                          S.op("act", lambda e: e.activation(out=ct[:].rearrange("p a b -> p (a b)"), in_=pv[:, 0:384], func=AF.Copy), reads=[PB(bank)], writes=[B("cT", par)])

                      def tileC2b(t, par, base):
                          ct = cT[par]
                          skr = stKR[t % 4]
                          bu = base + 1
                          for c in range(2):
                              S.op("pe", lambda e, c=c: e.matmul(pbank(bu)[:, 0:192], ct[:, c, :], wq_up_s[:, c, j * 192:(j + 1) * 192], start=(c == 0), stop=(c == 1)),
                                   reads=[B("cT", par), B("wq_up")], writes=[PB(bu)])
                          S.op("pe", lambda e: e.matmul(pbank(bu)[:, 256:512], ct[:, 2, :], wkv_up_s[:, j * 256:(j + 1) * 256], start=True, stop=True),
                               reads=[B("cT", par), B("wkv_up")], writes=[PB(bu)])
                          sq = stQC[par]
                          S.op("act", lambda e: e.activation(out=sq[:], in_=pbank(bu)[:, 0:192], func=AF.Copy), reads=[PB(bu)], writes=[B("stQC", par)])
                          rope_inplace(sq[:].rearrange("p (g d) -> p g d", d=96), 2, 64, 16, cosC, sinC, t, [B("stQC", par)], par)
                          sk = stKC[par]
                          kvv = pbank(bu)[:, 256:512].rearrange("p (h n) -> p h n", h=2)
                          S.op("dve", lambda e: e.tensor_copy(out=sk[:, :, 0:64], in_=kvv[:, :, 0:64]), reads=[PB(bu)], writes=[B("stKC", par)])
                          S.op("pool", lambda e: e.tensor_copy(out=sk[:, :, 64:96], in_=skr[:].unsqueeze(1).to_broadcast([128, 2, 32])),
                               reads=[B("stKR", t % 4), B("stKC", par)], writes=[B("stKC", par)])
                          for hh in range(2):
                              S.op("dve", lambda e, hh=hh: e.tensor_copy(out=VC[:, t, hh, hh * 64:(hh + 1) * 64], in_=kvv[:, hh, 64:128]),
                                   reads=[PB(bu), B("Vones")], writes=[B("VA", t)])

                      def tileC3(t, par, base):
                          sq = stQC[par]
                          sk = stKC[par]
                          bank = 4 + par
                          pv = pbank_bf(bank)
                          for which in range(2):
                              off0 = 512 if which == 0 else 0
                              for hh in range(2):
                                  src = sq[:, hh * 96:(hh + 1) * 96] if which == 0 else sk[:, hh, :]
                                  S.op("pe", lambda e, off0=off0, hh=hh, src=src: e.transpose(pv[0:96, off0 + hh * 128:off0 + (hh + 1) * 128], src, ident[:, :]),
                                       reads=[B("stQC", par) if which == 0 else B("stKC", par), B("ident")], writes=[PB(bank)])
                              dstT = (QCT if which == 0 else KCT)[0:96, :, t * 128:(t + 1) * 128]
                              srcv = pv[0:96, off0:off0 + 256].rearrange("p (h k) -> p h k", h=2)
                              if which == 0:
                                  S.op("act", lambda e, dstT=dstT, srcv=srcv: e.activation(out=dstT, in_=srcv, func=AF.Copy), reads=[PB(bank)], writes=[B("QT", t)])
                              else:
                                  S.op("dve", lambda e, dstT=dstT, srcv=srcv: e.tensor_copy(out=dstT, in_=srcv), reads=[PB(bank)], writes=[B("KT", t)])
                      proj_pass(l, [(2304, 2720)], False, [tileC, tileC2, tileC2b, tileC3])
                      attention_pairs(
                          1,
                          q_ap=lambda pj, hh, qc: QCT[0:96, hh, qc * 512:(qc + 1) * 512],
                          k_ap=lambda pj, hh, kt: KCT[0:96, hh, kt * 128:(kt + 1) * 128],
                          v_ap=lambda pj, hh, kt: VC[:, kt, hh, :],
                          qbufs=lambda qc: [B("QT", qc * 4 + i) for i in range(4)], kbuf=lambda kt: B("KT", kt), vbuf=lambda kt: B("VA", kt),
                          scale=96 ** -0.5,
                          ot_dst=lambda pj, hh, qc, j=j: OT[hh * 64:(hh + 1) * 64, 6 + j, qc * 512:(qc + 1) * 512],
                          ot_buf=lambda pj, hh, qc, j=j: B("OT", 6 + j, qc, hh), masked=False, kt_list_fn=lambda qc: list(range(NT)), part_of=lambda pj, hh: hh)

                  stage(7)
                  S.alias([B("w_out", c) for c in range(8)], [B("w_pass", c) for c in range(8)] + [B("wq_up"), B("wkv_up")])
                  for c in range(8):
                      lastw = dma_cast(w_out_s[:, c, :], DT["w_out"][l, :, c, :], [B("w_out", c)], "w_out")
                  S.join([B("w_out", c) for c in range(8)], lastw)
                  load_grow(gb, "gb", l, 1)
                  for t in range(NT):
                      yb = (t % 2) * 2
                      for c in range(8):
                          for n in range(2):
                              S.op("pe", lambda e, c=c, n=n, yb=yb, t=t: e.matmul(pbank(yb + n), OT[:, c, t * 128:(t + 1) * 128], w_out_s[:, c, n * 512:(n + 1) * 512], start=(c == 0), stop=(c == 7)),
                                   reads=[B("OT", c, t // 4, 0), B("OT", c, t // 4, 1), B("w_out", c)], writes=[PB(yb + n)])
                      post_norm_residual(yb, t)

                  stage(8)
                  mixbufs = [B("VA", t) for t in range(NT)] + [B("VBt", t) for t in range(NT)] + [B("QT", t) for t in range(NT)] + [B("KT", t) for t in range(NT)] + [B("Vones")]
                  wbufs_old = [B("w_out", c) for c in range(8)]
                  membufs = [B("memf"), B("w_mkv")]
                  S.alias(membufs, wbufs_old)
                  S.alias([B("memT"), B("w_mq"), B("w_mo"), B("KmT"), B("Vm")] + [B("QmT", t) for t in range(NT)] + [B("OmT", pj, qc, hh) for pj in range(2) for qc in range(4) for hh in range(2)], mixbufs)
                  load_grow(gpre, "gpre", l, 2)
                  load_grow(gb, "gb", l, 4)
                  dma_cast(w_mq_s[:], DT["w_mq"][l, :, :, :], [B("w_mq")], "w_mq")
                  dma_cast(w_mkv_s[:], DT["w_mkv"][l, :, :, :], [B("w_mkv")], "w_mkv")
                  dma_cast(w_mo_s[:], DT["w_mo"][l, :, :, :], [B("w_mo")], "w_mo")
                  for kt in range(2):
                      dma_sp(memf[:, kt, :], DT["mem"][b, kt, :, :], [B("memf")], ("memf", kt))
                  gmem = tmp_pn[0]
                  dma_sp(gmem, DT["grow"][l * 7 + 3:l * 7 + 4, :].partition_broadcast(128), [B("finA", 2), B("finA", 3)], "gmem")
                  for kt in range(2):
                      col = stat_col()
                      S.op("act", lambda e, kt=kt, col=col: e.activation(out=junk[:], in_=memf[:, kt, :], func=AF.Square, accum_out=ssq[:, col:col + 1]),
                           reads=[B("memf")], writes=[B("junk"), B("ssq", col)])
                      rstd_to(rstd[:, col:col + 1], B("rstd", col), ssq[:, col:col + 1], [B("ssq", col)], D)
                      hb = hn[kt]
                      S.op("dve", lambda e, kt=kt, col=col, hb=hb: e.scalar_tensor_tensor(out=hb[:], in0=memf[:, kt, :], scalar=rstd[:, col:col + 1], in1=gmem, op0=ALU.mult, op1=ALU.mult),
                           reads=[B("memf"), B("rstd", col), B("finA", 2), B("finA", 3)], writes=[B("hn", kt)])
                      pv = pbank_bf(7)
                      for c in range(8):
                          S.op("pe", lambda e, c=c, hb=hb: e.transpose(pv[:, c * 128:(c + 1) * 128], hb[:, c * 128:(c + 1) * 128], ident[:, :]),
                               reads=[B("hn", kt), B("ident")], writes=[PB(7)])
                      S.op("act", lambda e, kt=kt: e.activation(out=memT[:, :, kt * 128:(kt + 1) * 128], in_=pv[:, :].rearrange("p (c k) -> p c k", c=8), func=AF.Copy),
                           reads=[PB(7)], writes=[B("memT")])
                  for jj in range(2):
                      for c in range(8):
                          S.op("pe", lambda e, jj=jj, c=c: e.matmul(pbank(jj)[:, 0:256], w_mkv_s[:, c, jj * 128:(jj + 1) * 128], memT[:, c, :], start=(c == 0), stop=(c == 7)),
                               reads=[B("w_mkv"), B("memT")], writes=[PB(jj)])
                      S.op("act", lambda e, jj=jj: e.activation(out=KmT[:, jj, :], in_=pbank(jj)[:, 0:256], func=AF.Copy), reads=[PB(jj)], writes=[B("KmT")])
                  for h in range(4):
                      i_ = h % 2
                      S.op("pool", lambda e, h=h, i_=i_: e.memset(Vm[:, :, h, (1 - i_) * 64:(2 - i_) * 64], 1.0), reads=[B("Vm")], writes=[B("Vm")])
                  for kt in range(2):
                      for c in range(8):
                          S.op("pe", lambda e, kt=kt, c=c: e.matmul(pbank(2 + kt)[:, 0:256], memT[:, c, kt * 128:(kt + 1) * 128], w_mkv_s[:, c, 256:512], start=(c == 0), stop=(c == 7)),
                               reads=[B("w_mkv"), B("memT")], writes=[PB(2 + kt)])
                      for h in range(4):
                          i_ = h % 2
                          S.op("dve", lambda e, kt=kt, h=h, i_=i_: e.tensor_copy(out=Vm[:, kt, h, i_ * 64:(i_ + 1) * 64], in_=pbank(2 + kt)[:, h * 64:(h + 1) * 64]),
                               reads=[PB(2 + kt), B("Vm")], writes=[B("Vm")])
                  def memq_mm(t):
                      par = t % 2
                      for c in range(8):
                          S.op("pe", lambda e, c=c: e.matmul(pbank(par)[:, 0:256], hTt[par][:, c, :], w_mq_s[:, c, :], start=(c == 0), stop=(c == 7)),
                               reads=[B("hTt", par), B("w_mq")], writes=[PB(par)])
                      st = stA[par]
                      S.op("act", lambda e: e.activation(out=st[:, 0:256], in_=pbank(par)[:, 0:256], func=AF.Copy), reads=[PB(par)], writes=[B("stA", par)])

                  def memq_tr(t):
                      par = t % 2
                      st = stA[par]
                      bank = 4 + par
                      pv = pbank_bf(bank)
                      for k2 in range(2):
                          S.op("pe", lambda e, k2=k2: e.transpose(pv[:, k2 * 128:(k2 + 1) * 128], st[:, k2 * 128:(k2 + 1) * 128], ident[:, :]),
                               reads=[B("stA", par), B("ident")], writes=[PB(bank)])
                      S.op("dve", lambda e: e.tensor_copy(out=QmT[:, :, t * 128:(t + 1) * 128], in_=pv[:, 0:256].rearrange("p (h k) -> p h k", h=2)),
                           reads=[PB(bank)], writes=[B("QmT", t)])

                  norm_tile(0, True, hTt[0][:, :, :], B("hTt", 0), 0)
                  for i in range(NT + 1):
                      if i + 1 < NT:
                          p1 = (i + 1) % 2
                          norm_tile(i + 1, True, hTt[p1][:, :, :], B("hTt", p1), p1)
                      if i < NT:
                          memq_mm(i)
                      if i - 1 >= 0:
                          memq_tr(i - 1)
                  attention_pairs(
                      2,
                      q_ap=lambda pj, hh, qc: QmT[hh * 64:(hh + 1) * 64, pj, qc * 512:(qc + 1) * 512],
                      k_ap=lambda pj, hh, kt: KmT[hh * 64:(hh + 1) * 64, pj, kt * 128:(kt + 1) * 128],
                      v_ap=lambda pj, hh, kt: Vm[:, kt, 2 * pj + hh, :],
                      qbufs=lambda qc: [B("QmT", qc * 4 + i) for i in range(4)], kbuf=lambda kt: B("KmT"), vbuf=lambda kt: B("Vm"),
                      scale=64 ** -0.5,
                      ot_dst=lambda pj, hh, qc: OmT[hh * 64:(hh + 1) * 64, pj, qc * 512:(qc + 1) * 512],
                      ot_buf=lambda pj, hh, qc: B("OmT", pj, qc, hh), masked=False, kt_list_fn=lambda qc: [0, 1], part_of=lambda pj, hh: hh)
                  for t in range(NT):
                      yb = (t % 2) * 2 + 4
                      for c in range(2):
                          for n in range(2):
                              S.op("pe", lambda e, c=c, n=n, yb=yb, t=t: e.matmul(pbank(yb + n), OmT[:, c, t * 128:(t + 1) * 128], w_mo_s[:, c, n * 512:(n + 1) * 512], start=(c == 0), stop=(c == 1)),
                                   reads=[B("OmT", c, t // 4, 0), B("OmT", c, t // 4, 1), B("w_mo")], writes=[PB(yb + n)])
                      post_norm_residual(yb, t)

                  stage(9)
                  memall = [B("memT"), B("w_mq"), B("w_mo"), B("KmT"), B("Vm"), B("memf"), B("w_mkv")] + [B("QmT", t) for t in range(NT)] + [B("OmT", pj, qc, hh) for pj in range(2) for qc in range(4) for hh in range(2)]
                  S.alias([B("w_d", f) for f in range(NFC)] + [B("wgu", i) for i in range(NSLOT)], memall + [B("Pacc", 0, 0), B("Pacc", 0, 1), B("Pacc", 1, 0), B("Pacc", 1, 1)])
                  S.alias([B("hTq"), B("HT")] , [B("OT", c, qc, i) for c in range(8) for qc in range(4) for i in range(2)])
                  load_grow(gpre, "gpre", l, 5)
                  load_grow(gb, "gb", l, 6)
                  nload = [0]

                  def load_slot(f):
                      s_ = f % NSLOT
                      dma_cast(wgu[s_][:, 0, :, :], DT["w_g"][l, f, :, :, :], [B("wgu", s_)], ("wgu", s_))
                      dma_cast(wgu[s_][:, 1, :, :], DT["w_u"][l, f, :, :, :], [B("wgu", s_)], ("wgu", s_))
                  for q4 in range(4):
                      for ti in range(4):
                          t = q4 * 4 + ti
                          norm_tile(t, True, hTq[:, :, ti * 128:(ti + 1) * 128], B("hTq"), ti % 2)
                      if SUB == 7:
                          continue
                      for f in range(min(NSLOT - 1, NFC)):
                          load_slot(f)
                      for f in range(NFC):
                          if f + NSLOT - 1 < NFC:
                              load_slot(f + NSLOT - 1)
                          if q4 == 0:
                              lastw = dma_cast(w_d_s[:, f, :], DT["w_d"][l, :, f, :], [B("w_d", f)], "w_d")
                              if f == NFC - 1:
                                  S.join([B("w_d", ff) for ff in range(NFC)], lastw)
                          s_ = f % NSLOT
                          gbk, ubk = (f % 2) * 2, (f % 2) * 2 + 1
                          for c in range(8):
                              S.op("pe", lambda e, c=c, s_=s_, gbk=gbk: e.matmul(pbank(gbk), wgu[s_][:, 0, c, :], hTq[:, c, :], start=(c == 0), stop=(c == 7)),
                                   reads=[B("wgu", s_), B("hTq")], writes=[PB(gbk)])
                          for c in range(8):
                              S.op("pe", lambda e, c=c, s_=s_, ubk=ubk: e.matmul(pbank(ubk), wgu[s_][:, 1, c, :], hTq[:, c, :], start=(c == 0), stop=(c == 7)),
                                   reads=[B("wgu", s_), B("hTq")], writes=[PB(ubk)])
                          sgt = sg[f % 2]
                          S.op("act", lambda e, sgt=sgt, gbk=gbk: e.activation(out=sgt[:], in_=pbank(gbk), func=AF.Silu), reads=[PB(gbk)], writes=[B("sg", f % 2)])
                          S.op("dve", lambda e, sgt=sgt, ubk=ubk, f=f: e.tensor_tensor(out=HT[:, f, :], in0=pbank(ubk), in1=sgt[:], op=ALU.mult),
                               reads=[PB(ubk), B("sg", f % 2)], writes=[B("HT")])
                      if SUB == 8:
                          continue
                      for ti in range(4):
                          t = q4 * 4 + ti
                          yb = 4 if ti % 2 == 0 else 2
                          for f in range(NFC):
                              for n in range(2):
                                  S.op("pe", lambda e, f=f, n=n, yb=yb, ti=ti: e.matmul(pbank(yb + n), HT[:, f, ti * 128:(ti + 1) * 128], w_d_s[:, f, n * 512:(n + 1) * 512], start=(f == 0), stop=(f == NFC - 1)),
                                       reads=[B("HT"), B("w_d", f)], writes=[PB(yb + n)])
                          if SUB == 9:
                              continue
                          post_norm_residual(yb, t)
                      if SUB == 10:
                          break
                  nxt = [B("OT", c, qc, i) for c in range(8) for qc in range(4) for i in range(2)] + mixbufs + [B("w_pass", c) for c in range(8)] + [B("wq_up"), B("wkv_up"), B("Pacc", 0, 0), B("Pacc", 0, 1), B("Pacc", 1, 0), B("Pacc", 1, 1)]
                  S.alias(nxt, [B("hTq"), B("HT")] + [B("w_d", f) for f in range(NFC)] + [B("wgu", i) for i in range(NSLOT)])

              except _Stop:
                pass

            outbufs = []
            for t0_, t1_ in ((0, 12), (12, NT)):
                for t in range(t0_, t1_):
                    dma_sp(DT["out"][b, t, :, :], xs[:, t, :], [B("hbm_out", b, t)], ("out", t), reads=[B("x", t)])
                    outbufs.append(B("hbm_out", b, t))
                if b + 1 < NB:
                    for t in range(t0_, t1_):
                        dma_sp(xs[:, t, :], DT["x"][b + 1, t, :, :], [B("x", t)], ("x", t))
            if b == NB - 1:
                S.op("sp", lambda e: None, reads=outbufs)

        S.finalize()
        sems = {e_: es.enter_context(nc.semaphore("s_" + e_)) for e_ in S.ENGS}
        dsems = {k: es.enter_context(nc.semaphore("d_%d" % i)) for i, k in enumerate(S.dma_groups)}
        block = es.enter_context(nc.Block())
        block.sync(lambda e: S.emit_stream("sp", e, sems, dsems))
        block.tensor(lambda e: S.emit_stream("pe", e, sems, dsems))
        block.scalar(lambda e: S.emit_stream("act", e, sems, dsems))
        block.vector(lambda e: S.emit_stream("dve", e, sems, dsems))
        block.gpsimd(lambda e: S.emit_stream("pool", e, sems, dsems))
        build_program.stats = {e_: len(S.streams[e_]) for e_ in S.ENGS}
        build_program.nsem = len(dsems) + 5
    return nc


def _mask_table():
    m = np.zeros((128, MASK_W), np.float32)
    p = np.arange(128)[:, None]
    c = np.arange(MASK_W)[None, :]
    diff = c - p - MASK_C0 + 0
    a = np.abs(diff)
    m += (a <= 64)
    m += ((a % 4 == 0) & (a <= 256))
    m += ((a % 16 == 0) & (a <= 1024))
    return m.astype(np.float32)


def prep_shared(inp):
    f = np.float32
    L = DEPTH
    ng = np.asarray(inp["norm_gains"], f)
    sh = {}
    sh["ident"] = np.eye(128, dtype=f)
    sh["maskb"] = _mask_table()
    inv8 = (500000.0 ** (-np.arange(0, 16, 2, dtype=np.float32) / np.float32(16))).astype(f)
    inv16 = (500000.0 ** (-np.arange(0, 32, 2, dtype=np.float32) / np.float32(32))).astype(f)
    sh["invf"] = np.ascontiguousarray(np.tile(np.concatenate([inv8, inv16])[None, :], (128, 1)).astype(f))
    sh["gT"] = np.ascontiguousarray(ng.reshape(L, 7, 8, 128).transpose(3, 0, 1, 2).reshape(128, L * 7 * 8))
    sh["grow"] = np.ascontiguousarray(ng.reshape(L * 7, D))
    sh["w_in"] = np.ascontiguousarray(np.asarray(inp["w_in"], f).reshape(L, 8, 128, D_IN).transpose(0, 2, 1, 3))
    sh["w_out"] = np.ascontiguousarray(np.asarray(inp["w_out"], f).reshape(L, 8, 128, D).transpose(0, 2, 1, 3))
    sh["lam"] = np.ascontiguousarray(np.asarray(inp["diff_lambda"], f).reshape(L, 256))
    sh["subln"] = np.ascontiguousarray(np.asarray(inp["diff_subln"], f).T)
    sh["qng"] = np.ascontiguousarray(np.asarray(inp["mla_q_norm"], f).reshape(L, 2, 128).transpose(2, 0, 1).reshape(128, L * 2))
    sh["kvng"] = np.ascontiguousarray(np.asarray(inp["mla_kv_norm"], f).T)
    sh["wq_up"] = np.ascontiguousarray(np.asarray(inp["w_mla_q_up"], f).reshape(L, 2, 128, 384).transpose(0, 2, 1, 3))
    sh["wkv_up"] = np.ascontiguousarray(np.asarray(inp["w_mla_kv_up"], f))
    sh["w_mq"] = np.ascontiguousarray(np.asarray(inp["w_mem_q"], f).reshape(L, 8, 128, 256).transpose(0, 2, 1, 3))
    sh["w_mkv"] = np.ascontiguousarray(np.asarray(inp["w_mem_kv"], f).reshape(L, 8, 128, 512).transpose(0, 2, 1, 3))
    sh["w_mo"] = np.ascontiguousarray(np.asarray(inp["w_mem_o"], f).reshape(L, 2, 128, D).transpose(0, 2, 1, 3))
    sh["w_g"] = np.ascontiguousarray(np.asarray(inp["w_ffn_gate"], f).reshape(L, 8, 128, NFC, 128).transpose(0, 3, 2, 1, 4))
    sh["w_u"] = np.ascontiguousarray(np.asarray(inp["w_ffn_up"], f).reshape(L, 8, 128, NFC, 128).transpose(0, 3, 2, 1, 4))
    sh["w_d"] = np.ascontiguousarray(np.asarray(inp["w_ffn_down"], f).reshape(L, NFC, 128, D).transpose(0, 2, 1, 3))
    return sh


def prep_core(inp, b0, nb):
    x = np.asarray(inp["x"], np.float32)[b0:b0 + nb]
    mem = np.asarray(inp["mem"], np.float32)[b0:b0 + nb]
    pos = np.asarray(inp["positions"], np.int32)[b0:b0 + nb]
    return {
        "x": np.ascontiguousarray(x.reshape(nb, NT, 128, D)),
        "mem": np.ascontiguousarray(mem.reshape(nb, 2, 128, D)),
        "pos": np.ascontiguousarray(pos.reshape(nb, NT, 128).transpose(0, 2, 1)),
    }


_PROG = {}


def kernel(**inputs):
    nb = 16 // N_CORES
    key = (nb, (0, 1))
    if key not in _PROG:
        _PROG[key] = build_program(NB=nb, layers=(0, 1))
    nc = _PROG[key]
    sh = prep_shared(inputs)
    in_maps = []
    for c in range(N_CORES):
        m = dict(sh)
        m.update(prep_core(inputs, c * nb, nb))
        in_maps.append(m)
    res = run_bass_kernel_spmd(nc, in_maps, core_ids=list(range(N_CORES)))
    outs = [np.asarray(r["out"], np.float32).reshape(nb, S_LEN, D) for r in res.results]
    return np.concatenate(outs, axis=0)
```

```python
import math
from contextlib import ExitStack

import numpy as np
import concourse.bass as bass
import concourse.mybir as mybir
from concourse.bass_utils import run_bass_kernel_spmd

F32 = mybir.dt.float32
BF16 = mybir.dt.bfloat16
I32 = mybir.dt.int32
AF = mybir.ActivationFunctionType
ALU = mybir.AluOpType

N_CORES = 8
D = 1024
S_LEN = 2048
NT = 16
DEPTH = 2
D_IN = 2720
D_FF = 2816
NFC = 22
MEM_LEN = 256
EPS = 1e-6
MASK_W = 2944
MASK_C0 = 1408


class Buf:
    __slots__ = ("name", "w", "r")

    def __init__(self, name):
        self.name = name
        self.w = None
        self.r = []


class Op:
    __slots__ = ("eng", "idx", "fn", "deps", "waits", "signal", "semval", "dma", "dsem")

    def __init__(self, eng, fn):
        self.eng = eng
        self.fn = fn
        self.deps = []
        self.waits = []
        self.signal = False
        self.semval = 0
        self.dma = False
        self.dsem = None


class _Rec:
    def __init__(self):
        self.calls = []

    def __getattr__(self, name):
        def f(*a, **k):
            self.calls.append((name, a, k))
            return self
        return f


class Sched:
    ENGS = ("pe", "act", "dve", "pool", "sp")

    def __init__(self):
        self.ops = []
        self.streams = {e: [] for e in self.ENGS}
        self.bufs = {}
        self.dma_groups = {}

    def B(self, *key):
        b = self.bufs.get(key)
        if b is None:
            b = self.bufs[key] = Buf(key)
        return b

    def op(self, eng, fn, reads=(), writes=(), dma_key=None):
        rec = _Rec()
        fn(rec)
        o = Op(eng, rec.calls)
        deps = {}
        rawset = set()
        writes = list(writes) + [b for b in reads if b.name[0] == "ps" and b not in writes]
        for b in reads:
            if b.w is not None:
                deps[id(b.w)] = b.w
                rawset.add(id(b.w))
        for b in writes:
            for r in b.r:
                deps[id(r)] = r
            if b.w is not None:
                deps[id(b.w)] = b.w
        for d in deps.values():
            if d is o:
                continue
            if d.dma:
                if dma_key is not None and d.dsem == dma_key:
                    continue
                o.deps.append(d)
            else:
                if dma_key is None and d.eng == eng and eng == "pe":
                    continue
                o.deps.append(d)
        for b in reads:
            b.r.append(o)
        for b in writes:
            b.w = o
            b.r = []
        if dma_key is not None:
            o.dma = True
            o.dsem = dma_key
            self.dma_groups.setdefault(dma_key, []).append(o)
        o.idx = len(self.streams[eng])
        self.streams[eng].append(o)
        self.ops.append(o)
        return o

    def join(self, bufs, last_op):
        for b in bufs:
            if b.w is not None and b.w.dma and b.w.dsem == last_op.dsem:
                b.w = last_op

    def alias(self, new_bufs, old_bufs):
        latest = {}
        dmas = []
        for b in old_bufs:
            for o in list(b.r) + ([b.w] if b.w is not None else []):
                if o.dma:
                    dmas.append(o)
                else:
                    cur = latest.get(o.eng)
                    if cur is None or cur.idx < o.idx:
                        latest[o.eng] = o
        ops = list(latest.values()) + dmas
        for b in new_bufs:
            b.r = list(b.r) + ops

    def finalize(self):
        for k, lst in self.dma_groups.items():
            for i, o in enumerate(lst):
                o.semval = 16 * (i + 1)
        known = {e: {} for e in self.ENGS}
        for o in self.ops:
            kn = known[o.eng]
            need = {}
            for d in o.deps:
                src = ("dma", d.dsem) if d.dma else d.eng
                pos = d.semval if d.dma else d.idx
                if kn.get(src, -1) >= pos:
                    continue
                if src not in need or need[src][0] < pos:
                    need[src] = (pos, d)
            for src, (pos, d) in need.items():
                kn[src] = pos
                if not d.dma:
                    d.signal = True
                o.waits.append(d)
        for e in self.ENGS:
            c = 0
            for o in self.streams[e]:
                if o.dma:
                    continue
                if o.signal:
                    c += 1
                    o.semval = c

    def emit_stream(self, engname, engobj, sems, dma_sems):
        for o in self.streams[engname]:
            for d in o.waits:
                if d.dma:
                    engobj.wait_ge(dma_sems[d.dsem], d.semval)
                else:
                    engobj.wait_ge(sems[d.eng], d.semval)
            ins = None
            for (name, a_, k_) in o.fn:
                ins = getattr(engobj, name)(*a_, **k_)
            if ins is None:
                continue
            if o.dma:
                ins.then_inc(dma_sems[o.dsem], 16)
            elif o.signal:
                ins.then_inc(sems[o.eng], 1)


SUB = 99
N_WARM = 0


class _Stop(Exception):
    pass


def build_program(NB=2, layers=(0, 1), taps=(), upto=99, wdepth=DEPTH):
    nc = bass.Bass("TRN2", target_bir_lowering=False)
    DT = {}

    def din(name, shape, dt=F32):
        DT[name] = nc.dram_tensor(name, list(shape), dt, kind="ExternalInput").ap()

    def dout(name, shape, dt=F32):
        DT[name] = nc.dram_tensor(name, list(shape), dt, kind="ExternalOutput").ap()

    din("x", [NB, NT, 128, D])
    din("mem", [NB, 2, 128, D])
    din("pos", [NB, 128, NT], I32)
    din("ident", [128, 128])
    din("maskb", [128, MASK_W])
    din("invf", [128, 24])
    din("gT", [128, DEPTH * 7 * 8])
    din("grow", [DEPTH * 7, D])
    din("w_in", [wdepth, 128, 8, D_IN])
    din("w_out", [wdepth, 128, 8, D])
    din("lam", [DEPTH, 256])
    din("subln", [128, DEPTH])
    din("qng", [128, DEPTH * 2])
    din("kvng", [128, DEPTH])
    din("wq_up", [wdepth, 128, 2, 384])
    din("wkv_up", [wdepth, 128, 512])
    din("w_mq", [wdepth, 128, 8, 256])
    din("w_mkv", [wdepth, 128, 8, 512])
    din("w_mo", [wdepth, 128, 2, D])
    din("w_g", [wdepth, NFC, 128, 8, 128])
    din("w_u", [wdepth, NFC, 128, 8, 128])
    din("w_d", [wdepth, 128, NFC, D])
    dout("out", [NB, NT, 128, D])
    for name, shape, dt in taps:
        dout(name, shape, dt)

    S = Sched()
    B = S.B
    es = ExitStack()
    with es:
        def sb(name, shape, dt):
            return es.enter_context(nc.sbuf_tensor(name, list(shape), dt))


        xs = sb("xs", [128, NT, D], F32)
        ident = sb("ident_s", [128, 128], BF16)
        ones = sb("ones_s", [128, 128], BF16)
        maskb = sb("maskb_s", [128, MASK_W], BF16)
        invf = sb("invf_s", [128, 24], F32)
        epsT = sb("eps_s", [128, 1], F32)
        subln = sb("subln_s", [128, DEPTH], F32)
        qng = sb("qng_s", [128, DEPTH * 2], F32)
        kvng = sb("kvng_s", [128, DEPTH], F32)
        sublnS = sb("sublnS", [128, DEPTH], F32)
        lamraw = sb("lamraw", [128, 256], F32)
        lamt = sb("lamt", [128, 8], F32)
        neglam = sb("neglam", [128, DEPTH], F32)
        posi = sb("posi", [128, NT], I32)
        posf = sb("posf", [128, NT], F32)
        angP = sb("angP", [128, NT, 8], F32)
        angC = sb("angC", [128, NT, 16], F32)
        cosP = sb("cosP", [128, NT, 8], F32)
        sinP = sb("sinP", [128, NT, 8], F32)
        cosC = sb("cosC", [128, NT, 16], F32)
        sinC = sb("sinC", [128, NT, 16], F32)
        gpre = sb("gpre", [128, D], F32)
        gb = sb("gb", [128, D], F32)
        junk = sb("junk", [128, D], BF16)
        hn = [sb("hn%d" % i, [128, D], BF16) for i in range(2)]
        hTt = [sb("hTt%d" % i, [128, 8, 128], BF16) for i in range(2)]
        ssq = sb("ssq", [128, 64], F32)
        rstd = sb("rstd", [128, 64], F32)
        rcache = sb("rcache", [128, NT], F32)
        PT = [sb("PT%d" % i, [128, 1024], BF16) for i in range(3)]
        stA = [sb("stA%d" % i, [128, 512], BF16) for i in range(2)]
        ropeT = [sb("ropeT%d" % i, [128, 128], BF16) for i in range(2)]
        ropeU = [sb("ropeU%d" % i, [128, 128], BF16) for i in range(2)]
        stC = [sb("stC%d" % i, [128, 384], BF16) for i in range(2)]
        stKR = [sb("stKR%d" % i, [128, 32], BF16) for i in range(4)]
        stQC = [sb("stQC%d" % i, [128, 192], BF16) for i in range(2)]
        stKC = [sb("stKC%d" % i, [128, 2, 96], BF16) for i in range(2)]
        cT = [sb("cT%d" % i, [128, 3, 128], BF16) for i in range(2)]
        finAall = sb("finAall", [128, 4, 512], F32)
        finA = [finAall[:, i, :] for i in range(4)]
        finB = sb("finB", [128, 512], BF16)
        sg = [sb("sg%d" % i, [128, 512], BF16) for i in range(2)]
        ARENA = 45056
        arena = sb("arena", [128, ARENA], BF16)
        psum = es.enter_context(nc.psum_tensor("psum", [128, 4096], F32))

        def pbank(b0, nb=1):
            return psum[:, b0 * 512:(b0 + nb) * 512]

        def pbank_bf(b0):
            return psum[:, b0 * 512:(b0 + 1) * 512].bitcast(BF16)

        PB = lambda i: B("ps", i)

        O0, Q0, W0 = 0, 16384, 32768
        OT = arena[:, O0:O0 + 16384].rearrange("p (c t) -> p c t", c=8)
        w_pass = arena[:, W0:W0 + 6144].rearrange("p (c n) -> p c n", c=8)
        wq_up_s = arena[:, W0 + 6144:W0 + 6144 + 768].rearrange("p (c n) -> p c n", c=2)
        wkv_up_s = arena[:, W0 + 6912:W0 + 6912 + 512]
        w_out_s = arena[:, W0:W0 + 8192].rearrange("p (c n) -> p c n", c=8)
        QAT = arena[:, Q0:Q0 + 4096].rearrange("p (h t) -> p h t", h=2)
        KAT = arena[:, Q0 + 4096:Q0 + 8192].rearrange("p (h t) -> p h t", h=2)
        VA = arena[:, Q0 + 8192:Q0 + 12288].rearrange("p (t n) -> p t n", t=NT)
        QBT = arena[:, Q0:Q0 + 4096].rearrange("p (j t) -> p j t", j=2)
        KBT = arena[:, Q0 + 4096:Q0 + 8192].rearrange("p (j t) -> p j t", j=2)
        VB = arena[:, Q0 + 8192:Q0 + 16384].rearrange("p (t h n) -> p t h n", t=NT, h=4)
        QCT = arena[:, Q0:Q0 + 4096].rearrange("p (h t) -> p h t", h=2)
        KCT = arena[:, Q0 + 4096:Q0 + 8192].rearrange("p (h t) -> p h t", h=2)
        VC = arena[:, Q0 + 8192:Q0 + 12288].rearrange("p (t h n) -> p t h n", t=NT, h=2)
        Pacc = arena[:, W0 + 8192:W0 + 12288].bitcast(F32).rearrange("p (a n) -> p a n", a=2)
        QmT = arena[:, Q0:Q0 + 4096].rearrange("p (j t) -> p j t", j=2)
        OmT = arena[:, Q0 + 4096:Q0 + 8192].rearrange("p (j t) -> p j t", j=2)
        memT = arena[:, Q0 + 8192:Q0 + 10240].rearrange("p (c t) -> p c t", c=8)
        KmT = arena[:, Q0 + 10240:Q0 + 10752].rearrange("p (j t) -> p j t", j=2)
        Vm = arena[:, Q0 + 10752:Q0 + 11776].rearrange("p (t h n) -> p t h n", t=2, h=4)
        w_mq_s = arena[:, Q0 + 11776:Q0 + 13824].rearrange("p (c n) -> p c n", c=8)
        w_mo_s = arena[:, Q0 + 13824:Q0 + 15872].rearrange("p (c n) -> p c n", c=2)
        w_mkv_s = arena[:, W0:W0 + 4096].rearrange("p (c n) -> p c n", c=8)
        memf = arena[:, W0 + 4096:W0 + 8192].bitcast(F32).rearrange("p (t n) -> p t n", t=2)
        hTq = arena[:, O0:O0 + 4096].rearrange("p (c t) -> p c t", c=8)
        HT = arena[:, O0 + 4096:O0 + 4096 + 11264].rearrange("p (f t) -> p f t", f=NFC)
        w_d_s = arena[:, Q0:Q0 + 22528].rearrange("p (f n) -> p f n", f=NFC)
        NSLOT = 3
        SL0 = Q0 + 22528
        wgu = [arena[:, SL0 + i * 2048:SL0 + (i + 1) * 2048].rearrange("p (g c n) -> p g c n", g=2, c=8) for i in range(NSLOT)]
        assert SL0 + NSLOT * 2048 <= ARENA

        cnt = {"ss": 0}

        def stat_col():
            cnt["ss"] = (cnt["ss"] + 1) % 64
            return cnt["ss"]

        def dma_cast(dst_ap, src_ap, wbufs, key):
            return S.op("pool", lambda e: e.dma_start(out=dst_ap, in_=src_ap), writes=list(wbufs), dma_key=key)

        def dma_sp(dst_ap, src_ap, wbufs, key, reads=()):
            return S.op("sp", lambda e: e.dma_start(out=dst_ap, in_=src_ap), reads=list(reads), writes=list(wbufs), dma_key=key)

        def rstd_to(dst_ap, dst_buf, src_ap, src_bufs, n):
            S.op("act", lambda e: e.activation(out=dst_ap, in_=src_ap, func=AF.Ln, scale=1.0 / n, bias=epsT[:, 0:1]),
                 reads=list(src_bufs) + [B("eps")], writes=[dst_buf])
            S.op("act", lambda e: e.activation(out=dst_ap, in_=dst_ap, func=AF.Exp, scale=-0.5), reads=[dst_buf], writes=[dst_buf])

        dma_cast(ident[:], DT["ident"][:, :], [B("ident")], "c_ident")
        dma_cast(maskb[:], DT["maskb"][:, :], [B("maskb")], "c_maskb")
        dma_sp(invf[:], DT["invf"][:, :], [B("invf")], "c_invf")
        dma_sp(subln[:], DT["subln"][:, :], [B("subln")], "c_subln")
        dma_sp(qng[:], DT["qng"][:, :], [B("qng")], "c_qng")
        dma_sp(kvng[:], DT["kvng"][:, :], [B("kvng")], "c_kvng")
        S.op("dve", lambda e: e.memset(ones[:], 1.0), writes=[B("ones")])
        S.op("dve", lambda e: e.memset(epsT[:], EPS), writes=[B("eps")])
        for l in range(DEPTH):
            lam_init = 0.8 - 0.6 * math.exp(-0.3 * l)
            dma_sp(lamraw[:], DT["lam"][l:l + 1, :].partition_broadcast(128), [B("lamraw")], ("c_lam", l))
            S.op("dve", lambda e: e.tensor_tensor(out=lamraw[:, 0:64], in0=lamraw[:, 0:64], in1=lamraw[:, 64:128], op=ALU.mult), reads=[B("lamraw")], writes=[B("lamraw")])
            S.op("dve", lambda e: e.tensor_tensor(out=lamraw[:, 128:192], in0=lamraw[:, 128:192], in1=lamraw[:, 192:256], op=ALU.mult), reads=[B("lamraw")], writes=[B("lamraw")])
            S.op("dve", lambda e: e.tensor_reduce(out=lamt[:, 0:1], in_=lamraw[:, 0:64], axis=mybir.AxisListType.X, op=ALU.add), reads=[B("lamraw")], writes=[B("lamt")])
            S.op("dve", lambda e: e.tensor_reduce(out=lamt[:, 1:2], in_=lamraw[:, 128:192], axis=mybir.AxisListType.X, op=ALU.add), reads=[B("lamraw"), B("lamt")], writes=[B("lamt")])
            S.op("act", lambda e: e.activation(out=lamt[:, 2:4], in_=lamt[:, 0:2], func=AF.Exp), reads=[B("lamt")], writes=[B("lamt")])
            S.op("dve", lambda e, l=l, li=lam_init: e.scalar_tensor_tensor(out=neglam[:, l:l + 1], in0=lamt[:, 3:4], scalar=-li, in1=lamt[:, 2:3], op0=ALU.add, op1=ALU.subtract),
                 reads=[B("lamt")], writes=[B("neglam", l)])
            S.op("dve", lambda e, l=l, li=lam_init: e.tensor_scalar(out=sublnS[:, l:l + 1], in0=subln[:, l:l + 1], scalar1=1.0 - li, scalar2=None, op0=ALU.mult),
                 reads=[B("subln")], writes=[B("sublnS", l)])

        angK_all = finA[0][:, 0:256].rearrange("p (t w) -> p t w", t=NT)
        angI_all = finA[1][:, 0:256].bitcast(I32).rearrange("p (t w) -> p t w", t=NT)
        angT_all = finA[2][:, 0:256].rearrange("p (t w) -> p t w", t=NT)

        def range_reduce_sin(dst, ang, width, bname):
            BA = B(*bname) if isinstance(bname, tuple) else B(bname)
            k = angK_all[:, :, 0:width]
            ki = angI_all[:, :, 0:width]
            bk, bi = B("finA", 0), B("finA", 1)
            S.op("dve", lambda e: e.tensor_scalar(out=k, in0=ang, scalar1=1.0 / (2 * math.pi), scalar2=None, op0=ALU.mult), reads=[BA], writes=[bk])
            S.op("dve", lambda e: e.tensor_copy(out=ki, in_=k), reads=[bk], writes=[bi])
            S.op("dve", lambda e: e.tensor_copy(out=k, in_=ki), reads=[bi], writes=[bk])
            S.op("dve", lambda e: e.scalar_tensor_tensor(out=ang, in0=k, scalar=-2 * math.pi, in1=ang, op0=ALU.mult, op1=ALU.add), reads=[bk, BA], writes=[BA])
            S.op("dve", lambda e: e.tensor_scalar(out=k, in0=ang, scalar1=math.pi, scalar2=-2 * math.pi, op0=ALU.is_gt, op1=ALU.mult), reads=[BA], writes=[bk])
            S.op("dve", lambda e: e.tensor_tensor(out=ang, in0=ang, in1=k, op=ALU.add), reads=[bk, BA], writes=[BA])
            S.op("dve", lambda e: e.tensor_scalar(out=k, in0=ang, scalar1=-math.pi, scalar2=2 * math.pi, op0=ALU.is_lt, op1=ALU.mult), reads=[BA], writes=[bk])
            S.op("dve", lambda e: e.tensor_tensor(out=ang, in0=ang, in1=k, op=ALU.add), reads=[bk, BA], writes=[BA])
            S.op("act", lambda e: e.activation(out=dst, in_=ang, func=AF.Sin), reads=[BA], writes=[B("ropetab")])

        def norm_tile(t, first, dst_ap, dst_buf, par):
            xap = xs[:, t, :]
            if first:
                col = stat_col()
                S.op("act", lambda e: e.activation(out=junk[:], in_=xap, func=AF.Square, accum_out=ssq[:, col:col + 1]),
                     reads=[B("x", t)], writes=[B("junk"), B("ssq", col)])
                rstd_to(rcache[:, t:t + 1], B("rcache", t), ssq[:, col:col + 1], [B("ssq", col)], D)
            hb = hn[par]
            S.op("dve", lambda e: e.scalar_tensor_tensor(out=hb[:], in0=xap, scalar=rcache[:, t:t + 1], in1=gpre[:], op0=ALU.mult, op1=ALU.mult),
                 reads=[B("x", t), B("rcache", t), B("gpre")], writes=[B("hn", par)])
            pv = pbank_bf(7)
            for c in range(8):
                S.op("pe", lambda e, c=c: e.transpose(pv[:, c * 128:(c + 1) * 128], hb[:, c * 128:(c + 1) * 128], ident[:, :]),
                     reads=[B("hn", par), B("ident")], writes=[PB(7)])
            if t % 2 == 0:
                S.op("act", lambda e: e.activation(out=dst_ap, in_=pv[:, :].rearrange("p (c k) -> p c k", c=8), func=AF.Copy), reads=[PB(7)], writes=[dst_buf])
            else:
                S.op("dve", lambda e: e.tensor_copy(out=dst_ap, in_=pv[:, :].rearrange("p (c k) -> p c k", c=8)), reads=[PB(7)], writes=[dst_buf])

        def load_grow(dst, bufname, l, n):
            r = l * 7 + n
            dma_sp(dst[:], DT["grow"][r:r + 1, :].partition_broadcast(128), [B(bufname)], bufname)

        def post_norm_residual(ybank, t):
            yap = pbank(ybank, 2)
            col = stat_col()
            tmp = tmp_pn[t % 2]
            tb = [B("finA", 2), B("finA", 3)] if t % 2 == 0 else [B("finA", 0), B("finA", 1)]
            S.op("act", lambda e: e.activation(out=junk[:], in_=yap, func=AF.Square, accum_out=ssq[:, col:col + 1]),
                 reads=[PB(ybank), PB(ybank + 1)], writes=[B("junk"), B("ssq", col)])
            rstd_to(rstd[:, col:col + 1], B("rstd", col), ssq[:, col:col + 1], [B("ssq", col)], D)
            S.op("dve", lambda e: e.scalar_tensor_tensor(out=tmp, in0=yap, scalar=rstd[:, col:col + 1], in1=gb[:], op0=ALU.mult, op1=ALU.mult),
                 reads=[PB(ybank), PB(ybank + 1), B("rstd", col), B("gb")], writes=tb)
            S.op("pool", lambda e: e.tensor_tensor(out=xs[:, t, :], in0=xs[:, t, :], in1=tmp, op=ALU.add),
                 reads=[B("x", t)] + tb, writes=[B("x", t)])

        tmp_pn = [finAall[:, 2:4, :].rearrange("p a b -> p (a b)"), finAall[:, 0:2, :].rearrange("p a b -> p (a b)")]

        def rope_inplace(v, ngroups, roff, half, cos_t, sin_t, t, bufs, par):
            x1 = v[:, 0:ngroups, roff:roff + half]
            x2 = v[:, 0:ngroups, roff + half:roff + 2 * half]
            c = cos_t[:, t:t + 1, :].to_broadcast([128, ngroups, half])
            s = sin_t[:, t:t + 1, :].to_broadcast([128, ngroups, half])
            T = ropeT[par][:, 0:ngroups * 2 * half].rearrange("p (g d) -> p g d", g=ngroups)
            U = ropeU[par][:, 0:ngroups * 2 * half].rearrange("p (g d) -> p g d", g=ngroups)
            ta = T[:, 0:ngroups, 0:half]
            tb = T[:, 0:ngroups, half:2 * half]
            ua = U[:, 0:ngroups, 0:half]
            ub = U[:, 0:ngroups, half:2 * half]
            rb = [B("ropeT", par), B("ropeU", par)]
            S.op("pool", lambda e: e.tensor_tensor(out=ta, in0=x1, in1=c, op=ALU.mult), reads=bufs + [B("ropetab")], writes=[rb[0]])
            S.op("pool", lambda e: e.tensor_tensor(out=tb, in0=x2, in1=c, op=ALU.mult), reads=bufs + [B("ropetab"), rb[0]], writes=[rb[0]])
            S.op("pool", lambda e: e.tensor_tensor(out=ua, in0=x2, in1=s, op=ALU.mult), reads=bufs + [B("ropetab")], writes=[rb[1]])
            S.op("pool", lambda e: e.tensor_tensor(out=ub, in0=x1, in1=s, op=ALU.mult), reads=bufs + [B("ropetab"), rb[1]], writes=[rb[1]])
            S.op("pool", lambda e: e.tensor_tensor(out=x1, in0=ta, in1=ua, op=ALU.subtract), reads=rb + bufs, writes=bufs)
            S.op("pool", lambda e: e.tensor_tensor(out=x2, in0=tb, in1=ub, op=ALU.add), reads=rb + bufs, writes=bufs)

        def attention_pairs(npairs, q_ap, k_ap, v_ap, qbufs, kbuf, vbuf, scale, ot_dst, ot_buf, masked, kt_list_fn, part_of):
            acc = (0, 1)
            sbanks = [(2, 3), (4, 5), (6, 7)]
            blocks = [(pj, qc) for pj in range(npairs) for qc in range(4)]

            def emit_scores(pj, qc, i, kt):
                sb_ = sbanks[i % 3]
                for hh in range(2):
                    S.op("pe", lambda e, hh=hh: e.matmul(pbank(sb_[hh]), k_ap(pj, hh, kt), q_ap(pj, hh, qc), start=True, stop=True),
                         reads=[kbuf(kt)] + qbufs(qc), writes=[PB(sb_[hh])])

            def emit_exp(qc, i, kt):
                sb_ = sbanks[i % 3]
                pt = PT[i % 3]
                S.op("act", lambda e: e.activation(out=pt[:], in_=pbank(sb_[0], 2), func=AF.Exp, scale=scale),
                     reads=[PB(sb_[0]), PB(sb_[1])], writes=[B("PT", i % 3)])
                if masked:
                    c0 = qc * 512 - kt * 128 + MASK_C0
                    m = maskb[:, c0:c0 + 512].unsqueeze(1).to_broadcast([128, 2, 512])
                    ptv = pt[:].rearrange("p (a q) -> p a q", a=2)
                    S.op("dve", lambda e: e.tensor_tensor(out=ptv, in0=ptv, in1=m, op=ALU.mult),
                         reads=[B("PT", i % 3), B("maskb")], writes=[B("PT", i % 3)])

            def emit_pv(pj, i, kt, n):
                pt = PT[i % 3]
                for hh in range(2):
                    S.op("pe", lambda e, hh=hh: e.matmul(pbank(acc[hh]), v_ap(pj, hh, kt), pt[:, hh * 512:(hh + 1) * 512], start=(i == 0), stop=(i == n - 1)),
                         reads=[vbuf(kt), B("PT", i % 3)], writes=[PB(acc[hh])])

            def finalize(pj, qc):
                for hh in range(2):
                    i_ = part_of(pj, hh)
                    lo, hi = i_ * 64, (i_ + 1) * 64
                    dlo, dhi = (1 - i_) * 64, (2 - i_) * 64
                    rd = finA[hh]
                    S.op("act", lambda e: e.activation(out=rd[lo:hi, :], in_=pbank(acc[hh])[dlo:dhi, :], func=AF.Ln),
                         reads=[PB(acc[hh])], writes=[B("finA", hh)])
                    S.op("act", lambda e: e.activation(out=rd[lo:hi, :], in_=rd[lo:hi, :], func=AF.Exp, scale=-1.0), reads=[B("finA", hh)], writes=[B("finA", hh)])
                    dst = ot_dst(pj, hh, qc)
                    S.op("dve", lambda e: e.tensor_tensor(out=dst, in0=pbank(acc[hh])[lo:hi, :], in1=rd[lo:hi, :], op=ALU.mult),
                         reads=[PB(acc[hh]), B("finA", hh)], writes=[ot_buf(pj, hh, qc)])

            def prologue(pj, qc):
                kts = kt_list_fn(qc)
                emit_scores(pj, qc, 0, kts[0])
                if len(kts) > 1:
                    emit_scores(pj, qc, 1, kts[1])

            prologue(*blocks[0])
            for bi, (pj, qc) in enumerate(blocks):
                kts = kt_list_fn(qc)
                n = len(kts)
                for i in range(n):
                    if i + 2 < n:
                        emit_scores(pj, qc, i + 2, kts[i + 2])
                    emit_exp(qc, i, kts[i])
                    emit_pv(pj, i, kts[i], n)
                if bi + 1 < len(blocks):
                    prologue(*blocks[bi + 1])
                finalize(pj, qc)

        def load_pass_weights(l, col_ranges):
            for c in range(8):
                o = 0
                for (c0, c1) in col_ranges:
                    lastw = dma_cast(w_pass[:, c, o:o + (c1 - c0)], DT["w_in"][l, :, c, c0:c1], [B("w_pass", c)], "w_pass")
                    o += c1 - c0
            S.join([B("w_pass", c) for c in range(8)], lastw)

        def proj_pass(l, col_ranges, first, stages, preloaded=False):
            ncols = sum(c1 - c0 for c0, c1 in col_ranges)
            if not preloaded:
                load_pass_weights(l, col_ranges)
            nb = (ncols + 511) // 512
            norm_tile(0, first, hTt[0][:, :, :], B("hTt", 0), 0)
            for i in range(NT + len(stages) - 1):
                if i + 1 < NT:
                    p1 = (i + 1) % 2
                    norm_tile(i + 1, first, hTt[p1][:, :, :], B("hTt", p1), p1)
                if i < NT:
                    par = i % 2
                    base = par * 2
                    for c in range(8):
                        for n in range(nb):
                            w = min(512, ncols - n * 512)
                            S.op("pe", lambda e, c=c, n=n, w=w, par=par: e.matmul(pbank(base + n)[:, 0:w], hTt[par][:, c, :], w_pass[:, c, n * 512:n * 512 + w], start=(c == 0), stop=(c == 7)),
                                 reads=[B("hTt", par), B("w_pass", c)], writes=[PB(base + n)])
                    for _ in range(N_WARM):
                        S.op("pe", lambda e: e.matmul(pbank(6), ident[:, :], w_pass[:, 0, 0:512], start=True, stop=True),
                             reads=[B("ident"), B("w_pass", 0)], writes=[PB(6)])
                for k, fn in enumerate(stages):
                    t = i - k
                    if 0 <= t < NT:
                        fn(t, t % 2, (t % 2) * 2)

        for b in range(NB):
            if b == 0:
                for t in range(NT):
                    dma_sp(xs[:, t, :], DT["x"][b, t, :, :], [B("x", t)], ("x", t))
            dma_sp(posi[:], DT["pos"][b, :, :], [B("posi")], "posi")
            S.op("dve", lambda e: e.tensor_copy(out=posf[:], in_=posi[:]), reads=[B("posi")], writes=[B("posf")])
            S.op("dve", lambda e: e.tensor_tensor(out=angP[:], in0=posf[:].unsqueeze(2).to_broadcast([128, NT, 8]), in1=invf[:, 0:8].unsqueeze(1).to_broadcast([128, NT, 8]), op=ALU.mult),
                 reads=[B("posf"), B("invf")], writes=[B("angP")])
            S.op("dve", lambda e: e.tensor_tensor(out=angC[:], in0=posf[:].unsqueeze(2).to_broadcast([128, NT, 16]), in1=invf[:, 8:24].unsqueeze(1).to_broadcast([128, NT, 16]), op=ALU.mult),
                 reads=[B("posf"), B("invf")], writes=[B("angC")])
            aT8 = angT_all[:, :, 0:8]
            S.op("dve", lambda e: e.tensor_scalar(out=aT8, in0=angP[:], scalar1=math.pi / 2, scalar2=None, op0=ALU.add), reads=[B("angP")], writes=[B("finA", 2)])
            range_reduce_sin(cosP[:], aT8, 8, ("finA", 2))
            range_reduce_sin(sinP[:], angP[:], 8, "angP")
            S.op("dve", lambda e: e.tensor_scalar(out=angT_all[:], in0=angC[:], scalar1=math.pi / 2, scalar2=None, op0=ALU.add), reads=[B("angC")], writes=[B("finA", 2)])
            range_reduce_sin(cosC[:], angT_all[:], 16, ("finA", 2))
            range_reduce_sin(sinC[:], angC[:], 16, "angC")

            def stage(n):
                if n > upto:
                    raise _Stop()

            for l in layers:
              try:
                  stage(2)
                  load_grow(gpre, "gpre", l, 0)
                  scaleA = 64 ** -0.5
                  for j in range(2):
                      def tileA(t, par, base):
                          st = stA[par]
                          S.op("act", lambda e: e.activation(out=st[:], in_=pbank(base), func=AF.Copy), reads=[PB(base)], writes=[B("stA", par)])
                          S.op("dve", lambda e: e.tensor_copy(out=VA[:, t, :], in_=pbank(base + 1)[:, 0:256]), reads=[PB(base + 1)], writes=[B("VA", t)])
                          rope_inplace(st[:].rearrange("p (g d) -> p g d", d=64), 8, 0, 8, cosP, sinP, t, [B("stA", par)], par)

                      def tileA2(t, par, base):
                          st = stA[par]
                          bank = 4 + par
                          pv = pbank_bf(bank)
                          for k4 in range(4):
                              S.op("pe", lambda e, k4=k4: e.transpose(pv[:, k4 * 128:(k4 + 1) * 128], st[:, k4 * 128:(k4 + 1) * 128], ident[:, :]),
                                   reads=[B("stA", par), B("ident")], writes=[PB(bank)])
                          S.op("act", lambda e: e.activation(out=QAT[:, :, t * 128:(t + 1) * 128], in_=pv[:, 0:256].rearrange("p (h k) -> p h k", h=2), func=AF.Copy),
                               reads=[PB(bank)], writes=[B("QT", t)])
                          S.op("dve", lambda e: e.tensor_copy(out=KAT[:, :, t * 128:(t + 1) * 128], in_=pv[:, 256:512].rearrange("p (h k) -> p h k", h=2)),
                               reads=[PB(bank)], writes=[B("KT", t)])
                      proj_pass(l, [(j * 256, (j + 1) * 256), (512 + j * 256, 512 + (j + 1) * 256), (1024 + j * 256, 1024 + (j + 1) * 256)], (j == 0), [tileA, tileA2], preloaded=(j == 1))
                      if j == 0:
                          load_pass_weights(l, [(256, 512), (768, 1024), (1280, 1536)])
                      else:
                          load_pass_weights(l, [(1536, 2304)])
                      stage(3 if j == 0 else 4)
                      sbanksA = [(2, 3), (4, 5), (6, 7)]
                      itersA = [(hh, qc) for hh in range(2) for qc in range(4)]

                      def a_scores(hh, qc, kt):
                          sb_ = sbanksA[kt % 3]
                          for m in range(2):
                              S.op("pe", lambda e, m=m: e.matmul(pbank(sb_[m]), KAT[m * 64:(m + 1) * 64, hh, kt * 128:(kt + 1) * 128], QAT[m * 64:(m + 1) * 64, hh, qc * 512:(qc + 1) * 512], start=True, stop=True),
                                   reads=[B("KT", kt)] + [B("QT", qc * 4 + i) for i in range(4)], writes=[PB(sb_[m])])

                      def a_exp(kt, ai):
                          sb_ = sbanksA[kt % 3]
                          pt = PT[kt % 3]
                          S.op("act", lambda e: e.activation(out=pt[:], in_=pbank(sb_[0], 2), func=AF.Exp, scale=scaleA),
                               reads=[PB(sb_[0]), PB(sb_[1])], writes=[B("PT", kt % 3)])
                          pa = Pacc[:, ai, :]
                          if kt == 0:
                              S.op("dve", lambda e: e.tensor_copy(out=pa, in_=pt[:]), reads=[B("PT", kt % 3)], writes=[B("Pacc", ai, 0), B("Pacc", ai, 1)])
                          else:
                              S.op("dve", lambda e: e.tensor_tensor(out=pa, in0=pa, in1=pt[:], op=ALU.add), reads=[B("PT", kt % 3), B("Pacc", ai, 0), B("Pacc", ai, 1)], writes=[B("Pacc", ai, 0), B("Pacc", ai, 1)])

                      def a_pv(hh, kt):
                          pt = PT[kt % 3]
                          for m in range(2):
                              S.op("pe", lambda e, m=m: e.matmul(pbank(m), VA[:, kt, hh * 128:(hh + 1) * 128], pt[:, m * 512:(m + 1) * 512], start=(kt == 0), stop=(kt == NT - 1)),
                                   reads=[B("VA", kt), B("PT", kt % 3)], writes=[PB(m)])

                      def a_finalize(hh, qc, ai):
                          h = 2 * j + hh
                          rD0, rD1, oo, lnss = finA
                          S.op("dve", lambda e: e.tensor_copy(out=junk[:], in_=Pacc[:, ai, :]), reads=[B("Pacc", ai, 0), B("Pacc", ai, 1)], writes=[B("junk")])
                          for m in range(2):
                              S.op("pe", lambda e, m=m: e.matmul(pbank(6 + m), ones[:, :], junk[:, m * 512:(m + 1) * 512], start=True, stop=True), reads=[B("ones"), B("junk")], writes=[PB(6 + m)])
                          S.op("act", lambda e: e.activation(out=rD0[:], in_=pbank(6), func=AF.Ln), reads=[PB(6)], writes=[B("finA", 0)])
                          S.op("act", lambda e: e.activation(out=rD1[:], in_=pbank(7), func=AF.Ln), reads=[PB(7)], writes=[B("finA", 1)])
                          S.op("act", lambda e: e.activation(out=rD0[:], in_=rD0[:], func=AF.Exp, scale=-1.0), reads=[B("finA", 0)], writes=[B("finA", 0)])
                          S.op("act", lambda e: e.activation(out=rD1[:], in_=rD1[:], func=AF.Exp, scale=-1.0), reads=[B("finA", 1)], writes=[B("finA", 1)])
                          S.op("dve", lambda e: e.tensor_tensor(out=oo[:], in0=pbank(0), in1=rD0[:], op=ALU.mult), reads=[PB(0), B("finA", 0)], writes=[B("finA", 2)])
                          S.op("dve", lambda e: e.scalar_tensor_tensor(out=rD1[:], in0=pbank(1), scalar=neglam[:, l:l + 1], in1=rD1[:], op0=ALU.mult, op1=ALU.mult),
                               reads=[PB(1), B("finA", 1), B("neglam", l)], writes=[B("finA", 1)])
                          S.op("pool", lambda e: e.tensor_tensor(out=oo[:], in0=oo[:], in1=rD1[:], op=ALU.add), reads=[B("finA", 1), B("finA", 2)], writes=[B("finA", 2)])
                          S.op("act", lambda e: e.activation(out=finB[:], in_=oo[:], func=AF.Square), reads=[B("finA", 2)], writes=[B("finB")])
                          S.op("pe", lambda e: e.matmul(pbank(6), ones[:, :], finB[:, :], start=True, stop=True), reads=[B("ones"), B("finB")], writes=[PB(6)])
                          rstd_to(lnss[:], B("finA", 3), pbank(6), [PB(6)], 128)
                          S.op("dve", lambda e: e.scalar_tensor_tensor(out=OT[:, h, qc * 512:(qc + 1) * 512], in0=oo[:], scalar=sublnS[:, l:l + 1], in1=lnss[:], op0=ALU.mult, op1=ALU.mult),
                               reads=[B("finA", 2), B("finA", 3), B("sublnS", l)], writes=[B("OT", h, qc, 0), B("OT", h, qc, 1)])

                      a_scores(itersA[0][0], itersA[0][1], 0)
                      a_scores(itersA[0][0], itersA[0][1], 1)
                      for ai_, (hh, qc) in enumerate(itersA):
                          ai = ai_ % 2
                          for kt in range(NT):
                              if kt + 2 < NT:
                                  a_scores(hh, qc, kt + 2)
                              a_exp(kt, ai)
                              a_pv(hh, kt)
                          if ai_ + 1 < len(itersA):
                              a_scores(itersA[ai_ + 1][0], itersA[ai_ + 1][1], 0)
                              a_scores(itersA[ai_ + 1][0], itersA[ai_ + 1][1], 1)
                          a_finalize(hh, qc, ai)

                  stage(5)

                  def c_weights():
                      load_pass_weights(l, [(2304, 2720)])
                      dma_cast(wq_up_s[:], DT["wq_up"][l, :, :, :], [B("wq_up")], "wq_up")
                      dma_cast(wkv_up_s[:], DT["wkv_up"][l, :, :], [B("wkv_up")], "wkv_up")
                      for c in range(2):
                          S.op("pool", lambda e, c=c: e.tensor_scalar(out=wq_up_s[:, c, :], in0=wq_up_s[:, c, :], scalar1=qng[:, l * 2 + c:l * 2 + c + 1], scalar2=None, op0=ALU.mult),
                               reads=[B("wq_up"), B("qng")], writes=[B("wq_up")])
                      S.op("pool", lambda e: e.tensor_scalar(out=wkv_up_s[:], in0=wkv_up_s[:], scalar1=kvng[:, l:l + 1], scalar2=None, op0=ALU.mult),
                           reads=[B("wkv_up"), B("kvng")], writes=[B("wkv_up")])

                  def out_weights():
                      S.alias([B("w_out", c) for c in range(8)], [B("w_pass", c) for c in range(8)] + [B("wq_up"), B("wkv_up")])
                      for c in range(8):
                          lastw = dma_cast(w_out_s[:, c, :], DT["w_out"][l, :, c, :], [B("w_out", c)], "w_out")
                      S.join([B("w_out", c) for c in range(8)], lastw)
                      load_grow(gb, "gb", l, 1)

                  S.alias([B("VBt", t) for t in range(NT)] + [B("Vones")], [B("VA", t) for t in range(NT)] + [B("Vones")])
                  for h in range(4):
                      i_ = h % 2
                      S.op("pool", lambda e, h=h, i_=i_: e.memset(VB[:, :, h, (1 - i_) * 64:(2 - i_) * 64], 1.0), writes=[B("Vones")])

                  def tileB(t, par, base):
                      st = stA[par]
                      S.op("act", lambda e: e.activation(out=st[:], in_=pbank(base), func=AF.Copy), reads=[PB(base)], writes=[B("stA", par)])
                      for h in range(4):
                          i_ = h % 2
                          S.op("dve", lambda e, h=h, i_=i_: e.tensor_copy(out=VB[:, t, h, i_ * 64:(i_ + 1) * 64], in_=pbank(base + 1)[:, h * 64:(h + 1) * 64]),
                               reads=[PB(base + 1), B("Vones")], writes=[B("VBt", t)])
                      rope_inplace(st[:].rearrange("p (g d) -> p g d", d=64), 8, 0, 8, cosP, sinP, t, [B("stA", par)], par)

                  def tileB2(t, par, base):
                      st = stA[par]
                      bank = 4 + par
                      pv = pbank_bf(bank)
                      for k4 in range(4):
                          S.op("pe", lambda e, k4=k4: e.transpose(pv[:, k4 * 128:(k4 + 1) * 128], st[:, k4 * 128:(k4 + 1) * 128], ident[:, :]),
                               reads=[B("stA", par), B("ident")], writes=[PB(bank)])
                      S.op("act", lambda e: e.activation(out=QBT[:, :, t * 128:(t + 1) * 128], in_=pv[:, 0:256].rearrange("p (h k) -> p h k", h=2), func=AF.Copy),
                           reads=[PB(bank)], writes=[B("QT", t)])
                      S.op("dve", lambda e: e.tensor_copy(out=KBT[:, :, t * 128:(t + 1) * 128], in_=pv[:, 256:512].rearrange("p (h k) -> p h k", h=2)),
                           reads=[PB(bank)], writes=[B("KT", t)])
                  proj_pass(l, [(1536, 2304)], False, [tileB, tileB2], preloaded=True)
                  c_weights()

                  def kts_B(qc):
                      lo = max(0, (qc * 512 - 1024) // 128)
                      hi = min(NT - 1, (qc * 512 + 511 + 1024) // 128)
                      return list(range(lo, hi + 1))
                  attention_pairs(
                      2,
                      q_ap=lambda pj, hh, qc: QBT[hh * 64:(hh + 1) * 64, pj, qc * 512:(qc + 1) * 512],
                      k_ap=lambda pj, hh, kt: KBT[hh * 64:(hh + 1) * 64, pj, kt * 128:(kt + 1) * 128],
                      v_ap=lambda pj, hh, kt: VB[:, kt, 2 * pj + hh, :],
                      qbufs=lambda qc: [B("QT", qc * 4 + i) for i in range(4)], kbuf=lambda kt: B("KT", kt), vbuf=lambda kt: B("VBt", kt),
                      scale=64 ** -0.5,
                      ot_dst=lambda pj, hh, qc: OT[hh * 64:(hh + 1) * 64, 4 + pj, qc * 512:(qc + 1) * 512],
                      ot_buf=lambda pj, hh, qc: B("OT", 4 + pj, qc, hh), masked=True, kt_list_fn=kts_B, part_of=lambda pj, hh: hh)

                  stage(6)
                  for j in range(2):
                      S.alias([B("VA", t) for t in range(NT)] + [B("Vones")], [B("VBt", t) for t in range(NT)] + [B("VA", t) for t in range(NT)] + [B("Vones")])
                      for hh in range(2):
                          S.op("pool", lambda e, hh=hh: e.memset(VC[:, :, hh, (1 - hh) * 64:(2 - hh) * 64], 1.0), writes=[B("Vones")])

                      def tileC(t, par, base):
                          pc = pbank(base)
                          colq, colk = stat_col(), stat_col()
                          S.op("act", lambda e: e.activation(out=junk[:, 0:256], in_=pc[:, 0:256], func=AF.Square, accum_out=ssq[:, colq:colq + 1]),
                               reads=[PB(base)], writes=[B("junk"), B("ssq", colq)])
                          S.op("act", lambda e: e.activation(out=junk[:, 256:384], in_=pc[:, 256:384], func=AF.Square, accum_out=ssq[:, colk:colk + 1]),
                               reads=[PB(base)], writes=[B("junk"), B("ssq", colk)])
                          rstd_to(rstd[:, colq:colq + 1], B("rstd", colq), ssq[:, colq:colq + 1], [B("ssq", colq)], 256)
                          rstd_to(rstd[:, colk:colk + 1], B("rstd", colk), ssq[:, colk:colk + 1], [B("ssq", colk)], 128)
                          sc = stC[par]
                          S.op("dve", lambda e: e.tensor_scalar(out=sc[:, 0:256], in0=pc[:, 0:256], scalar1=rstd[:, colq:colq + 1], scalar2=None, op0=ALU.mult),
                               reads=[PB(base), B("rstd", colq)], writes=[B("stC", par)])
                          S.op("dve", lambda e: e.tensor_scalar(out=sc[:, 256:384], in0=pc[:, 256:384], scalar1=rstd[:, colk:colk + 1], scalar2=None, op0=ALU.mult),
                               reads=[PB(base), B("rstd", colk), B("stC", par)], writes=[B("stC", par)])
                          skr = stKR[t % 4]
                          S.op("dve", lambda e: e.tensor_copy(out=skr[:], in_=pc[:, 384:416]), reads=[PB(base)], writes=[B("stKR", t % 4)])
                          rope_inplace(skr[:].rearrange("p (g d) -> p g d", g=1), 1, 0, 16, cosC, sinC, t, [B("stKR", t % 4)], par)

                      def tileC2(t, par, base):
                          sc = stC[par]
                          bank = 4 + par
                          pv = pbank_bf(bank)
                          for k3 in range(3):
                              S.op("pe", lambda e, k3=k3: e.transpose(pv[:, k3 * 128:(k3 + 1) * 128], sc[:, k3 * 128:(k3 + 1) * 128], ident[:, :]),
                                   reads=[B("stC", par), B("ident")], writes=[PB(bank)])
                          ct = cT[par]
                          S.op("act", lambda e: e.activation(out=ct[:].rearrange("p a b -> p (a b)"), in_=pv[:, 0:384], func=AF.Copy), reads=[PB(bank)], writes=[B("cT", par)])

                      def tileC2b(t, par, base):
                          ct = cT[par]
                          skr = stKR[t % 4]
                          bu = base + 1
                          for c in range(2):
                              S.op("pe", lambda e, c=c: e.matmul(pbank(bu)[:, 0:192], ct[:, c, :], wq_up_s[:, c, j * 192:(j + 1) * 192], start=(c == 0), stop=(c == 1)),
                                   reads=[B("cT", par), B("wq_up")], writes=[PB(bu)])
                          S.op("pe", lambda e: e.matmul(pbank(bu)[:, 256:512], ct[:, 2, :], wkv_up_s[:, j * 256:(j + 1) * 256], start=True, stop=True),
                               reads=[B("cT", par), B("wkv_up")], writes=[PB(bu)])
                          sq = stQC[par]
                          S.op("act", lambda e: e.activation(out=sq[:], in_=pbank(bu)[:, 0:192], func=AF.Copy), reads=[PB(bu)], writes=[B("stQC", par)])
                          rope_inplace(sq[:].rearrange("p (g d) -> p g d", d=96), 2, 64, 16, cosC, sinC, t, [B("stQC", par)], par)
                          sk = stKC[par]
                          kvv = pbank(bu)[:, 256:512].rearrange("p (h n) -> p h n", h=2)
                          S.op("dve", lambda e: e.tensor_copy(out=sk[:, :, 0:64], in_=kvv[:, :, 0:64]), reads=[PB(bu)], writes=[B("stKC", par)])
                          S.op("pool", lambda e: e.tensor_copy(out=sk[:, :, 64:96], in_=skr[:].unsqueeze(1).to_broadcast([128, 2, 32])),
                               reads=[B("stKR", t % 4), B("stKC", par)], writes=[B("stKC", par)])
                          for hh in range(2):
                              S.op("dve", lambda e, hh=hh: e.tensor_copy(out=VC[:, t, hh, hh * 64:(hh + 1) * 64], in_=kvv[:, hh, 64:128]),
                                   reads=[PB(bu), B("Vones")], writes=[B("VA", t)])

                      def tileC3(t, par, base):
                          sq = stQC[par]
                          sk = stKC[par]
                          bank = 4 + par
                          pv = pbank_bf(bank)
                          for which in range(2):
                              off0 = 512 if which == 0 else 0
                              for hh in range(2):
                                  src = sq[:, hh * 96:(hh + 1) * 96] if which == 0 else sk[:, hh, :]
                                  S.op("pe", lambda e, off0=off0, hh=hh, src=src: e.transpose(pv[0:96, off0 + hh * 128:off0 + (hh + 1) * 128], src, ident[:, :]),
                                       reads=[B("stQC", par) if which == 0 else B("stKC", par), B("ident")], writes=[PB(bank)])
                              dstT = (QCT if which == 0 else KCT)[0:96, :, t * 128:(t + 1) * 128]
                              srcv = pv[0:96, off0:off0 + 256].rearrange("p (h k) -> p h k", h=2)
                              if which == 0:
                                  S.op("act", lambda e, dstT=dstT, srcv=srcv: e.activation(out=dstT, in_=srcv, func=AF.Copy), reads=[PB(bank)], writes=[B("QT", t)])
                              else:
                                  S.op("dve", lambda e, dstT=dstT, srcv=srcv: e.tensor_copy(out=dstT, in_=srcv), reads=[PB(bank)], writes=[B("KT", t)])
                      proj_pass(l, [(2304, 2720)], False, [tileC, tileC2, tileC2b, tileC3], preloaded=True)
                      if j == 0:
                          c_weights()
                      else:
                          out_weights()
                      attention_pairs(
                          1,
                          q_ap=lambda pj, hh, qc: QCT[0:96, hh, qc * 512:(qc + 1) * 512],
                          k_ap=lambda pj, hh, kt: KCT[0:96, hh, kt * 128:(kt + 1) * 128],
                          v_ap=lambda pj, hh, kt: VC[:, kt, hh, :],
                          qbufs=lambda qc: [B("QT", qc * 4 + i) for i in range(4)], kbuf=lambda kt: B("KT", kt), vbuf=lambda kt: B("VA", kt),
                          scale=96 ** -0.5,
                          ot_dst=lambda pj, hh, qc, j=j: OT[hh * 64:(hh + 1) * 64, 6 + j, qc * 512:(qc + 1) * 512],
                          ot_buf=lambda pj, hh, qc, j=j: B("OT", 6 + j, qc, hh), masked=False, kt_list_fn=lambda qc: list(range(NT)), part_of=lambda pj, hh: hh)

                  stage(7)
                  for t in range(NT):
                      yb = (t % 2) * 2
                      for c in range(8):
                          for n in range(2):
                              S.op("pe", lambda e, c=c, n=n, yb=yb, t=t: e.matmul(pbank(yb + n), OT[:, c, t * 128:(t + 1) * 128], w_out_s[:, c, n * 512:(n + 1) * 512], start=(c == 0), stop=(c == 7)),
                                   reads=[B("OT", c, t // 4, 0), B("OT", c, t // 4, 1), B("w_out", c)], writes=[PB(yb + n)])
                      post_norm_residual(yb, t)

                  stage(8)
                  mixbufs = [B("VA", t) for t in range(NT)] + [B("VBt", t) for t in range(NT)] + [B("QT", t) for t in range(NT)] + [B("KT", t) for t in range(NT)] + [B("Vones")]
                  wbufs_old = [B("w_out", c) for c in range(8)]
                  membufs = [B("memf"), B("w_mkv")]
                  S.alias(membufs, wbufs_old)
                  S.alias([B("memT"), B("w_mq"), B("w_mo"), B("KmT"), B("Vm")] + [B("QmT", t) for t in range(NT)] + [B("OmT", pj, qc, hh) for pj in range(2) for qc in range(4) for hh in range(2)], mixbufs)
                  load_grow(gpre, "gpre", l, 2)
                  load_grow(gb, "gb", l, 4)
                  dma_cast(w_mq_s[:], DT["w_mq"][l, :, :, :], [B("w_mq")], "w_mq")
                  dma_cast(w_mkv_s[:], DT["w_mkv"][l, :, :, :], [B("w_mkv")], "w_mkv")
                  dma_cast(w_mo_s[:], DT["w_mo"][l, :, :, :], [B("w_mo")], "w_mo")
                  for kt in range(2):
                      dma_sp(memf[:, kt, :], DT["mem"][b, kt, :, :], [B("memf")], ("memf", kt))
                  gmem = tmp_pn[0]
                  dma_sp(gmem, DT["grow"][l * 7 + 3:l * 7 + 4, :].partition_broadcast(128), [B("finA", 2), B("finA", 3)], "gmem")
                  for kt in range(2):
                      col = stat_col()
                      S.op("act", lambda e, kt=kt, col=col: e.activation(out=junk[:], in_=memf[:, kt, :], func=AF.Square, accum_out=ssq[:, col:col + 1]),
                           reads=[B("memf")], writes=[B("junk"), B("ssq", col)])
                      rstd_to(rstd[:, col:col + 1], B("rstd", col), ssq[:, col:col + 1], [B("ssq", col)], D)
                      hb = hn[kt]
                      S.op("dve", lambda e, kt=kt, col=col, hb=hb: e.scalar_tensor_tensor(out=hb[:], in0=memf[:, kt, :], scalar=rstd[:, col:col + 1], in1=gmem, op0=ALU.mult, op1=ALU.mult),
                           reads=[B("memf"), B("rstd", col), B("finA", 2), B("finA", 3)], writes=[B("hn", kt)])
                      pv = pbank_bf(7)
                      for c in range(8):
                          S.op("pe", lambda e, c=c, hb=hb: e.transpose(pv[:, c * 128:(c + 1) * 128], hb[:, c * 128:(c + 1) * 128], ident[:, :]),
                               reads=[B("hn", kt), B("ident")], writes=[PB(7)])
                      S.op("act", lambda e, kt=kt: e.activation(out=memT[:, :, kt * 128:(kt + 1) * 128], in_=pv[:, :].rearrange("p (c k) -> p c k", c=8), func=AF.Copy),
                           reads=[PB(7)], writes=[B("memT")])
                  for jj in range(2):
                      for c in range(8):
                          S.op("pe", lambda e, jj=jj, c=c: e.matmul(pbank(jj)[:, 0:256], w_mkv_s[:, c, jj * 128:(jj + 1) * 128], memT[:, c, :], start=(c == 0), stop=(c == 7)),
                               reads=[B("w_mkv"), B("memT")], writes=[PB(jj)])
                      S.op("act", lambda e, jj=jj: e.activation(out=KmT[:, jj, :], in_=pbank(jj)[:, 0:256], func=AF.Copy), reads=[PB(jj)], writes=[B("KmT")])
                  for h in range(4):
                      i_ = h % 2
                      S.op("pool", lambda e, h=h, i_=i_: e.memset(Vm[:, :, h, (1 - i_) * 64:(2 - i_) * 64], 1.0), reads=[B("Vm")], writes=[B("Vm")])
                  for kt in range(2):
                      for c in range(8):
                          S.op("pe", lambda e, kt=kt, c=c: e.matmul(pbank(2 + kt)[:, 0:256], memT[:, c, kt * 128:(kt + 1) * 128], w_mkv_s[:, c, 256:512], start=(c == 0), stop=(c == 7)),
                               reads=[B("w_mkv"), B("memT")], writes=[PB(2 + kt)])
                      for h in range(4):
                          i_ = h % 2
                          S.op("dve", lambda e, kt=kt, h=h, i_=i_: e.tensor_copy(out=Vm[:, kt, h, i_ * 64:(i_ + 1) * 64], in_=pbank(2 + kt)[:, h * 64:(h + 1) * 64]),
                               reads=[PB(2 + kt), B("Vm")], writes=[B("Vm")])
                  def memq_mm(t):
                      par = t % 2
                      for c in range(8):
                          S.op("pe", lambda e, c=c: e.matmul(pbank(par)[:, 0:256], hTt[par][:, c, :], w_mq_s[:, c, :], start=(c == 0), stop=(c == 7)),
                               reads=[B("hTt", par), B("w_mq")], writes=[PB(par)])
                      st = stA[par]
                      S.op("act", lambda e: e.activation(out=st[:, 0:256], in_=pbank(par)[:, 0:256], func=AF.Copy), reads=[PB(par)], writes=[B("stA", par)])

                  def memq_tr(t):
                      par = t % 2
                      st = stA[par]
                      bank = 4 + par
                      pv = pbank_bf(bank)
                      for k2 in range(2):
                          S.op("pe", lambda e, k2=k2: e.transpose(pv[:, k2 * 128:(k2 + 1) * 128], st[:, k2 * 128:(k2 + 1) * 128], ident[:, :]),
                               reads=[B("stA", par), B("ident")], writes=[PB(bank)])
                      S.op("dve", lambda e: e.tensor_copy(out=QmT[:, :, t * 128:(t + 1) * 128], in_=pv[:, 0:256].rearrange("p (h k) -> p h k", h=2)),
                           reads=[PB(bank)], writes=[B("QmT", t)])

                  norm_tile(0, True, hTt[0][:, :, :], B("hTt", 0), 0)
                  for i in range(NT + 1):
                      if i + 1 < NT:
                          p1 = (i + 1) % 2
                          norm_tile(i + 1, True, hTt[p1][:, :, :], B("hTt", p1), p1)
                      if i < NT:
                          memq_mm(i)
                      if i - 1 >= 0:
                          memq_tr(i - 1)
                  attention_pairs(
                      2,
                      q_ap=lambda pj, hh, qc: QmT[hh * 64:(hh + 1) * 64, pj, qc * 512:(qc + 1) * 512],
                      k_ap=lambda pj, hh, kt: KmT[hh * 64:(hh + 1) * 64, pj, kt * 128:(kt + 1) * 128],
                      v_ap=lambda pj, hh, kt: Vm[:, kt, 2 * pj + hh, :],
                      qbufs=lambda qc: [B("QmT", qc * 4 + i) for i in range(4)], kbuf=lambda kt: B("KmT"), vbuf=lambda kt: B("Vm"),
                      scale=64 ** -0.5,
                      ot_dst=lambda pj, hh, qc: OmT[hh * 64:(hh + 1) * 64, pj, qc * 512:(qc + 1) * 512],
                      ot_buf=lambda pj, hh, qc: B("OmT", pj, qc, hh), masked=False, kt_list_fn=lambda qc: [0, 1], part_of=lambda pj, hh: hh)
                  for t in range(NT):
                      yb = (t % 2) * 2 + 4
                      for c in range(2):
                          for n in range(2):
                              S.op("pe", lambda e, c=c, n=n, yb=yb, t=t: e.matmul(pbank(yb + n), OmT[:, c, t * 128:(t + 1) * 128], w_mo_s[:, c, n * 512:(n + 1) * 512], start=(c == 0), stop=(c == 1)),
                                   reads=[B("OmT", c, t // 4, 0), B("OmT", c, t // 4, 1), B("w_mo")], writes=[PB(yb + n)])
                      post_norm_residual(yb, t)

                  stage(9)
                  memall = [B("memT"), B("w_mq"), B("w_mo"), B("KmT"), B("Vm"), B("memf"), B("w_mkv")] + [B("QmT", t) for t in range(NT)] + [B("OmT", pj, qc, hh) for pj in range(2) for qc in range(4) for hh in range(2)]
                  S.alias([B("w_d", f) for f in range(NFC)] + [B("wgu", i) for i in range(NSLOT)], memall + [B("Pacc", 0, 0), B("Pacc", 0, 1), B("Pacc", 1, 0), B("Pacc", 1, 1)])
                  S.alias([B("hTq"), B("HT")] , [B("OT", c, qc, i) for c in range(8) for qc in range(4) for i in range(2)])
                  load_grow(gpre, "gpre", l, 5)
                  load_grow(gb, "gb", l, 6)
                  nload = [0]

                  def load_slot(f):
                      s_ = f % NSLOT
                      dma_cast(wgu[s_][:, 0, :, :], DT["w_g"][l, f, :, :, :], [B("wgu", s_)], ("wgu", s_))
                      dma_cast(wgu[s_][:, 1, :, :], DT["w_u"][l, f, :, :, :], [B("wgu", s_)], ("wgu", s_))
                  for q4 in range(4):
                      for ti in range(4):
                          t = q4 * 4 + ti
                          norm_tile(t, True, hTq[:, :, ti * 128:(ti + 1) * 128], B("hTq"), ti % 2)
                      if SUB == 7:
                          continue
                      for f in range(min(NSLOT - 1, NFC)):
                          load_slot(f)
                      for f in range(NFC):
                          if f + NSLOT - 1 < NFC:
                              load_slot(f + NSLOT - 1)
                          if q4 == 0:
                              lastw = dma_cast(w_d_s[:, f, :], DT["w_d"][l, :, f, :], [B("w_d", f)], "w_d")
                              if f == NFC - 1:
                                  S.join([B("w_d", ff) for ff in range(NFC)], lastw)
                          s_ = f % NSLOT
                          gbk, ubk = (f % 2) * 2, (f % 2) * 2 + 1
                          for c in range(8):
                              S.op("pe", lambda e, c=c, s_=s_, gbk=gbk: e.matmul(pbank(gbk), wgu[s_][:, 0, c, :], hTq[:, c, :], start=(c == 0), stop=(c == 7)),
                                   reads=[B("wgu", s_), B("hTq")], writes=[PB(gbk)])
                          for c in range(8):
                              S.op("pe", lambda e, c=c, s_=s_, ubk=ubk: e.matmul(pbank(ubk), wgu[s_][:, 1, c, :], hTq[:, c, :], start=(c == 0), stop=(c == 7)),
                                   reads=[B("wgu", s_), B("hTq")], writes=[PB(ubk)])
                          sgt = sg[f % 2]
                          S.op("act", lambda e, sgt=sgt, gbk=gbk: e.activation(out=sgt[:], in_=pbank(gbk), func=AF.Silu), reads=[PB(gbk)], writes=[B("sg", f % 2)])
                          S.op("dve", lambda e, sgt=sgt, ubk=ubk, f=f: e.tensor_tensor(out=HT[:, f, :], in0=pbank(ubk), in1=sgt[:], op=ALU.mult),
                               reads=[PB(ubk), B("sg", f % 2)], writes=[B("HT")])
                      if SUB == 8:
                          continue
                      for ti in range(4):
                          t = q4 * 4 + ti
                          yb = 4 if ti % 2 == 0 else 2
                          for f in range(NFC):
                              for n in range(2):
                                  S.op("pe", lambda e, f=f, n=n, yb=yb, ti=ti: e.matmul(pbank(yb + n), HT[:, f, ti * 128:(ti + 1) * 128], w_d_s[:, f, n * 512:(n + 1) * 512], start=(f == 0), stop=(f == NFC - 1)),
                                       reads=[B("HT"), B("w_d", f)], writes=[PB(yb + n)])
                          if SUB == 9:
                              continue
                          post_norm_residual(yb, t)
                      if SUB == 10:
                          break
                  nxt = [B("OT", c, qc, i) for c in range(8) for qc in range(4) for i in range(2)] + mixbufs + [B("w_pass", c) for c in range(8)] + [B("wq_up"), B("wkv_up"), B("Pacc", 0, 0), B("Pacc", 0, 1), B("Pacc", 1, 0), B("Pacc", 1, 1)]
                  S.alias(nxt, [B("hTq"), B("HT")] + [B("w_d", f) for f in range(NFC)] + [B("wgu", i) for i in range(NSLOT)])

              except _Stop:
                pass

            outbufs = []
            for t0_, t1_ in ((0, 12), (12, NT)):
                for t in range(t0_, t1_):
                    dma_sp(DT["out"][b, t, :, :], xs[:, t, :], [B("hbm_out", b, t)], ("out", t), reads=[B("x", t)])
                    outbufs.append(B("hbm_out", b, t))
                if b + 1 < NB:
                    for t in range(t0_, t1_):
                        dma_sp(xs[:, t, :], DT["x"][b + 1, t, :, :], [B("x", t)], ("x", t))
            if b == NB - 1:
                S.op("sp", lambda e: None, reads=outbufs)

        S.finalize()
        sems = {e_: es.enter_context(nc.semaphore("s_" + e_)) for e_ in S.ENGS}
        dsems = {k: es.enter_context(nc.semaphore("d_%d" % i)) for i, k in enumerate(S.dma_groups)}
        block = es.enter_context(nc.Block())
        block.sync(lambda e: S.emit_stream("sp", e, sems, dsems))
        block.tensor(lambda e: S.emit_stream("pe", e, sems, dsems))
        block.scalar(lambda e: S.emit_stream("act", e, sems, dsems))
        block.vector(lambda e: S.emit_stream("dve", e, sems, dsems))
        block.gpsimd(lambda e: S.emit_stream("pool", e, sems, dsems))
        build_program.stats = {e_: len(S.streams[e_]) for e_ in S.ENGS}
        build_program.nsem = len(dsems) + 5
    return nc


def _mask_table():
    m = np.zeros((128, MASK_W), np.float32)
    p = np.arange(128)[:, None]
    c = np.arange(MASK_W)[None, :]
    diff = c - p - MASK_C0 + 0
    a = np.abs(diff)
    m += (a <= 64)
    m += ((a % 4 == 0) & (a <= 256))
    m += ((a % 16 == 0) & (a <= 1024))
    return m.astype(np.float32)


def prep_shared(inp):
    f = np.float32
    L = DEPTH
    ng = np.asarray(inp["norm_gains"], f)
    sh = {}
    sh["ident"] = np.eye(128, dtype=f)
    sh["maskb"] = _mask_table()
    inv8 = (500000.0 ** (-np.arange(0, 16, 2, dtype=np.float32) / np.float32(16))).astype(f)
    inv16 = (500000.0 ** (-np.arange(0, 32, 2, dtype=np.float32) / np.float32(32))).astype(f)
    sh["invf"] = np.ascontiguousarray(np.tile(np.concatenate([inv8, inv16])[None, :], (128, 1)).astype(f))
    sh["gT"] = np.ascontiguousarray(ng.reshape(L, 7, 8, 128).transpose(3, 0, 1, 2).reshape(128, L * 7 * 8))
    sh["grow"] = np.ascontiguousarray(ng.reshape(L * 7, D))
    sh["w_in"] = np.ascontiguousarray(np.asarray(inp["w_in"], f).reshape(L, 8, 128, D_IN).transpose(0, 2, 1, 3))
    sh["w_out"] = np.ascontiguousarray(np.asarray(inp["w_out"], f).reshape(L, 8, 128, D).transpose(0, 2, 1, 3))
    sh["lam"] = np.ascontiguousarray(np.asarray(inp["diff_lambda"], f).reshape(L, 256))
    sh["subln"] = np.ascontiguousarray(np.asarray(inp["diff_subln"], f).T)
    sh["qng"] = np.ascontiguousarray(np.asarray(inp["mla_q_norm"], f).reshape(L, 2, 128).transpose(2, 0, 1).reshape(128, L * 2))
    sh["kvng"] = np.ascontiguousarray(np.asarray(inp["mla_kv_norm"], f).T)
    sh["wq_up"] = np.ascontiguousarray(np.asarray(inp["w_mla_q_up"], f).reshape(L, 2, 128, 384).transpose(0, 2, 1, 3))
    sh["wkv_up"] = np.ascontiguousarray(np.asarray(inp["w_mla_kv_up"], f))
    sh["w_mq"] = np.ascontiguousarray(np.asarray(inp["w_mem_q"], f).reshape(L, 8, 128, 256).transpose(0, 2, 1, 3))
    sh["w_mkv"] = np.ascontiguousarray(np.asarray(inp["w_mem_kv"], f).reshape(L, 8, 128, 512).transpose(0, 2, 1, 3))
    sh["w_mo"] = np.ascontiguousarray(np.asarray(inp["w_mem_o"], f).reshape(L, 2, 128, D).transpose(0, 2, 1, 3))
    sh["w_g"] = np.ascontiguousarray(np.asarray(inp["w_ffn_gate"], f).reshape(L, 8, 128, NFC, 128).transpose(0, 3, 2, 1, 4))
    sh["w_u"] = np.ascontiguousarray(np.asarray(inp["w_ffn_up"], f).reshape(L, 8, 128, NFC, 128).transpose(0, 3, 2, 1, 4))
    sh["w_d"] = np.ascontiguousarray(np.asarray(inp["w_ffn_down"], f).reshape(L, NFC, 128, D).transpose(0, 2, 1, 3))
    return sh


def prep_core(inp, b0, nb):
    x = np.asarray(inp["x"], np.float32)[b0:b0 + nb]
    mem = np.asarray(inp["mem"], np.float32)[b0:b0 + nb]
    pos = np.asarray(inp["positions"], np.int32)[b0:b0 + nb]
    return {
        "x": np.ascontiguousarray(x.reshape(nb, NT, 128, D)),
        "mem": np.ascontiguousarray(mem.reshape(nb, 2, 128, D)),
        "pos": np.ascontiguousarray(pos.reshape(nb, NT, 128).transpose(0, 2, 1)),
    }


_PROG = {}


def kernel(**inputs):
    nb = 16 // N_CORES
    key = (nb, (0, 1))
    if key not in _PROG:
        _PROG[key] = build_program(NB=nb, layers=(0, 1))
    nc = _PROG[key]
    sh = prep_shared(inputs)
    in_maps = []
    for c in range(N_CORES):
        m = dict(sh)
        m.update(prep_core(inputs, c * nb, nb))
        in_maps.append(m)
    res = run_bass_kernel_spmd(nc, in_maps, core_ids=list(range(N_CORES)))
    outs = [np.asarray(r["out"], np.float32).reshape(nb, S_LEN, D) for r in res.results]
    return np.concatenate(outs, axis=0)
```
